# Optimizing a Trainium2 kernel written in Bass

```python
import math, functools
import jax, jax.numpy as jnp
from jax import lax
import numpy as np

D_MODEL = 1024
BATCH = 2
SEQ = 8192
DEPTH = 1
DEC_BATCH = 128
DEC_SEQ = 1
PAST_LEN = 8192
PAGE_SIZE = 128

MLA_HEADS = 8
Q_LORA = 384
KV_LORA = 256
NOPE_DIM = 64
ROPE_DIM = 32
V_DIM = 64
ROPE_THETA = 10000.0
Q_BLOCK = 128
M_HEADS = 4
M_DIM = 128
CHUNK = 128
MLA_WIDTH = MLA_HEADS * V_DIM
MLSTM_WIDTH = M_HEADS * M_DIM
MIX_WIDTH = MLA_WIDTH + MLSTM_WIDTH
D_FF = 2816
PLE_DIM = 256
EPS = 1e-6
OFF_KV = Q_LORA
OFF_KR = OFF_KV + KV_LORA
OFF_MQ = OFF_KR + ROPE_DIM
OFF_MK = OFF_MQ + MLSTM_WIDTH
OFF_MV = OFF_MK + MLSTM_WIDTH
OFF_MO = OFF_MV + MLSTM_WIDTH
OFF_MI = OFF_MO + MLSTM_WIDTH
OFF_MF = OFF_MI + M_HEADS
IN_COLS = OFF_MF + M_HEADS

kernel_name = "hymba_mla_mlstm_macaron_step"


def rmsnorm(x, g):
    xf = x.astype(jnp.float32)
    y = xf * lax.rsqrt(jnp.mean(xf * xf, axis=-1, keepdims=True) + EPS)
    return (y * g.astype(jnp.float32)).astype(x.dtype)


def rope(x, pos):
    half = ROPE_DIM // 2
    inv = ROPE_THETA ** (-jnp.arange(half, dtype=jnp.float32) / half)
    ang = pos.astype(jnp.float32)[:, None] * inv[None, :]
    shape = (1, pos.shape[0]) + (1,) * (x.ndim - 3) + (half,)
    cos = jnp.cos(ang).reshape(shape)
    sin = jnp.sin(ang).reshape(shape)
    xf = x.astype(jnp.float32)
    x1, x2 = xf[..., :half], xf[..., half:]
    return jnp.concatenate([x1 * cos - x2 * sin, x1 * sin + x2 * cos], axis=-1).astype(x.dtype)


def swiglu(x, wg, wu, wd):
    return (jax.nn.silu(x @ wg) * (x @ wu)) @ wd


def prompt_attention(q_nope, q_rope, ckv, krope, w_uk, w_uv):
    B, S, H, _ = q_nope.shape
    qb = math.gcd(S, Q_BLOCK)
    nb = S // qb
    scale = (NOPE_DIM + ROPE_DIM) ** -0.5
    k_nope = (ckv @ w_uk).reshape(B, S, H, NOPE_DIM)
    v = (ckv @ w_uv).reshape(B, S, H, V_DIM)
    qn = q_nope.reshape(B, nb, qb, H, NOPE_DIM).swapaxes(0, 1)
    qr = q_rope.reshape(B, nb, qb, H, ROPE_DIM).swapaxes(0, 1)
    kpos = jnp.arange(S)

    def block(args):
        qn_b, qr_b, bi = args
        s = (jnp.einsum('bqhd,bkhd->bhqk', qn_b, k_nope, preferred_element_type=jnp.float32)
             + jnp.einsum('bqhr,bkr->bhqk', qr_b, krope, preferred_element_type=jnp.float32)) * scale
        qpos = bi * qb + jnp.arange(qb)
        s = jnp.where(kpos[None, :] <= qpos[:, None], s, -jnp.inf)
        p = jax.nn.softmax(s, axis=-1)
        return jnp.einsum('bhqk,bkhd->bqhd', p.astype(v.dtype), v)

    out = lax.map(block, (qn, qr, jnp.arange(nb)))
    return out.swapaxes(0, 1).reshape(B, S, H * V_DIM)


def sample_attention(q_nope, q_rope, ckv, krope, w_uk, w_uv, cache_c, cache_r, page_table):
    Bd, Sq, H, _ = q_nope.shape
    scale = (NOPE_DIM + ROPE_DIM) ** -0.5
    past_c = cache_c[page_table].reshape(Bd, -1, KV_LORA)
    past_r = cache_r[page_table].reshape(Bd, -1, ROPE_DIM)
    n_past = past_c.shape[1]
    q_abs = jnp.einsum('bqhn,chn->bqhc', q_nope, w_uk.reshape(KV_LORA, H, NOPE_DIM))
    s_past = (jnp.einsum('bqhc,bkc->bhqk', q_abs, past_c, preferred_element_type=jnp.float32)
              + jnp.einsum('bqhr,bkr->bhqk', q_rope, past_r, preferred_element_type=jnp.float32))
    s_new = (jnp.einsum('bqhc,bkc->bhqk', q_abs, ckv, preferred_element_type=jnp.float32)
             + jnp.einsum('bqhr,bkr->bhqk', q_rope, krope, preferred_element_type=jnp.float32))
    causal = jnp.tril(jnp.ones((Sq, Sq), dtype=bool))
    s_new = jnp.where(causal, s_new, -jnp.inf)
    p = jax.nn.softmax(jnp.concatenate([s_past, s_new], axis=-1) * scale, axis=-1)
    p = p.astype(ckv.dtype)
    o_lat = (jnp.einsum('bhqk,bkc->bqhc', p[..., :n_past], past_c)
             + jnp.einsum('bhqk,bkc->bqhc', p[..., n_past:], ckv))
    out = jnp.einsum('bqhc,chv->bqhv', o_lat, w_uv.reshape(KV_LORA, H, V_DIM))
    return out.reshape(Bd, Sq, H * V_DIM)


def mlstm_chunkwise(q, k, v, ig, lf, C0, n0, m0):
    B, S, H, D = q.shape
    L = math.gcd(S, CHUNK)
    nc = S // L
    causal = jnp.tril(jnp.ones((L, L), dtype=bool))

    def to_chunks(a):
        return a.reshape((B, nc, L) + a.shape[2:]).swapaxes(0, 1)

    def step(carry, xs):
        C, n, m = carry
        qc, kc, vc, ic, fc = xs
        bT = jnp.cumsum(fc, axis=1).transpose(0, 2, 1)
        iT = ic.transpose(0, 2, 1)
        Dm = bT[..., :, None] - bT[..., None, :] + iT[..., None, :]
        Dm = jnp.where(causal, Dm, -jnp.inf)
        inter = bT + m[..., None]
        m_row = jnp.maximum(inter, jnp.max(Dm, axis=-1))
        w_inter = jnp.exp(inter - m_row)
        Sqk = jnp.einsum('bthd,bshd->bhts', qc, kc) * jnp.exp(Dm - m_row[..., None])
        w_t = w_inter.transpose(0, 2, 1)[..., None]
        num = jnp.einsum('bhts,bshd->bthd', Sqk, vc) + jnp.einsum('bthd,bhde->bthe', qc, C) * w_t
        den = jnp.sum(Sqk, axis=-1) + jnp.einsum('bthd,bhd->bht', qc, n) * w_inter
        den = jnp.maximum(jnp.abs(den), jnp.exp(-m_row))
        h = num / den.transpose(0, 2, 1)[..., None]
        bL = bT[..., -1]
        g = bL[..., None] - bT + iT
        m_new = jnp.maximum(bL + m, jnp.max(g, axis=-1))
        a = jnp.exp(bL + m - m_new)
        ws = jnp.exp(g - m_new[..., None])
        C_new = a[..., None, None] * C + jnp.einsum('bhs,bshd,bshe->bhde', ws, kc, vc)
        n_new = a[..., None] * n + jnp.einsum('bhs,bshd->bhd', ws, kc)
        return (C_new, n_new, m_new), h

    (C, n, m), hs = lax.scan(step, (C0, n0, m0), (to_chunks(q), to_chunks(k), to_chunks(v), to_chunks(ig), to_chunks(lf)))
    return hs.swapaxes(0, 1).reshape(B, S, H, D), C, n, m


def trunk_layer(x, pe, pos, attn_fn, C0, n0, m0, lw):
    B, S, _ = x.shape
    h = x + 0.5 * swiglu(rmsnorm(x, lw['g_ff1']), lw['w_ff1_gate'], lw['w_ff1_up'], lw['w_ff1_down'])
    z = rmsnorm(h, lw['g_mix']) @ lw['w_in']
    zq, zkv, zkr, mq, mk, mv, mo, mi, mf = jnp.split(
        z, [OFF_KV, OFF_KR, OFF_MQ, OFF_MK, OFF_MV, OFF_MO, OFF_MI, OFF_MF], axis=-1)
    ckv = rmsnorm(zkv, lw['g_kv'])
    krope = rope(zkr, pos)
    q = (rmsnorm(zq, lw['g_q']) @ lw['w_uq']).reshape(B, S, MLA_HEADS, NOPE_DIM + ROPE_DIM)
    q_nope = q[..., :NOPE_DIM]
    q_rope = rope(q[..., NOPE_DIM:], pos)
    a = attn_fn(q_nope, q_rope, ckv, krope, lw['w_uk'], lw['w_uv'])
    f32 = jnp.float32
    qm = mq.astype(f32).reshape(B, S, M_HEADS, M_DIM)
    km = mk.astype(f32).reshape(B, S, M_HEADS, M_DIM) * (M_DIM ** -0.5)
    vm = mv.astype(f32).reshape(B, S, M_HEADS, M_DIM)
    ig = mi.astype(f32) + lw['b_gate_i'].astype(f32)
    lf = jax.nn.log_sigmoid(mf.astype(f32) + lw['b_gate_f'].astype(f32))
    hm, C, n, m = mlstm_chunkwise(qm, km, vm, ig, lf, C0, n0, m0)
    hm = jax.nn.sigmoid(mo.astype(f32)).reshape(B, S, M_HEADS, M_DIM) * hm
    hm = rmsnorm(hm, lw['g_mlstm_out']).reshape(B, S, MLSTM_WIDTH).astype(x.dtype)
    mix = jnp.concatenate([rmsnorm(a, lw['g_attn_out']), hm], axis=-1) @ lw['w_out']
    h = h + mix
    h = h + 0.5 * swiglu(rmsnorm(h, lw['g_ff2']), lw['w_ff2_gate'], lw['w_ff2_up'], lw['w_ff2_down'])
    h = h + jax.nn.sigmoid(rmsnorm(h, lw['g_ple']) @ lw['w_ple_gate']) * (pe @ lw['w_ple_proj'])
    return h, ckv, krope, C, n, m


def setup_inputs(seed: int = 0) -> dict:
    key = jax.random.key(seed)
    ks = iter(jax.random.split(key, 64))
    f32 = jnp.float32

    def normal(shape, scale=1.0):
        return jax.random.normal(next(ks), shape, f32) * scale

    def gain(shape):
        return 1.0 + normal(shape, 0.02)

    def dense(fan_in, fan_out):
        return normal((DEPTH, fan_in, fan_out), fan_in ** -0.5)

    n_pages = PAST_LEN // PAGE_SIZE
    n_used = DEC_BATCH * n_pages
    n_phys = (n_used * 5) // 4
    page_table = jax.random.permutation(next(ks), n_phys)[:n_used].reshape(DEC_BATCH, n_pages).astype(jnp.int32)
    b_f = jnp.broadcast_to(jnp.linspace(3.0, 6.0, M_HEADS, dtype=f32), (DEPTH, M_HEADS)) + normal((DEPTH, M_HEADS), 0.01)
    return {
        'x_prompt': normal((BATCH, SEQ, D_MODEL)),
        'x_sample': normal((DEC_BATCH, DEC_SEQ, D_MODEL)),
        'p_prompt': normal((DEPTH, BATCH, SEQ, PLE_DIM)),
        'p_sample': normal((DEPTH, DEC_BATCH, DEC_SEQ, PLE_DIM)),
        'cache_ckv': normal((DEPTH, n_phys, PAGE_SIZE, KV_LORA)),
        'cache_krope': normal((DEPTH, n_phys, PAGE_SIZE, ROPE_DIM)),
        'state_C': normal((DEPTH, DEC_BATCH, M_HEADS, M_DIM, M_DIM), 0.5),
        'state_n': normal((DEPTH, DEC_BATCH, M_HEADS, M_DIM), 0.5),
        'state_m': normal((DEPTH, DEC_BATCH, M_HEADS), 0.5),
        'page_table': page_table,
        'g_ff1': gain((DEPTH, D_MODEL)),
        'w_ff1_gate': dense(D_MODEL, D_FF),
        'w_ff1_up': dense(D_MODEL, D_FF),
        'w_ff1_down': dense(D_FF, D_MODEL),
        'g_mix': gain((DEPTH, D_MODEL)),
        'w_in': dense(D_MODEL, IN_COLS),
        'g_q': gain((DEPTH, Q_LORA)),
        'w_uq': dense(Q_LORA, MLA_HEADS * (NOPE_DIM + ROPE_DIM)),
        'g_kv': gain((DEPTH, KV_LORA)),
        'w_uk': dense(KV_LORA, MLA_HEADS * NOPE_DIM),
        'w_uv': dense(KV_LORA, MLA_HEADS * V_DIM),
        'b_gate_i': normal((DEPTH, M_HEADS), 0.1),
        'b_gate_f': b_f,
        'g_attn_out': gain((DEPTH, MLA_WIDTH)),
        'g_mlstm_out': gain((DEPTH, M_HEADS, M_DIM)),
        'w_out': dense(MIX_WIDTH, D_MODEL),
        'g_ff2': gain((DEPTH, D_MODEL)),
        'w_ff2_gate': dense(D_MODEL, D_FF),
        'w_ff2_up': dense(D_MODEL, D_FF),
        'w_ff2_down': dense(D_FF, D_MODEL),
        'g_ple': gain((DEPTH, D_MODEL)),
        'w_ple_gate': dense(D_MODEL, D_MODEL),
        'w_ple_proj': dense(PLE_DIM, D_MODEL),
        'g_final': gain((D_MODEL,)),
    }


def reference(x_prompt, x_sample, p_prompt, p_sample, cache_ckv, cache_krope, state_C, state_n, state_m,
              page_table, g_ff1, w_ff1_gate, w_ff1_up, w_ff1_down, g_mix, w_in, g_q, w_uq, g_kv, w_uk, w_uv,
              b_gate_i, b_gate_f, g_attn_out, g_mlstm_out, w_out, g_ff2, w_ff2_gate, w_ff2_up, w_ff2_down,
              g_ple, w_ple_gate, w_ple_proj, g_final):
    f32 = jnp.float32
    pos_p = jnp.arange(SEQ)
    pos_s = PAST_LEN + jnp.arange(DEC_SEQ)
    hp, hs = x_prompt, x_sample
    ckv_p, kr_p, C_p, n_p, m_p = [], [], [], [], []
    ckv_s, kr_s, C_s, n_s, m_s = [], [], [], [], []
    for i in range(DEPTH):
        lw = dict(g_ff1=g_ff1[i], w_ff1_gate=w_ff1_gate[i], w_ff1_up=w_ff1_up[i], w_ff1_down=w_ff1_down[i],
                  g_mix=g_mix[i], w_in=w_in[i], g_q=g_q[i], w_uq=w_uq[i], g_kv=g_kv[i], w_uk=w_uk[i], w_uv=w_uv[i],
                  b_gate_i=b_gate_i[i], b_gate_f=b_gate_f[i], g_attn_out=g_attn_out[i], g_mlstm_out=g_mlstm_out[i],
                  w_out=w_out[i], g_ff2=g_ff2[i], w_ff2_gate=w_ff2_gate[i], w_ff2_up=w_ff2_up[i],
                  w_ff2_down=w_ff2_down[i], g_ple=g_ple[i], w_ple_gate=w_ple_gate[i], w_ple_proj=w_ple_proj[i])
        C0 = jnp.zeros((BATCH, M_HEADS, M_DIM, M_DIM), f32)
        n0 = jnp.zeros((BATCH, M_HEADS, M_DIM), f32)
        m0 = jnp.zeros((BATCH, M_HEADS), f32)
        hp, c1, r1, C1, n1, m1 = trunk_layer(hp, p_prompt[i], pos_p, prompt_attention, C0, n0, m0, lw)
        ckv_p.append(c1); kr_p.append(r1); C_p.append(C1); n_p.append(n1); m_p.append(m1)
        attn_s = functools.partial(sample_attention, cache_c=cache_ckv[i], cache_r=cache_krope[i], page_table=page_table)
        hs, c2, r2, C2, n2, m2 = trunk_layer(hs, p_sample[i], pos_s, attn_s, state_C[i].astype(f32),
                                             state_n[i].astype(f32), state_m[i].astype(f32), lw)
        ckv_s.append(c2); kr_s.append(r2); C_s.append(C2); n_s.append(n2); m_s.append(m2)
    y_prompt = rmsnorm(hp, g_final)
    y_sample = rmsnorm(hs, g_final)
    return (y_prompt, y_sample,
            jnp.stack(ckv_p), jnp.stack(kr_p), jnp.stack(C_p), jnp.stack(n_p), jnp.stack(m_p),
            jnp.stack(ckv_s), jnp.stack(kr_s), jnp.stack(C_s), jnp.stack(n_s), jnp.stack(m_s))
```

```python
from contextlib import ExitStack
import os
import numpy as np
import ml_dtypes
import concourse.bass as bass
import concourse.mybir as mybir
from concourse.bass_utils import run_bass_kernel_spmd

F32 = mybir.dt.float32
BF16 = mybir.dt.bfloat16
I32 = mybir.dt.int32
ALU = mybir.AluOpType
AF = mybir.ActivationFunctionType
AX = mybir.AxisListType

NCORES = 8
D = 1024
KC = 8
FF = 2816
FC = 22
TP = 2048
TS = 16
TT = TP + TS
SEQ = 8192
EPS = 1e-6
IN_COLS = 2728
OFF_KV, OFF_KR, OFF_MQ, OFF_MK, OFF_MV, OFF_MO, OFF_MI, OFF_MF = 384, 640, 672, 1184, 1696, 2208, 2720, 2724
SCALE = 96.0 ** -0.5
NPAGE = 64
NEG = -30000.0

ENGS = ("pe", "act", "dve", "pool", "sp")


class Rec:
    __slots__ = ("eng", "fn", "deps", "is_dma", "sem", "val", "marked", "inc", "snap")

    def __init__(self, eng, fn):
        self.eng = eng
        self.fn = fn
        self.deps = []
        self.is_dma = False
        self.sem = None
        self.val = None
        self.marked = False
        self.inc = 16
        self.snap = None


class Plan:
    def __init__(self, nc):
        self.nc = nc
        self.streams = {e: [] for e in ENGS}
        self.res = {}
        self.dma_sem_counts = {}
        self.finals = []

    PSUM_STR = {"ss_ps", "ss", "TR", "BC", "BT", "BS", "BU", "PSA", "PSB", "PSC", "PSD", "PSE", "PSF", "PSG", "PSH", "scC", "scD"}
    PSUM_TUP = {"gu", "dn", "acc", "S", "O", "bank", "psr"}

    @staticmethod
    def _is_psum(key):
        if isinstance(key, str):
            return key in Plan.PSUM_STR
        return isinstance(key, tuple) and len(key) > 0 and key[0] in Plan.PSUM_TUP

    def _track(self, rec, reads, writes):
        reads = list(reads)
        writes = list(writes) + [r for r in reads if Plan._is_psum(r) and r not in writes]
        deps = []
        for r in reads:
            st = self.res.get(r)
            if st is not None and st[0] is not None:
                deps.append(st[0])
        for w in writes:
            st = self.res.get(w)
            if st is not None:
                if st[0] is not None:
                    deps.append(st[0])
                deps.extend(st[1])
        seen = set()
        for d in deps:
            if d is rec or id(d) in seen:
                continue
            seen.add(id(d))
            rec.deps.append(d)
        for r in reads:
            st = self.res.setdefault(r, [None, []])
            st[1].append(rec)
        for w in writes:
            self.res[w] = [rec, []]

    def op(self, eng, fn, reads=(), writes=()):
        rec = Rec(eng, fn)
        self.streams[eng].append(rec)
        self._track(rec, reads, writes)
        return rec

    def dma(self, queue, fn, semkey, reads=(), writes=(), final=False, n=1, inc=None):
        rec = Rec(queue, fn)
        rec.is_dma = True
        rec.inc = 16 * n if inc is None else inc
        self.streams[queue].append(rec)
        c = self.dma_sem_counts.get(semkey, 0) + rec.inc
        self.dma_sem_counts[semkey] = c
        rec.sem = semkey
        rec.val = c
        self._track(rec, reads, writes)
        if final:
            self.finals.append(rec)
        return rec

    def barrier_final(self, eng="sp"):
        rec = Rec(eng, None)
        rec.deps = list(self.finals)
        rec.snap = dict(self.dma_sem_counts)
        self.streams[eng].append(rec)

    def barrier(self):
        last = []
        for e in ENGS:
            for r in reversed(self.streams[e]):
                if r.fn is not None and not r.is_dma:
                    last.append(r)
                    break
        snap = dict(self.dma_sem_counts)
        for e in ENGS:
            rec = Rec(e, None)
            rec.deps = [r for r in last if not r.is_dma]
            rec.snap = snap
            self.streams[e].append(rec)
        self.res = {}

    def emit(self, stack):
        nc = self.nc
        for e in ENGS:
            for rec in self.streams[e]:
                for d in rec.deps:
                    if not d.is_dma:
                        if d.eng == rec.eng and d.eng == "pe":
                            continue
                        d.marked = True
        semh = {}
        for e in ENGS:
            semh[("eng", e)] = stack.enter_context(nc.semaphore("sem_" + e))
            cnt = 0
            for rec in self.streams[e]:
                if not rec.is_dma and rec.marked:
                    cnt += 1
                    rec.sem = ("eng", e)
                    rec.val = cnt
        for i, k in enumerate(self.dma_sem_counts):
            semh[k] = stack.enter_context(nc.semaphore("d%d" % i))
        self.n_sems = len(semh)
        block = stack.enter_context(nc.Block())
        stats = {}

        def run_stream(e):
            def body(eng):
                known = {}
                nw = 0
                for rec in self.streams[e]:
                    need = {}
                    for d in rec.deps:
                        if (not d.is_dma) and d.eng == e and e == "pe":
                            continue
                        if d.val is None:
                            continue
                        if need.get(d.sem, 0) < d.val:
                            need[d.sem] = d.val
                    if rec.snap is not None:
                        for k_, v_ in rec.snap.items():
                            if need.get(k_, 0) < v_:
                                need[k_] = v_
                    if rec.snap is not None and e == "sp" and os.environ.get("KDBGB"):
                        print("BARRIER sp need", {str(s_): v_ for s_, v_ in need.items() if isinstance(s_, tuple) and s_[0] == "eng"}, "known", {str(s_): v_ for s_, v_ in known.items() if isinstance(s_, tuple) and s_[0] == "eng"})
                    for s, v in need.items():
                        if known.get(s, 0) >= v:
                            continue
                        eng.wait_ge(semh[s], v)
                        known[s] = v
                        nw += 1
                    if rec.fn is None:
                        continue
                    ins = rec.fn(eng)
                    if rec.is_dma:
                        if isinstance(ins, (list, tuple)):
                            ins = [i_ for i_ in ins if i_ is not None]
                            assert len(ins) * 16 == rec.inc
                            for i_ in ins:
                                i_.then_inc(semh[rec.sem], 16)
                        else:
                            ins.then_inc(semh[rec.sem], rec.inc)
                    elif rec.marked:
                        ins.then_inc(semh[rec.sem], 1)
                stats[e] = (len(self.streams[e]), nw)
            return body

        block.tensor(run_stream("pe"))
        block.scalar(run_stream("act"))
        block.vector(run_stream("dve"))
        block.gpsimd(run_stream("pool"))
        block.sync(run_stream("sp"))
        self.stats = stats


class Ring:
    def __init__(self, name, n):
        self.name, self.n, self.i = name, n, 0

    def next(self):
        s = self.i % self.n
        self.i += 1
        return s, (self.name, s)


class Arena:
    LO, HI = 16512, 229344

    def __init__(self, nc):
        self.nc, self.top, self.n = nc, Arena.LO, 0

    def sb(self, name, shape, dt):
        esz = 2 if dt == BF16 else 4
        nbytes = esz
        for d in shape[1:]:
            nbytes *= d
        off = (self.top + 31) // 32 * 32
        assert off + nbytes <= Arena.HI, ("SBUF overflow", name, off, nbytes)
        self.top = off + nbytes
        self.n += 1
        return self.nc.alloc_sbuf_tensor_at("%s_%d" % (name, self.n), list(shape), dt, offset=off)

    def mark(self):
        return self.top

    def release(self, m):
        self.top = m


def token_tiles():
    tiles = [(i * 256, 256) for i in range(TP // 256)]
    tiles.append((TP, TS))
    return tiles


def build(stages="A"):
    import os
    nc = bass.Bass("TRN2", target_bir_lowering=False)
    st = ExitStack()
    dram_in = {}
    dram_out = {}

    def din(name, shape, dt=F32):
        dram_in[name] = (tuple(shape), dt)
        return nc.dram_tensor(name, list(shape), dt, kind="ExternalInput").ap()

    def dout(name, shape, dt=F32):
        dram_out[name] = (tuple(shape), dt)
        return nc.dram_tensor(name, list(shape), dt, kind="ExternalOutput").ap()

    xT = din("xT", [D, TT])
    gc = din("gc", [128, 48])
    wg1 = din("wg1", [D, FF]); wu1 = din("wu1", [D, FF]); wd1 = din("wd1", [FF, D])
    ones_bf = din("ones_bf", [128, 128], BF16)
    ident_bf_d = din("ident_bf", [128, 128], BF16)
    ident_f_d = din("ident_f", [128, 128], F32)
    wB_d = din("wB", [D, 1280])
    wQm_d = din("wQm", [384, 320])
    wKm_d = din("wKm", [256, 256])
    wVm_d = din("wVm", [256, 128])
    gkv_bc_d = din("gkv_bc", [128, 256])
    CT_d = din("CT", [32, SEQ]); ST_d = din("ST", [32, SEQ])
    Ctok_d = din("Ctok", [SEQ, 32]); Stok_d = din("Stok", [SEQ, 32])
    amask_d = din("amask", [128, 4, 512], BF16)
    gml_bc_d = din("gml_bc", [128, 128])
    trimask_d = din("trimask", [128, 128])
    gb_d = din("gb", [1, 4])
    wout_d = din("wout", [D, D]); wpg_d = din("wpg", [D, D]); wpp_d = din("wpp", [256, D])
    wg2 = din("wg2", [D, FF]); wu2 = din("wu2", [D, FF]); wd2 = din("wd2", [FF, D])
    gattn_bc_d = din("gattn_bc", [128, 512])
    peT = din("peT", [256, TT])
    w_in_d = din("w_in", [D, IN_COLS])
    w_uq_d = din("w_uq", [384, 768]); wukT_d = din("wukT", [128, 2048]); w_uv_d = din("w_uv", [256, 512])
    gq_bc_d = din("gq_bc", [TS, 384]); c8_d = din("c8", [TS, 2, 8, 32]); iota_d = din("iota", [128, 1]); negrow_d = din("negrow", [1, 128])
    pt_d = din("pt", [1, 1024], I32)
    cache_d = din("cache", [int(os.environ.get("KNPHYS", 10240)) * 128, 288])
    stC_d = din("stC", [64, 16384]); stn_d = din("stn", [64, 128]); stm_d = din("stm", [64, 1])
    gbs_d = din("gbs", [64, 4]); gml_pm_d = din("gml_pm", [64, 128])
    hT_d = nc.dram_tensor("hT_d", [D, TT], F32).ap()
    uT_loc = [nc.dram_tensor("uT_loc%d" % i, [128, 2 * TP], BF16) for i in range(4)]
    ag1_out = [nc.dram_tensor("ag1_out%d" % i, [512, 2 * TP], BF16) for i in range(4)]
    ag2_in = [nc.dram_tensor("ag2_in%d" % i, [128, 4096], BF16) for i in range(4)]
    ag2_out = [nc.dram_tensor("ag2_out%d" % i, [512, 4096], BF16) for i in range(4)]
    ckv_p_out = dout("ckv_p_out", [SEQ, 256])
    kr_p_out = dout("kr_p_out", [SEQ, 32])
    C_p_out = dout("C_p_out", [128, 128])
    n_p_out = dout("n_p_out", [128, 1])
    m_p_out = dout("m_p_out", [1, 1])
    yT_out = dout("yT_out", [D, TT])
    mixs_d = nc.dram_tensor("mixs_d", [TS, D], BF16)
    us_d = nc.dram_tensor("us_d", [128, KC * TS], BF16)
    zs_d = nc.dram_tensor("zs_d", [TS, IN_COLS], F32)
    ckv_s_out = dout("ckv_s_out", [TS, 256]); kr_s_out = dout("kr_s_out", [TS, 32])
    C_s_out = dout("C_s_out", [64, 16384]); n_s_out = dout("n_s_out", [64, 128]); m_s_out = dout("m_s_out", [64, 1])
    mixloc = [nc.dram_tensor("mixloc%d" % i, [512, 4, 256], BF16) for i in range(4)]
    DBG = bool(os.environ.get("KDBG"))
    if DBG:
        dbg_h = dout("dbg_h", [D, TT])
        dbg_u = dout("dbg_u", [D, TT], BF16)
        dbg_a = dout("dbg_a", [SEQ, 256], BF16)
        dbg_ub = dout("dbg_ub", [128, KC, 512], BF16)
        dbg_pt = dout("dbg_pt", [128, 324], F32)

    with st:
        P = Plan(nc)
        A = Arena(nc)
        sb = A.sb

        gc_t = sb("gc_t", [128, 48], F32)
        ones_t = sb("ones_t", [128, 128], BF16)
        ident_bf = sb("ident_bf", [128, 128], BF16)
        ident_f = sb("ident_f", [128, 128], F32)
        P.dma("sp", lambda e: e.dma_start(out=gc_t[:], in_=gc), "c0", writes=["gc_t"])
        P.dma("sp", lambda e: e.dma_start(out=ones_t[:], in_=ones_bf), "c1", writes=["ones_t"])
        P.dma("sp", lambda e: e.dma_start(out=ident_bf[:], in_=ident_bf_d), "c2", writes=["ident_bf"])
        P.dma("sp", lambda e: e.dma_start(out=ident_f[:], in_=ident_f_d), "c3", writes=["ident_f"])
        GC_FF1, GC_MIX, GC_FF2, GC_PLE, GC_FIN, GC_Q, GC_KV = 0, 8, 16, 24, 32, 40, 43
        persist_mark = A.mark()

        def new_phase(nbanks_f32=8):
            P.barrier()
            A.release(persist_mark)
            ph = ExitStack()
            banks = [ph.enter_context(nc.psum_tensor("pb%d_%d" % (i, A.n), [128, 512], F32)) for i in range(nbanks_f32)]
            return ph, banks

        def load_rows(stg, stg_ring, src, r0, c0, ncols, dst_ap, dst_key, nrows=128):
            s, skey = stg_ring.next()
            P.dma("pool", lambda e: e.dma_start(out=stg[0:nrows, s, 0:ncols], in_=src[r0:r0 + nrows, c0:c0 + ncols]),
                  ("stg", s), writes=[skey])
            P.op("pool", lambda e: e.tensor_copy(dst_ap, stg[0:nrows, s, 0:ncols]), reads=[skey], writes=[dst_key])

        def ffn_sweep(wg, wu, wd, gcol_in, src_dram, dst_dram, emit_u):
            ph, pb = new_phase(5)
            with ph:
                wg_t = sb("wg_t", [128, KC, FF], BF16)
                wu_t = sb("wu_t", [128, KC, FF], BF16)
                wd_t = sb("wd_t", [128, FC, D], BF16)
                stg = sb("stg", [128, 2, 1408], F32)
                stg_ring = Ring("stg", 2)
                NT = 256
                h_t = sb("h_t", [128, 2, KC, NT], F32)
                h_ring = Ring("h", 2)
                sq_t = sb("sq_t", [128, KC, NT], BF16)
                xn_t = sb("xn_t", [128, KC, NT], BF16)
                hid_t = sb("hid_t", [128, FC, NT], BF16)
                sg_t = sb("sg_t", [128, 2, NT], F32)
                sg_ring = Ring("sg", 2)
                rs_t = sb("rs_t", [128, 2, NT], F32)
                u_t = sb("u_t", [128, KC, NT], BF16)
                gu_ring = Ring("gu", 2)
                ss_ps = pb[2]
                dn_ring = Ring("dn", 2)

                for half in range(2):
                    c0 = half * 1408
                    for k in range(KC):
                        load_rows(stg, stg_ring, wg, k * 128, c0, 1408, wg_t[:, k, c0:c0 + 1408], ("wg", half))
                        load_rows(stg, stg_ring, wu, k * 128, c0, 1408, wu_t[:, k, c0:c0 + 1408], ("wu", half))
                for f in range(FC):
                    load_rows(stg, stg_ring, wd, f * 128, 0, 1024, wd_t[:, f, :], ("wd", f))

                def rms_fm(src_ap_fn, src_key, gcol, dst_t, dst_key, n):
                    P.op("act", lambda e: e.activation(sq_t[:, :, 0:n], src_ap_fn(slice(0, KC)), AF.Square),
                         reads=[src_key], writes=["sq"])
                    for k in range(KC):
                        P.op("pe", lambda e, k=k: e.matmul(ss_ps[:, 0:n], ones_t[:], sq_t[:, k, 0:n], start=(k == 0), stop=(k == KC - 1)),
                             reads=["sq", "ones_t"], writes=["ss_ps"])
                    P.op("act", lambda e: e.activation(rs_t[:, 0, 0:n], ss_ps[:, 0:n], AF.Sqrt, bias=EPS, scale=1.0 / D),
                         reads=["ss_ps"], writes=["rs0"])
                    P.op("dve", lambda e: e.reciprocal(rs_t[:, 1, 0:n], rs_t[:, 0, 0:n]), reads=["rs0"], writes=["rs1"])
                    for k in range(KC):
                        P.op("dve", lambda e, k=k: e.scalar_tensor_tensor(out=dst_t[:, k, 0:n], in0=src_ap_fn(k), scalar=gc_t[:, gcol + k:gcol + k + 1],
                                                                         in1=rs_t[:, 1, 0:n], op0=ALU.mult, op1=ALU.mult),
                             reads=[src_key, "rs1", "gc_t"], writes=[dst_key])

                tiles = token_tiles()
                if os.environ.get("KMAXT"):
                    tiles = tiles[:int(os.environ["KMAXT"])]
                def do_tile(t0, n, hs, hkey):
                    P.dma("sp", lambda e, hs=hs, t0=t0, n=n: e.dma_start(out=h_t[:, hs, :, 0:n], in_=src_dram[:, t0:t0 + n].rearrange("(k p) n -> p k n", p=128)),
                          ("h_ld", hs), reads=["src_dram"], writes=[hkey])
                    rms_fm(lambda k, hs=hs, n=n: h_t[:, hs, k, 0:n], hkey, gcol_in, xn_t, "xn", n)
                    for f in range(FC):
                        g, gkey = gu_ring.next()
                        half = 0 if f < 11 else 1
                        for k in range(KC):
                            P.op("pe", lambda e, k=k, f=f, g=g: e.matmul(pb[g][:, 0:n], wg_t[:, k, f * 128:(f + 1) * 128], xn_t[:, k, 0:n],
                                                                        start=(k == 0), stop=(k == KC - 1)),
                                 reads=["xn", ("wg", half)], writes=[gkey])
                        for k in range(KC):
                            P.op("pe", lambda e, k=k, f=f, g=g: e.matmul(pb[g][:, 256:256 + n], wu_t[:, k, f * 128:(f + 1) * 128], xn_t[:, k, 0:n],
                                                                        start=(k == 0), stop=(k == KC - 1)),
                                 reads=["xn", ("wu", half)], writes=[gkey])
                        s, skey = sg_ring.next()
                        P.op("act", lambda e, g=g, s=s: e.activation(sg_t[:, s, 0:n], pb[g][:, 0:n], AF.Silu), reads=[gkey], writes=[skey])
                        P.op("dve", lambda e, g=g, s=s, f=f: e.tensor_tensor(out=hid_t[:, f, 0:n], in0=sg_t[:, s, 0:n], in1=pb[g][:, 256:256 + n], op=ALU.mult),
                             reads=[skey, gkey], writes=[("hid", f)])
                    for o in range(KC):
                        d, dkey = dn_ring.next()
                        for f in range(FC):
                            P.op("pe", lambda e, f=f, o=o, d=d: e.matmul(pb[3 + d][:, 0:n], wd_t[:, f, o * 128:(o + 1) * 128], hid_t[:, f, 0:n],
                                                                        start=(f == 0), stop=(f == FC - 1)),
                                 reads=[("hid", f), ("wd", f)], writes=[dkey])
                        P.op("dve", lambda e, o=o, d=d, hs=hs: e.scalar_tensor_tensor(out=h_t[:, hs, o, 0:n], in0=pb[3 + d][:, 0:n], scalar=0.5,
                                                                                     in1=h_t[:, hs, o, 0:n], op0=ALU.mult, op1=ALU.add),
                             reads=[dkey, hkey], writes=[hkey])
                    P.dma("sp", lambda e, hs=hs, t0=t0, n=n: e.dma_start(out=dst_dram[:, t0:t0 + n].rearrange("(k p) n -> p k n", p=128), in_=h_t[:, hs, :, 0:n]),
                          ("h_st", hs), reads=[hkey], writes=["dst_dram"])
                    if emit_u:
                        rms_fm(lambda k, hs=hs, n=n: h_t[:, hs, k, 0:n], hkey, GC_MIX, u_t, "u", n)
                        if t0 >= TP:
                            P.dma("sp", lambda e, n=n: e.dma_start(out=us_d.ap().rearrange("p (k n) -> p k n", k=KC), in_=u_t[:, :, 0:n]), "us_st", reads=["u"], writes=["us_d"])
                        if t0 < TP:
                            P.dma("sp", lambda e, t0=t0, n=n: [e.dma_start(out=uT_loc[i].ap().rearrange("p (k n) -> p k n", k=2)[:, :, t0:t0 + n], in_=u_t[:, 2 * i:2 * i + 2, 0:n])
                                                                for i in range(4)],
                                  "u_st", reads=["u"], writes=["uT_loc"], n=4)
                        if DBG:
                            P.dma("sp", lambda e, hs=hs, t0=t0, n=n: e.dma_start(out=dbg_h[:, t0:t0 + n].rearrange("(k p) n -> p k n", p=128), in_=h_t[:, hs, :, 0:n]),
                                  ("h_st2", hs), reads=[hkey])
                            P.dma("sp", lambda e, t0=t0, n=n: e.dma_start(out=dbg_u[:, t0:t0 + n].rearrange("(k p) n -> p k n", p=128), in_=u_t[:, :, 0:n]),
                                  "u_st2", reads=["u"])

                for (t0, n) in tiles:
                    hs, hkey = h_ring.next()
                    do_tile(t0, n, hs, hkey)

        if "A" in stages:
            ffn_sweep(wg1, wu1, wd1, GC_FF1, xT, hT_d, True)

        GROUPS = [[0, 1, 2, 3], [4, 5, 6, 7]]
        if "G" in stages:
            P.barrier()
            for i in range(4):
                P.dma("pool", lambda e, i=i: e.collective_compute("AllGather", ALU.bypass, replica_groups=GROUPS,
                                                                  ins=[uT_loc[i].ap()], outs=[ag1_out[i].ap()]),
                      ("cc1", i), reads=["uT_loc"], writes=[("ag1_out", i)], inc=1)

        def phase_B():
            ph, pb = new_phase(8)
            with ph:
                stg = sb("stgb", [128, 2, 1408], F32)
                stg_ring = Ring("stg", 2)
                wB_t = sb("wB_t", [128, KC, 704], BF16)
                wQ_t = sb("wQ_t", [128, 3, 320], BF16)
                wK_t = sb("wK_t", [128, 2, 256], BF16)
                wV_t = sb("wV_t", [128, 2, 128], BF16)
                gkv_bc = sb("gkv_bc", [128, 256], F32)
                amask = sb("amask", [128, 4, 512], BF16)
                QT = sb("QT", [128, 2, SEQ], BF16)
                KT = sb("KT", [128, 2, SEQ], BF16)
                VA = sb("VA", [128, 2, 64, 65], BF16)
                qkmax = sb("qkmax", [128, 4], F32)
                tmax = sb("tmax", [128, 2], F32)
                negm = sb("negm", [128, 4], F32)
                u_t = sb("ub_t", [128, 2, KC, 512], BF16)
                u_ring = Ring("ub", 2)
                zq_sb = sb("zq_sb", [128, 3, 512], F32)
                sqb = sb("sqb", [128, 3, 512], BF16)
                qlat = sb("qlat", [128, 3, 512], BF16)
                ckvT = sb("ckvT", [128, 2, 512], BF16)
                rsb = sb("rsb", [128, 2, 512], F32)
                cs_t = sb("cs_t", [32, 2, 2, 512], F32)
                cs_ring = Ring("cs", 2)
                cst_t = sb("cst_t", [128, 2, 2, 4, 32], F32)
                rp_t = sb("rp_t", [32, 2, 512], F32)
                otok = sb("otok", [128, 2, 320], F32)
                otok_ring = Ring("otok", 2)
                junk = sb("junk", [128, 256], F32)
                st1 = sb("st1", [128, 4], F32)
                PT = sb("PT", [128, 3, 512], BF16)
                pt_ring = Ring("pt", 3)
                osb = sb("osb", [65, 512], F32)
                atok = sb("atok", [128, 2, 4, 64], BF16)
                atok_ring = Ring("atok", 2)
                rd = sb("rd", [128, 4], F32)
                acc_ring = Ring("acc", 5)
                SS, TR, O0 = 5, 6, 7

                for k in range(KC):
                    load_rows(stg, stg_ring, wB_d, k * 128, 0, 704, wB_t[:, k, :], "wB")
                for c in range(3):
                    load_rows(stg, stg_ring, wQm_d, c * 128, 0, 320, wQ_t[:, c, :], "wQ")
                for c in range(2):
                    load_rows(stg, stg_ring, wKm_d, c * 128, 0, 256, wK_t[:, c, :], "wK")
                    load_rows(stg, stg_ring, wVm_d, c * 128, 0, 128, wV_t[:, c, :], "wV")
                P.dma("sp", lambda e: e.dma_start(out=gkv_bc[:], in_=gkv_bc_d), "c0", writes=["gkv_bc"])
                P.dma("sp", lambda e: e.dma_start(out=amask[:], in_=amask_d), "c1", writes=["amask"])
                P.op("pool", lambda e: e.memset(QT[:, 0, :], 0.0), writes=["QT"])
                P.op("pool", lambda e: e.memset(QT[:, 1, :], 0.0), writes=["QT"])
                P.op("pool", lambda e: e.memset(KT[:, 0, :], 0.0), writes=["KT"])
                P.op("pool", lambda e: e.memset(KT[:, 1, :], 0.0), writes=["KT"])
                P.op("pool", lambda e: e.memset(VA[:], 1.0), writes=["VA"])
                P.op("dve", lambda e: e.memset(qkmax[:], 0.0), writes=["qkmax"])

                def proj_fm(out_bank, lhs_fn, rhs_fn, nk, m, reads, okey, n=512):
                    for k in range(nk):
                        P.op("pe", lambda e, k=k: e.matmul(pb[out_bank][0:m, 0:n], lhs_fn(k), rhs_fn(k), start=(k == 0), stop=(k == nk - 1)),
                             reads=reads, writes=[okey])

                def norm_fm(src_t, nch, dim, gcol, dst_t, dst_key, skey):
                    P.op("act", lambda e: e.activation(sqb[:, 0:nch, :], src_t[:, 0:nch, :], AF.Square), reads=[skey], writes=["sqb"])
                    for c in range(nch):
                        P.op("pe", lambda e, c=c: e.matmul(pb[SS][:, :], ones_t[:], sqb[:, c, :], start=(c == 0), stop=(c == nch - 1)),
                             reads=["sqb", "ones_t"], writes=["ss"])
                    P.op("act", lambda e: e.activation(rsb[:, 0, :], pb[SS][:, :], AF.Sqrt, bias=EPS, scale=1.0 / dim), reads=["ss"], writes=["rsb0"])
                    P.op("dve", lambda e: e.reciprocal(rsb[:, 1, :], rsb[:, 0, :]), reads=["rsb0"], writes=["rsb1"])
                    for c in range(nch):
                        P.op("dve", lambda e, c=c: e.scalar_tensor_tensor(out=dst_t[:, c, :], in0=src_t[:, c, :], scalar=gc_t[:, gcol + c:gcol + c + 1],
                                                                         in1=rsb[:, 1, :], op0=ALU.mult, op1=ALU.mult),
                             reads=[skey, "rsb1", "gc_t"], writes=[dst_key])

                def bound_update(src_ap_fn, skeys, col):
                    P.op("act", lambda e: e.activation(sqb[:, 0, :], src_ap_fn(), AF.Square), reads=list(skeys), writes=["sqb"])
                    P.op("pe", lambda e: e.matmul(pb[SS][:, :], ones_t[:], sqb[:, 0, :], start=True, stop=True), reads=["sqb", "ones_t"], writes=["ss"])
                    P.op("dve", lambda e: e.reduce_max(out=tmax[:, 0:1], in_=pb[SS][:, :], axis=AX.X), reads=["ss"], writes=["tmax"])
                    P.op("dve", lambda e: e.tensor_tensor(out=qkmax[:, col:col + 1], in0=qkmax[:, col:col + 1], in1=tmax[:, 0:1], op=ALU.max),
                         reads=["tmax", "qkmax"], writes=["qkmax"])

                ntt = int(os.environ.get("KNTT", 16))
                for tt in range(ntt):
                    rank, c0 = tt // 4, (tt % 4) * 512
                    tok0 = tt * 512
                    cols = slice(tok0, tok0 + 512)
                    us, ukey = u_ring.next()
                    P.dma("sp", lambda e, us=us, rank=rank, c0=c0: [e.dma_start(
                        out=u_t[:, us, 2 * i:2 * i + 2, :], in_=ag1_out[i].ap()[rank * 128:(rank + 1) * 128, :].rearrange("p (k n) -> p k n", k=2)[:, :, c0:c0 + 512])
                        for i in range(4)],
                        ("ub_ld", us), reads=[("ag1_out", i) for i in range(4)], writes=[ukey], n=4)
                    if DBG and tt == 1:
                        P.dma("sp", lambda e, us=us: e.dma_start(out=dbg_ub, in_=u_t[:, us, :, :]), "dbgub", reads=[ukey])
                    cs, cskey = cs_ring.next()
                    P.dma("sp", lambda e, cs=cs, tok0=tok0: [e.dma_start(out=cs_t[:, cs, 0, :], in_=CT_d[:, tok0:tok0 + 512]),
                                                            e.dma_start(out=cs_t[:, cs, 1, :], in_=ST_d[:, tok0:tok0 + 512]),
                                                            e.dma_start(out=cst_t[:, cs, 0, :, :], in_=Ctok_d[tok0:tok0 + 512, :].rearrange("(j p) r -> p j r", p=128)),
                                                            e.dma_start(out=cst_t[:, cs, 1, :, :], in_=Stok_d[tok0:tok0 + 512, :].rearrange("(j p) r -> p j r", p=128))],
                          ("cs_ld", cs), writes=[cskey], n=4)
                    for c in range(3):
                        a, akey = acc_ring.next()
                        proj_fm(a, lambda k, c=c: wB_t[:, k, c * 128:(c + 1) * 128], lambda k, us=us: u_t[:, us, k, :], KC, 128, [ukey, "wB"], akey)
                        P.op("act", lambda e, a=a, c=c: e.activation(zq_sb[:, c, :], pb[a][:, :], AF.Copy), reads=[akey], writes=["zq_sb"])
                    norm_fm(zq_sb, 3, 384, GC_Q, qlat, "qlat", "zq_sb")
                    for c in range(2):
                        a, akey = acc_ring.next()
                        proj_fm(a, lambda k, c=c: wB_t[:, k, 384 + c * 128:384 + (c + 1) * 128], lambda k, us=us: u_t[:, us, k, :], KC, 128, [ukey, "wB"], akey)
                        P.op("act", lambda e, a=a, c=c: e.activation(zq_sb[:, c, :], pb[a][:, :], AF.Copy), reads=[akey], writes=["zq_sb"])
                    norm_fm(zq_sb, 2, 256, GC_KV, ckvT, "ckvT", "zq_sb")
                    a1, a1key = acc_ring.next()
                    proj_fm(a1, lambda k: wB_t[:, k, 640:672], lambda k, us=us: u_t[:, us, k, :], KC, 32, [ukey, "wB"], a1key)
                    a2, a2key = acc_ring.next()
                    proj_fm(a2, lambda k: wB_t[:, k, 672:704], lambda k, us=us: u_t[:, us, k, :], KC, 32, [ukey, "wB"], a2key)
                    P.op("dve", lambda e, a1=a1, cs=cs: e.tensor_tensor(out=rp_t[:, 0, :], in0=pb[a1][0:32, :], in1=cs_t[:, cs, 0, :], op=ALU.mult),
                         reads=[a1key, cskey], writes=["rp0"])
                    P.op("dve", lambda e, a2=a2, cs=cs: e.tensor_tensor(out=rp_t[:, 1, :], in0=pb[a2][0:32, :], in1=cs_t[:, cs, 1, :], op=ALU.mult),
                         reads=[a2key, cskey], writes=["rp1"])
                    P.op("dve", lambda e, cols=cols: e.tensor_tensor(out=KT[0:32, 0, cols], in0=rp_t[:, 0, :], in1=rp_t[:, 1, :], op=ALU.add),
                         reads=["rp0", "rp1", "KT"], writes=[("KT", tt)])
                    P.op("pool", lambda e, cols=cols: e.tensor_copy(KT[0:32, 1, cols], KT[0:32, 0, cols]), reads=[("KT", tt), "KT"], writes=[("KT1", tt)])
                    for j in range(4):
                        a, akey = acc_ring.next()
                        proj_fm(a, lambda k, us=us, j=j: u_t[:, us, k, j * 128:(j + 1) * 128], lambda k: wB_t[:, k, 384:704], KC, 128, [ukey, "wB"], akey, n=320)
                        os_, okey = otok_ring.next()
                        P.op("act", lambda e, a=a: e.activation(junk[:], pb[a][:, 0:256], AF.Square), reads=[akey], writes=["junk"])
                        P.op("dve", lambda e: e.reduce_sum(out=st1[:, 0:1], in_=junk[:], axis=AX.X), reads=["junk"], writes=["st1a"])
                        P.op("act", lambda e: e.activation(st1[:, 1:2], st1[:, 0:1], AF.Sqrt, bias=EPS, scale=1.0 / 256), reads=["st1a"], writes=["st1b"])
                        P.op("dve", lambda e: e.reciprocal(st1[:, 2:3], st1[:, 1:2]), reads=["st1b"], writes=["st1c"])
                        P.op("dve", lambda e, a=a, os_=os_: e.scalar_tensor_tensor(out=otok[:, os_, 0:256], in0=pb[a][:, 0:256], scalar=st1[:, 2:3], in1=gkv_bc[:],
                                                                                 op0=ALU.mult, op1=ALU.mult), reads=[akey, "st1c", "gkv_bc"], writes=[okey])
                        P.op("dve", lambda e, a=a, os_=os_, cs=cs, j=j: e.tensor_tensor(out=otok[:, os_, 256:288], in0=pb[a][:, 256:288], in1=cst_t[:, cs, 0, j, :], op=ALU.mult),
                             reads=[akey, cskey], writes=[okey])
                        P.op("dve", lambda e, a=a, os_=os_, cs=cs, j=j: e.tensor_tensor(out=otok[:, os_, 288:320], in0=pb[a][:, 288:320], in1=cst_t[:, cs, 1, j, :], op=ALU.mult),
                             reads=[akey, cskey], writes=[okey])
                        P.op("dve", lambda e, os_=os_: e.tensor_tensor(out=otok[:, os_, 256:288], in0=otok[:, os_, 256:288], in1=otok[:, os_, 288:320], op=ALU.add),
                             reads=[okey], writes=[okey])
                        if DBG and tt == 0 and j == 0:
                            dpt = sb("dpt", [128, 324], F32)
                            P.op("dve", lambda e, a=a: e.tensor_copy(dpt[:, 0:320], pb[a][:, 0:320]), reads=[akey], writes=["dpt"])
                            P.op("dve", lambda e: e.tensor_copy(dpt[:, 320:324], st1[:, 0:4]), reads=["st1c"], writes=["dpt"])
                            P.dma("sp", lambda e: e.dma_start(out=dbg_pt, in_=dpt[:]), "dbgpt", reads=["dpt"])
                        r0 = tok0 + j * 128
                        P.dma("sp", lambda e, os_=os_, r0=r0: [e.dma_start(out=ckv_p_out[r0:r0 + 128, :], in_=otok[:, os_, 0:256]),
                                                              e.dma_start(out=kr_p_out[r0:r0 + 128, :], in_=otok[:, os_, 256:288])],
                              ("otok_st", os_), reads=[okey], n=2)
                    for h in range(2):
                        qa, qakey = acc_ring.next()
                        proj_fm(qa, lambda c, h=h: wQ_t[:, c, h * 160:h * 160 + 128], lambda c: qlat[:, c, :], 3, 128, ["qlat", "wQ"], qakey)
                        qb, qbkey = acc_ring.next()
                        proj_fm(qb, lambda c, h=h: wQ_t[:, c, h * 160 + 128:h * 160 + 160], lambda c: qlat[:, c, :], 3, 32, ["qlat", "wQ"], qbkey)
                        P.op("act", lambda e, qa=qa, h=h, cols=cols: e.activation(QT[64:128, h, cols], pb[qa][64:128, :], AF.Copy), reads=[qakey, "QT"], writes=[("QT", h, tt)])
                        P.op("dve", lambda e, qa=qa, cs=cs: e.tensor_tensor(out=rp_t[:, 0, :], in0=pb[qa][0:32, :], in1=cs_t[:, cs, 0, :], op=ALU.mult),
                             reads=[qakey, cskey], writes=["rp0"])
                        P.op("dve", lambda e, qb=qb, cs=cs: e.tensor_tensor(out=rp_t[:, 1, :], in0=pb[qb][0:32, :], in1=cs_t[:, cs, 1, :], op=ALU.mult),
                             reads=[qbkey, cskey], writes=["rp1"])
                        P.op("dve", lambda e, h=h, cols=cols: e.tensor_tensor(out=QT[0:32, h, cols], in0=rp_t[:, 0, :], in1=rp_t[:, 1, :], op=ALU.add),
                             reads=["rp0", "rp1", "QT"], writes=[("QT", h, tt)])
                        bound_update(lambda h=h, cols=cols: QT[:, h, cols], [("QT", h, tt)], h)
                    for h in range(2):
                        a, akey = acc_ring.next()
                        proj_fm(a, lambda c, h=h: wK_t[:, c, h * 128:(h + 1) * 128], lambda c: ckvT[:, c, :], 2, 128, ["ckvT", "wK"], akey)
                        kt_keys = [("KT", tt)] if h == 0 else [("KT1", tt)]
                        P.op("act", lambda e, a=a, h=h, cols=cols: e.activation(KT[64:128, h, cols], pb[a][64:128, :], AF.Copy), reads=[akey, "KT"], writes=[("KTn", h, tt)])
                        bound_update(lambda h=h, cols=cols: KT[:, h, cols], kt_keys + [("KTn", h, tt)], 2 + h)
                    a, akey = acc_ring.next()
                    for j in range(4):
                        for c in range(2):
                            P.op("pe", lambda e, a=a, j=j, c=c: e.matmul(pb[a][:, j * 128:(j + 1) * 128], ckvT[:, c, j * 128:(j + 1) * 128], wV_t[:, c, :],
                                                                        start=(c == 0), stop=(c == 1)), reads=["ckvT", "wV"], writes=[akey])
                    for h in range(2):
                        P.op("act", lambda e, a=a, h=h, tt=tt: e.activation(VA[:, h, tt * 4:(tt + 1) * 4, 0:64],
                                                                           pb[a][:, :].rearrange("p (j h d) -> p j h d", j=4, h=2)[:, :, h, :], AF.Copy),
                             reads=[akey, "VA"], writes=[("VA", h, tt)])

                P.barrier()
                P.op("dve", lambda e: e.tensor_tensor(out=negm[:, 0:2], in0=qkmax[:, 0:2], in1=qkmax[:, 2:4], op=ALU.mult), writes=["negm"])
                P.op("act", lambda e: e.activation(negm[:, 2:4], negm[:, 0:2], AF.Sqrt), reads=["negm"], writes=["negm2"])
                P.op("dve", lambda e: e.tensor_scalar(out=negm[:, 0:2], in0=negm[:, 2:4], scalar1=-SCALE * 1.02, scalar2=None, op0=ALU.mult),
                     reads=["negm2"], writes=["negm3"])

                s_ring = Ring("S", 5)
                o_banks = [O0]
                nqt = int(os.environ.get("KNQT", 16))
                for h in range(2):
                    for qt in range(nqt):
                        nkb = 4 * (qt + 1)
                        okey = ("O", 0)
                        for kb in range(nkb):
                            sbk, skey = s_ring.next()
                            diag = kb >= 4 * qt
                            P.op("pe", lambda e, sbk=sbk, h=h, kb=kb, qt=qt, diag=diag: e.matmul(pb[sbk][:, :], KT[:, h, kb * 128:(kb + 1) * 128], QT[:, h, qt * 512:(qt + 1) * 512],
                                                                                               start=True, stop=(not diag)), writes=[skey])
                            if diag:
                                P.op("pe", lambda e, sbk=sbk, j=kb - 4 * qt: e.matmul(pb[sbk][:, :], ident_bf[:], amask[:, j, :], start=False, stop=True),
                                     reads=["amask", "ident_bf"], writes=[skey])
                            ps_, pkey = pt_ring.next()
                            P.op("act", lambda e, sbk=sbk, ps_=ps_, h=h: e.activation(PT[:, ps_, :], pb[sbk][:, :], AF.Exp, bias=negm[:, h:h + 1], scale=SCALE),
                                 reads=[skey, "negm3"], writes=[pkey])
                            P.op("pe", lambda e, ps_=ps_, h=h, kb=kb, nkb=nkb: e.matmul(pb[O0][0:65, :], VA[:, h, kb, :], PT[:, ps_, :], start=(kb == 0), stop=(kb == nkb - 1)),
                                 reads=[pkey], writes=[okey])
                        P.op("act", lambda e: e.activation(osb[:, :], pb[O0][0:65, :], AF.Copy), reads=[okey], writes=["osb"])
                        for j in range(4):
                            P.op("pe", lambda e, j=j: e.transpose(pb[TR][:, j * 65:(j + 1) * 65], osb[0:65, j * 128:(j + 1) * 128], ident_f[0:65, 0:65]),
                                 reads=["osb", "ident_f"], writes=["TR"])
                        P.op("dve", lambda e: e.reciprocal(rd[:, :], pb[TR][:, 0:260].rearrange("p (j c) -> p j c", j=4)[:, :, 64]), reads=["TR"], writes=["rd"])
                        as_, akey = atok_ring.next()
                        for j in range(4):
                            P.op("dve", lambda e, j=j, as_=as_: e.tensor_scalar(out=atok[:, as_, j, :], in0=pb[TR][:, j * 65:j * 65 + 64], scalar1=rd[:, j:j + 1], scalar2=None, op0=ALU.mult),
                                 reads=["TR", "rd"], writes=[akey])
                        P.dma("sp", lambda e, as_=as_, qt=qt, h=h: e.dma_start(
                            out=ag2_in[qt % 4].ap().rearrange("p (t c) -> (p t) c", c=256)[(qt // 4) * 512:(qt // 4 + 1) * 512, h * 64:(h + 1) * 64].rearrange("(j p) d -> p j d", p=128),
                            in_=atok[:, as_, :, :]),
                            ("atok_st", as_), reads=[akey], writes=["ag2_in"])

        if "B" in stages:
            phase_B()

        def phase_M():
            ph, pb = new_phase(8)
            with ph:
                stg = sb("stgm", [128, 2, 1408], F32)
                stg_ring = Ring("stg", 2)
                wM_t = sb("wM_t", [128, KC, 576], BF16)
                gml_bc = sb("gml_bc", [128, 128], F32)
                trimask = sb("trimask", [128, 128], F32)
                gb = sb("gb", [1, 4], F32)
                onesrow = sb("onesrow", [1, 128], F32)
                u_t = sb("um_t", [128, 2, KC, 512], BF16)
                u_ring = Ring("um", 2)
                QmT = sb("QmT", [128, 512], BF16)
                KmT = sb("KmT", [128, 512], BF16)
                rw = sb("rw", [1, 14, 512], F32)
                R_IG, R_LF, R_B, R_GI, R_CM, R_M, R_NM, R_W, R_T, R_EMR, R_WS, R_Z, R_E = range(13)
                mst = sb("mst", [1, 4], F32)
                mst2 = sb("mst2", [1, 2], F32)
                MSTOP = int(os.environ.get("KMSTOP", 9))
                cols_sb = sb("cols_sb", [128, 2, 8], F32)
                cols_ring = Ring("colsb", 2)
                ET = sb("ET", [128, 128], F32)
                AT = sb("AT", [128, 128], BF16)
                kw = sb("kw", [128, 128], BF16)
                vaug = sb("vaug", [128, 2, 130], BF16)
                v_ring = Ring("vaug", 2)
                sgo = sb("sgo", [128, 2, 128], F32)
                Cst = sb("Cst", [128, 129], F32)
                Cb = sb("Cb", [128, 129], BF16)
                tmpn = sb("tmpn", [128, 129], F32)
                num = sb("num", [128, 129], F32)
                dn = sb("dn", [128, 4], F32)
                hm = sb("hm", [128, 128], F32)
                junkm = sb("junkm", [128, 128], F32)
                hout = sb("hout", [128, 2, 128], BF16)
                ho_ring = Ring("hout", 2)
                BQ, BK, BI, BF_, BT, BC, BS, BU = range(8)

                for k in range(KC):
                    load_rows(stg, stg_ring, wB_d, k * 128, 704, 576, wM_t[:, k, :], "wM")
                P.dma("sp", lambda e: [e.dma_start(out=gml_bc[:], in_=gml_bc_d), e.dma_start(out=trimask[:], in_=trimask_d),
                                       e.dma_start(out=gb[:], in_=gb_d)], "c0", writes=["mconst"], n=3)
                P.op("dve", lambda e: e.memset(onesrow[:], 1.0), writes=["onesrow"])
                P.op("dve", lambda e: e.memset(rw[:, R_Z, :], 0.0), writes=["rwz"])
                P.op("dve", lambda e: e.memset(mst[:], 0.0), writes=["mst"])
                P.op("dve", lambda e: e.memset(Cst[:], 0.0), writes=["Cst"])
                P.op("pool", lambda e: e.memset(Cb[:], 0.0), writes=["Cb"])
                P.op("pool", lambda e: e.memset(vaug[:], 1.0), writes=["vaug_init"])

                def m_chunk(tt, c, us, ukey):
                    cc = tt * 4 + c
                    cs_ = slice(c * 128, (c + 1) * 128)
                    mp, mn = cc % 2, (cc + 1) % 2
                    row = lambda r: rw[0:1, r, cs_]
                    P.op("dve", lambda e: e.tensor_tensor_scan(row(R_B), onesrow[0:1, :], row(R_LF), 0.0, ALU.mult, ALU.add), reads=["lf", "onesrow"], writes=["b"])
                    P.op("dve", lambda e: e.tensor_tensor(out=row(R_GI), in0=row(R_IG), in1=row(R_B), op=ALU.subtract), reads=["ig", "b"], writes=["gi"])
                    P.op("dve", lambda e: e.tensor_tensor_scan(row(R_CM), row(R_GI), row(R_Z), -1e30, ALU.max, ALU.add), reads=["gi", "rwz"], writes=["cm"])
                    P.op("dve", lambda e: e.tensor_scalar(out=row(R_M), in0=row(R_CM), scalar1=mst[0:1, mp:mp + 1], scalar2=None, op0=ALU.max), reads=["cm", "mst"], writes=["M"])
                    P.op("dve", lambda e: e.tensor_tensor(out=mst[0:1, mn:mn + 1], in0=rw[0:1, R_M, c * 128 + 127:c * 128 + 128], in1=rw[0:1, R_B, c * 128 + 127:c * 128 + 128], op=ALU.add),
                         reads=["M", "b", "mst"], writes=["mstn"])
                    P.op("dve", lambda e: e.tensor_scalar(out=row(R_NM), in0=row(R_M), scalar1=-1.0, scalar2=None, op0=ALU.mult), reads=["M"], writes=["nM"])
                    P.op("act", lambda e: e.activation(row(R_W), row(R_M), AF.Exp, bias=mst[0:1, mp:mp + 1], scale=-1.0), reads=["M", "mst"], writes=["w"])
                    P.op("dve", lambda e: e.tensor_tensor(out=row(R_T), in0=row(R_B), in1=row(R_M), op=ALU.add), reads=["b", "M"], writes=["t"])
                    P.op("act", lambda e: e.activation(row(R_EMR), row(R_T), AF.Exp, scale=-1.0), reads=["t"], writes=["emr"])
                    P.op("dve", lambda e: e.tensor_tensor(out=mst[0:1, 2:3], in0=rw[0:1, R_B, c * 128 + 127:c * 128 + 128], in1=mst[0:1, mn:mn + 1], op=ALU.subtract),
                         reads=["b", "mstn"], writes=["dlt"])
                    P.op("act", lambda e: e.activation(row(R_WS), row(R_GI), AF.Exp, bias=mst[0:1, 2:3]), reads=["gi", "dlt"], writes=["ws"])
                    P.op("act", lambda e: e.activation(mst[0:1, 3:4], mst[0:1, mp:mp + 1], AF.Exp, bias=mst[0:1, 2:3]), reads=["mst", "dlt"], writes=["a11"])
                    if MSTOP < 2:
                        return
                    for ci, (r, key) in enumerate([(R_GI, "gi"), (R_W, "w"), (R_EMR, "emr"), (R_WS, "ws")]):
                        P.op("pe", lambda e, ci=ci, r=r: e.matmul(pb[BC][:, 2 * ci:2 * ci + 2], row(r), onesrow[0:1, 0:2], start=True, stop=True), reads=[key, "onesrow"], writes=["BC"])
                    P.op("dve", lambda e: e.tensor_copy(mst2[0:1, 0:1], mst[0:1, 3:4]), reads=["a11"], writes=["a11b"])
                    P.op("dve", lambda e: e.tensor_copy(mst2[0:1, 1:2], mst[0:1, 3:4]), reads=["a11"], writes=["a11b"])
                    P.op("pe", lambda e: e.matmul(pb[BC][:, 8:10], onesrow[0:1, :], mst2[0:1, 0:2], start=True, stop=True), reads=["a11b", "onesrow"], writes=["BC"])
                    P.op("pe", lambda e: e.matmul(pb[BC][:, 128:256], onesrow[0:1, :], row(R_NM), start=True, stop=False), reads=["nM", "onesrow"], writes=["BC"])
                    P.op("pe", lambda e: e.matmul(pb[BC][:, 128:256], ident_f[:], trimask[:], start=False, stop=True), reads=["mconst", "ident_f"], writes=["BC"])
                    cb_, cbkey = cols_ring.next()
                    P.op("dve", lambda e: e.tensor_copy(cols_sb[:, cb_, 0:5], pb[BC][:, 0:10].rearrange("p (c two) -> p c two", two=2)[:, :, 0]), reads=["BC"], writes=[cbkey])
                    P.op("act", lambda e: e.activation(ET[:], pb[BC][:, 128:256], AF.Exp, bias=cols_sb[:, cb_, 0:1]), reads=["BC", cbkey], writes=["ET"])
                    if MSTOP < 3:
                        return
                    for k in range(KC):
                        P.op("pe", lambda e, k=k: e.matmul(pb[BT][:, 0:384], u_t[:, us, k, cs_], wM_t[:, k, 128:512], start=(k == 0), stop=(k == KC - 1)),
                             reads=[ukey, "wM"], writes=["BT"])
                    P.op("dve", lambda e: e.tensor_scalar(out=cols_sb[:, cb_, 5:6], in0=cols_sb[:, cb_, 3:4], scalar1=128.0 ** -0.5, scalar2=None, op0=ALU.mult), reads=[cbkey], writes=[cbkey])
                    P.op("dve", lambda e: e.tensor_scalar(out=kw[:], in0=pb[BT][:, 0:128], scalar1=cols_sb[:, cb_, 5:6], scalar2=None, op0=ALU.mult),
                         reads=["BT", cbkey], writes=["kw"])
                    vs, vkey = v_ring.next()
                    if MSTOP == 3 and os.environ.get("KMSUB") == "a":
                        return
                    P.op("act", lambda e: e.activation(vaug[:, vs, 0:128], pb[BT][:, 128:256], AF.Copy), reads=["BT", "vaug_init", "kw"], writes=[vkey])
                    if MSTOP == 3 and os.environ.get("KMSUB") == "b":
                        return
                    P.op("act", lambda e: e.activation(sgo[:, 0, :], pb[BT][:, 256:384], AF.Exp, scale=-1.0), reads=["BT"], writes=["sgo0"])
                    P.op("dve", lambda e: e.tensor_scalar(out=sgo[:, 0, :], in0=sgo[:, 0, :], scalar1=1.0, scalar2=None, op0=ALU.add), reads=["sgo0"], writes=["sgo0"])
                    P.op("dve", lambda e: e.reciprocal(sgo[:, 1, :], sgo[:, 0, :]), reads=["sgo0"], writes=["sgo1"])
                    if MSTOP < 4:
                        return
                    P.op("pe", lambda e: e.matmul(pb[BS][:, 0:128], KmT[:, cs_], QmT[:, cs_], start=True, stop=True), reads=["KmT", "QmT"], writes=["BS"])
                    P.op("dve", lambda e: e.tensor_tensor(out=AT[:], in0=ET[:], in1=pb[BS][:, 0:128], op=ALU.mult), reads=["ET", "BS"], writes=["AT"])
                    P.op("pe", lambda e: e.matmul(pb[BS][:, 128:257], AT[:], vaug[:, vs, 0:129], start=True, stop=True), reads=["AT", vkey], writes=["BS"])
                    P.op("pe", lambda e: e.matmul(pb[BU][:, 0:129], QmT[:, cs_], Cb[:], start=True, stop=True), reads=["QmT", "Cb"], writes=["BU"])
                    P.op("dve", lambda e: e.tensor_scalar(out=tmpn[:], in0=pb[BU][:, 0:129], scalar1=cols_sb[:, cb_, 1:2], scalar2=None, op0=ALU.mult), reads=["BU", cbkey], writes=["tmpn"])
                    P.op("dve", lambda e: e.tensor_tensor(out=num[:], in0=tmpn[:], in1=pb[BS][:, 128:257], op=ALU.add), reads=["tmpn", "BS"], writes=["num"])
                    P.op("dve", lambda e: e.tensor_scalar(out=dn[:, 1:2], in0=num[:, 128:129], scalar1=-1.0, scalar2=None, op0=ALU.mult), reads=["num"], writes=["dn1"])
                    P.op("dve", lambda e: e.tensor_tensor(out=dn[:, 0:1], in0=num[:, 128:129], in1=dn[:, 1:2], op=ALU.max), reads=["num", "dn1"], writes=["dn0"])
                    P.op("dve", lambda e: e.tensor_scalar(out=dn[:, 0:1], in0=dn[:, 0:1], scalar1=cols_sb[:, cb_, 2:3], scalar2=None, op0=ALU.max), reads=["dn0", cbkey], writes=["dn0"])
                    P.op("dve", lambda e: e.reciprocal(dn[:, 1:2], dn[:, 0:1]), reads=["dn0"], writes=["dn1"])
                    P.op("dve", lambda e: e.scalar_tensor_tensor(out=hm[:], in0=num[:, 0:128], scalar=dn[:, 1:2], in1=sgo[:, 1, :], op0=ALU.mult, op1=ALU.mult),
                         reads=["num", "dn1", "sgo1"], writes=["hm"])
                    P.op("act", lambda e: e.activation(junkm[:], hm[:], AF.Square), reads=["hm"], writes=["junkm"])
                    P.op("dve", lambda e: e.reduce_sum(out=dn[:, 2:3], in_=junkm[:], axis=AX.X), reads=["junkm"], writes=["dn2"])
                    P.op("act", lambda e: e.activation(dn[:, 3:4], dn[:, 2:3], AF.Sqrt, bias=EPS, scale=1.0 / 128), reads=["dn2"], writes=["dn3"])
                    P.op("dve", lambda e: e.reciprocal(dn[:, 2:3], dn[:, 3:4]), reads=["dn3"], writes=["dn4"])
                    hs_, hkey_ = ho_ring.next()
                    P.op("dve", lambda e: e.scalar_tensor_tensor(out=hout[:, hs_, :], in0=hm[:], scalar=dn[:, 2:3], in1=gml_bc[:], op0=ALU.mult, op1=ALU.mult),
                         reads=["hm", "dn4", "mconst"], writes=[hkey_])
                    qt, j = cc // 4, cc % 4
                    r0 = (qt // 4) * 512 + j * 128
                    P.dma("sp", lambda e: e.dma_start(out=ag2_in[qt % 4].ap().rearrange("p (t c) -> (p t) c", c=256)[r0:r0 + 128, 128:256], in_=hout[:, hs_, :]),
                          ("hout_st", hs_), reads=[hkey_], writes=["ag2_in"])
                    P.op("pe", lambda e: e.matmul(pb[BU][:, 256:385], kw[:], vaug[:, vs, 0:129], start=True, stop=True), reads=["kw", vkey], writes=["BU"])
                    P.op("dve", lambda e: e.scalar_tensor_tensor(out=Cst[:], in0=Cst[:], scalar=cols_sb[:, cb_, 4:5], in1=pb[BU][:, 256:385], op0=ALU.mult, op1=ALU.add),
                         reads=["BU", cbkey, "Cst"], writes=["Cst"])
                    P.op("act", lambda e: e.activation(Cb[:], Cst[:], AF.Copy), reads=["Cst"], writes=["Cb"])

                def m_tile(tt, us, ukey):
                    rank, c0 = tt // 4, (tt % 4) * 512
                    P.dma("sp", lambda e: [e.dma_start(
                        out=u_t[:, us, 2 * i:2 * i + 2, :], in_=ag1_out[i].ap()[rank * 128:(rank + 1) * 128, :].rearrange("p (k n) -> p k n", k=2)[:, :, c0:c0 + 512])
                        for i in range(4)], ("um_ld", us), reads=[("ag1_out", i) for i in range(4)], writes=[ukey], n=4)
                    for (bank, c0w, dst, key, scl) in [(BQ, 0, QmT, "QmT", 1.0), (BK, 128, KmT, "KmT", 128.0 ** -0.5)]:
                        for k in range(KC):
                            P.op("pe", lambda e, k=k, bank=bank, c0w=c0w: e.matmul(pb[bank][:, :], wM_t[:, k, c0w:c0w + 128], u_t[:, us, k, :], start=(k == 0), stop=(k == KC - 1)),
                                 reads=[ukey, "wM"], writes=[("bank", bank)])
                        P.op("act", lambda e, bank=bank, dst=dst, scl=scl: e.activation(dst[:], pb[bank][:, :], AF.Copy, scale=scl), reads=[("bank", bank)], writes=[key])
                    for (bank, col) in [(BI, 512), (BF_, 513)]:
                        for k in range(KC):
                            P.op("pe", lambda e, k=k, bank=bank, col=col: e.matmul(pb[bank][0:1, :], wM_t[:, k, col:col + 1], u_t[:, us, k, :], start=(k == 0), stop=(k == KC - 1)),
                                 reads=[ukey, "wM"], writes=[("bank", bank)])
                    P.op("dve", lambda e: e.tensor_scalar(out=rw[0:1, R_IG, :], in0=pb[BI][0:1, :], scalar1=gb[0:1, 0:1], scalar2=None, op0=ALU.add), reads=[("bank", BI), "mconst"], writes=["ig"])
                    P.op("act", lambda e: e.activation(rw[0:1, R_E, :], pb[BF_][0:1, :], AF.Exp, bias=gb[0:1, 2:3], scale=-1.0), reads=[("bank", BF_), "mconst"], writes=["rwe"])
                    P.op("act", lambda e: e.activation(rw[0:1, R_E, :], rw[0:1, R_E, :], AF.Ln, bias=1.0), reads=["rwe"], writes=["rwe"])
                    P.op("dve", lambda e: e.tensor_scalar(out=rw[0:1, R_LF, :], in0=rw[0:1, R_E, :], scalar1=-1.0, scalar2=None, op0=ALU.mult), reads=["rwe"], writes=["lf"])
                    for c in range(4):
                        m_chunk(tt, c, us, ukey)

                nmt = int(os.environ.get("KNMT", 16))
                for tt in range(nmt):
                    us, ukey = u_ring.next()
                    m_tile(tt, us, ukey)
                mfin = (nmt * 4) % 2
                P.dma("sp", lambda e: [e.dma_start(out=C_p_out, in_=Cst[:, 0:128]), e.dma_start(out=n_p_out, in_=Cst[:, 128:129]),
                                       e.dma_start(out=m_p_out, in_=mst[0:1, mfin:mfin + 1])], "mout", reads=["Cst", "mstn"], n=3)

        if "M" in stages:
            phase_M()

        def phase_S1():
            ph, pb = new_phase(6)
            with ph:
                stg = sb("stgs", [128, 2, 1408], F32)
                stg_ring = Ring("stg", 2)
                wIn_t = sb("wIn_t", [128, KC, IN_COLS], BF16)
                us_t = sb("us_t", [128, KC, TS], BF16)
                z_sb = sb("z_sb", [TS, IN_COLS], F32)
                Cpm = sb("Cpm", [64, 128, 128], F32)
                pmq = sb("pmq", [64, 4, 128], F32)
                pmg = sb("pmg", [64, 16], F32)
                npm = sb("npm", [64, 128], F32)
                gbs = sb("gbs", [64, 4], F32)
                gml_pm = sb("gml_pm", [64, 128], F32)
                ks = sb("ks", [64, 128], F32)
                kwv = sb("kwv", [64, 128], F32)
                qCacc = sb("qCacc", [64, 128], F32)
                tv = sb("tv", [64, 2, 128], F32)
                jk = sb("jk", [64, 128], F32)
                hs_ = sb("hs_", [64, 128], F32)
                hob = sb("hob", [64, 128], BF16)
                for half in range(2):
                    c0 = half * 1364
                    for k in range(KC):
                        load_rows(stg, stg_ring, w_in_d, k * 128, c0, 1364, wIn_t[:, k, c0:c0 + 1364], "wIn")
                P.dma("sp", lambda e: [e.dma_start(out=us_t[:], in_=us_d.ap().rearrange("p (k n) -> p k n", k=KC)),
                                       e.dma_start(out=Cpm[:], in_=stC_d.rearrange("p (d e) -> p d e", d=128)),
                                       e.dma_start(out=npm[:], in_=stn_d), e.dma_start(out=pmg[:, 0:1], in_=stm_d),
                                       e.dma_start(out=gbs[:], in_=gbs_d), e.dma_start(out=gml_pm[:], in_=gml_pm_d)],
                      "s1ld", reads=["us_d"], writes=["us_t", "Cpm", "pmconst"], n=6)
                col = 0
                gi_ = 0
                while col < IN_COLS:
                    w_ = min(512, IN_COLS - col)
                    a = gi_ % 4
                    for k in range(KC):
                        P.op("pe", lambda e, k=k, a=a, col=col, w_=w_: e.matmul(pb[a][0:TS, 0:w_], us_t[:, k, :], wIn_t[:, k, col:col + w_], start=(k == 0), stop=(k == KC - 1)),
                             reads=["us_t", "wIn"], writes=[("acc", a)])
                    P.op("act", lambda e, a=a, col=col, w_=w_: e.activation(z_sb[:, col:col + w_], pb[a][0:TS, 0:w_], AF.Copy), reads=[("acc", a)], writes=["z_sb"])
                    col += w_
                    gi_ += 1
                P.dma("sp", lambda e: e.dma_start(out=zs_d.ap(), in_=z_sb[:]), "zs_st", reads=["z_sb"], writes=["zs_d"])
                def pm_load(e):
                    outs = []
                    for h in range(4):
                        for qi, off in enumerate([OFF_MQ, OFF_MK, OFF_MV, OFF_MO]):
                            outs.append(e.dma_start(out=pmq[h * 16:(h + 1) * 16, qi, :], in_=zs_d.ap()[:, off + h * 128:off + (h + 1) * 128]))
                        outs.append(e.dma_start(out=pmg[h * 16:(h + 1) * 16, 1:2], in_=zs_d.ap()[:, OFF_MI + h:OFF_MI + h + 1], allow_slow_non_contiguous=True))
                        outs.append(e.dma_start(out=pmg[h * 16:(h + 1) * 16, 2:3], in_=zs_d.ap()[:, OFF_MF + h:OFF_MF + h + 1], allow_slow_non_contiguous=True))
                    return outs
                P.dma("sp", pm_load, "pmld", reads=["zs_d"], writes=["pm"], n=24)
                G = lambda c: pmg[:, c:c + 1]
                P.op("dve", lambda e: e.tensor_tensor(out=G(3), in0=G(1), in1=gbs[:, 0:1], op=ALU.add), reads=["pm", "pmconst"], writes=["g3"])
                P.op("act", lambda e: e.activation(G(15), G(2), AF.Exp, bias=gbs[:, 2:3], scale=-1.0), reads=["pm", "pmconst"], writes=["g15"])
                P.op("act", lambda e: e.activation(G(15), G(15), AF.Ln, bias=1.0), reads=["g15"], writes=["g15"])
                P.op("dve", lambda e: e.tensor_scalar(out=G(4), in0=G(15), scalar1=-1.0, scalar2=None, op0=ALU.mult), reads=["g15"], writes=["g4"])
                P.op("dve", lambda e: e.tensor_tensor(out=G(5), in0=G(4), in1=G(0), op=ALU.add), reads=["g4", "pmconst"], writes=["g5"])
                P.op("dve", lambda e: e.tensor_tensor(out=G(6), in0=G(5), in1=G(3), op=ALU.max), reads=["g5", "g3"], writes=["g6"])
                P.op("dve", lambda e: e.tensor_tensor(out=G(15), in0=G(5), in1=G(6), op=ALU.subtract), reads=["g5", "g6", "g15"], writes=["g15b"])
                P.op("act", lambda e: e.activation(G(7), G(15), AF.Exp), reads=["g15b"], writes=["g7"])
                P.op("dve", lambda e: e.tensor_tensor(out=G(15), in0=G(3), in1=G(6), op=ALU.subtract), reads=["g3", "g6", "g7"], writes=["g15c"])
                P.op("act", lambda e: e.activation(G(8), G(15), AF.Exp), reads=["g15c"], writes=["g8"])
                P.op("act", lambda e: e.activation(G(9), G(6), AF.Exp, scale=-1.0), reads=["g6"], writes=["g9"])
                P.op("dve", lambda e: e.tensor_scalar(out=ks[:], in0=pmq[:, 1, :], scalar1=128.0 ** -0.5, scalar2=None, op0=ALU.mult), reads=["pm"], writes=["ks"])
                P.op("dve", lambda e: e.tensor_tensor(out=jk[:], in0=pmq[:, 0, :], in1=ks[:], op=ALU.mult), reads=["pm", "ks"], writes=["jk"])
                P.op("dve", lambda e: e.reduce_sum(out=G(10), in_=jk[:], axis=AX.X), reads=["jk"], writes=["g10"])
                P.op("dve", lambda e: e.tensor_tensor(out=jk[:], in0=pmq[:, 0, :], in1=npm[:], op=ALU.mult), reads=["pm", "Cpm", "g10"], writes=["jk"])
                P.op("dve", lambda e: e.reduce_sum(out=G(11), in_=jk[:], axis=AX.X), reads=["jk"], writes=["g11"])
                P.op("dve", lambda e: e.tensor_tensor(out=G(12), in0=G(10), in1=G(8), op=ALU.mult), reads=["g10", "g8"], writes=["g12"])
                P.op("dve", lambda e: e.tensor_scalar(out=kwv[:], in0=ks[:], scalar1=G(8), scalar2=None, op0=ALU.mult), reads=["ks", "g8"], writes=["kwv"])
                P.op("dve", lambda e: e.memset(qCacc[:], 0.0), writes=["qCacc"])
                for d in range(128):
                    P.op("dve", lambda e, d=d: e.scalar_tensor_tensor(out=qCacc[:], in0=Cpm[:, d, :], scalar=pmq[:, 0, d:d + 1], in1=qCacc[:], op0=ALU.mult, op1=ALU.add),
                         reads=["Cpm", "pm", "qCacc"], writes=["qCacc"])
                    P.op("dve", lambda e, d=d: e.tensor_scalar(out=tv[:, d % 2, :], in0=pmq[:, 2, :], scalar1=kwv[:, d:d + 1], scalar2=None, op0=ALU.mult),
                         reads=["pm", "kwv"], writes=[("tv", d % 2)])
                    P.op("dve", lambda e, d=d: e.scalar_tensor_tensor(out=Cpm[:, d, :], in0=Cpm[:, d, :], scalar=G(7), in1=tv[:, d % 2, :], op0=ALU.mult, op1=ALU.add),
                         reads=[("tv", d % 2), "g7", "qCacc"], writes=["Cpm"])
                P.dma("sp", lambda e: e.dma_start(out=C_s_out.rearrange("p (d e) -> p d e", d=128), in_=Cpm[:]), "cs_st", reads=["Cpm"])
                P.op("dve", lambda e: e.scalar_tensor_tensor(out=jk[:], in0=npm[:], scalar=G(7), in1=kwv[:], op0=ALU.mult, op1=ALU.add), reads=["g7", "kwv", "g11"], writes=["jk2"])
                P.dma("sp", lambda e: [e.dma_start(out=n_s_out, in_=jk[:]), e.dma_start(out=m_s_out, in_=pmg[:, 6:7])], "ns_st", reads=["jk2", "g6"], n=2)
                P.op("dve", lambda e: e.tensor_scalar(out=hs_[:], in0=qCacc[:], scalar1=G(7), scalar2=None, op0=ALU.mult), reads=["qCacc", "g7"], writes=["hs"])
                P.op("dve", lambda e: e.scalar_tensor_tensor(out=hs_[:], in0=pmq[:, 2, :], scalar=G(12), in1=hs_[:], op0=ALU.mult, op1=ALU.add), reads=["hs", "g12", "pm"], writes=["hs"])
                P.op("dve", lambda e: e.scalar_tensor_tensor(out=G(13), in0=G(11), scalar=G(7), in1=G(12), op0=ALU.mult, op1=ALU.add), reads=["g11", "g7", "g12"], writes=["g13"])
                P.op("dve", lambda e: e.tensor_scalar(out=G(15), in0=G(13), scalar1=-1.0, scalar2=None, op0=ALU.mult), reads=["g13", "g8"], writes=["g15d"])
                P.op("dve", lambda e: e.tensor_tensor(out=G(13), in0=G(13), in1=G(15), op=ALU.max), reads=["g15d"], writes=["g13b"])
                P.op("dve", lambda e: e.tensor_tensor(out=G(13), in0=G(13), in1=G(9), op=ALU.max), reads=["g13b", "g9"], writes=["g13c"])
                P.op("dve", lambda e: e.reciprocal(G(14), G(13)), reads=["g13c"], writes=["g14"])
                P.op("act", lambda e: e.activation(jk[:], pmq[:, 3, :], AF.Exp, scale=-1.0), reads=["pm", "jk2"], writes=["jk2", "jk3"])
                P.op("dve", lambda e: e.tensor_scalar(out=jk[:], in0=jk[:], scalar1=1.0, scalar2=None, op0=ALU.add), reads=["jk3"], writes=["jk3"])
                P.op("dve", lambda e: e.reciprocal(jk[:], jk[:]), reads=["jk3"], writes=["jk3"])
                P.op("dve", lambda e: e.scalar_tensor_tensor(out=hs_[:], in0=hs_[:], scalar=G(14), in1=jk[:], op0=ALU.mult, op1=ALU.mult), reads=["hs", "g14", "jk3"], writes=["hs2"])
                P.op("act", lambda e: e.activation(jk[:], hs_[:], AF.Square), reads=["hs2"], writes=["jk4"])
                P.op("dve", lambda e: e.reduce_sum(out=G(15), in_=jk[:], axis=AX.X), reads=["jk4"], writes=["g15e"])
                P.op("act", lambda e: e.activation(G(13), G(15), AF.Sqrt, bias=EPS, scale=1.0 / 128), reads=["g15e"], writes=["g13d"])
                P.op("dve", lambda e: e.reciprocal(G(14), G(13)), reads=["g13d"], writes=["g14b"])
                P.op("dve", lambda e: e.scalar_tensor_tensor(out=hob[:], in0=hs_[:], scalar=G(14), in1=gml_pm[:], op0=ALU.mult, op1=ALU.mult), reads=["hs2", "g14b", "pmconst"], writes=["hob"])
                P.dma("sp", lambda e: [e.dma_start(out=mixs_d.ap().rearrange("p (r c) -> p r c", r=4)[:, h, 128:256], in_=hob[h * 16:(h + 1) * 16, :]) for h in range(4)],
                      "hob_st", reads=["hob"], writes=["mixs_hm"], n=4)

        if "S" in stages:
            phase_S1()

        def phase_S2():
            ph, pb = new_phase(6)
            with ph:
                pT_b = ph.enter_context(nc.psum_tensor("pT_b", [128, 1024], BF16))
                pTr_b = ph.enter_context(nc.psum_tensor("pTr_b", [128, 1024], BF16))
                C0, D0, E_, F_, X0, X1 = range(6)
                stg = sb("stg2", [128, 2, 1408], F32)
                stg_ring = Ring("stg", 2)
                wuq_t = sb("wuq_t", [128, 3, 768], BF16)
                wukT_t = sb("wukT_t", [128, 8, 256], BF16)
                wuv_t = sb("wuv_t", [128, 2, 512], BF16)
                zt = sb("zt", [TS, 672], F32)
                gq_bc = sb("gq_bc", [TS, 384], F32)
                gkv16 = sb("gkv16", [TS, 256], F32)
                c8 = sb("c8", [TS, 2, 8, 32], F32)
                iota_p = sb("iota_p", [128, 1], F32)
                pt_i = sb("pt_i", [128, 1024], I32)
                pt_f = sb("pt_f", [128, 1024], F32)
                idx_i = sb("idx_i", [128, 1024], I32)
                jk = sb("jk2", [TS, 768], F32)
                sts = sb("sts", [128, 8], F32)
                qlat_b = sb("qlat_b", [TS, 384], BF16)
                qlatT = sb("qlatT", [128, 3, TS], BF16)
                q_sb = sb("q_sb", [TS, 8, 96], F32)
                qsw = sb("qsw", [TS, 8, 32], F32)
                qn_b = sb("qn_b", [TS, 8, 64], BF16)
                qr_b = sb("qr_b", [TS, 8, 32], BF16)
                qnT = sb("qnT", [128, 4, TS], BF16)
                QAT = sb("QAT", [128, 2, TS, 8], BF16)
                QRT = sb("QRT", [32, TS, 8], BF16)
                kvs = sb("kvs", [TS, 288], F32)
                ksw = sb("ksw", [TS, 32], F32)
                Gnew = sb("Gnew", [128, TS, 288], F32)
                ones_f = sb("ones_f", [128, 128], F32)
                negrow = sb("negrow", [1, 128], F32)
                G = sb("Gst", [128, 2, 4, 288], F32)
                g_ring = Ring("G", 2)
                Gb = sb("Gb", [128, 2, 65, 288], BF16)
                KT = sb("KTs", [128, 2, 512], BF16)
                KTr = sb("KTrs", [32, 512], BF16)
                PT = sb("PTs", [128, 65, 8], BF16)
                mx1 = sb("mx1", [128, 8], F32)
                mxh = sb("mxh", [8, 2], F32)
                diag = sb("diag", [8, 8], F32)
                negmx = sb("negmx", [128, 8], F32)
                t8 = sb("t8", [128, 8], F32)
                pr = sb("pr", [128, 8], F32)
                rden = sb("rden", [128, 8], F32)
                OLT = sb("OLT", [128, 2, 8, TS], BF16)
                amix = sb("amix", [TS, 4, 128], BF16)

                for c in range(3):
                    load_rows(stg, stg_ring, w_uq_d, c * 128, 0, 768, wuq_t[:, c, :], "wuq")
                for i in range(8):
                    load_rows(stg, stg_ring, wukT_d, 0, i * 256, 256, wukT_t[:, i, :], "wukT")
                for c in range(2):
                    load_rows(stg, stg_ring, w_uv_d, c * 128, 0, 512, wuv_t[:, c, :], "wuv")
                P.dma("sp", lambda e: [e.dma_start(out=zt[:], in_=zs_d.ap()[:, 0:672]), e.dma_start(out=gq_bc[:], in_=gq_bc_d), e.dma_start(out=gkv16[:], in_=gkv_bc_d[0:TS, :]),
                                       e.dma_start(out=c8[:], in_=c8_d), e.dma_start(out=iota_p[:], in_=iota_d), e.dma_start(out=negrow[:], in_=negrow_d),
                                       e.dma_start(out=pt_i[:], in_=pt_d.partition_broadcast(128))],
                      "s2ld", reads=["zs_d"], writes=["s2c"], n=7)
                P.op("dve", lambda e: e.memset(ones_f[:], 1.0), writes=["ones_f"])
                P.op("pool", lambda e: e.memset(Gnew[:], 0.0), writes=["Gnew"])
                P.op("pool", lambda e: e.memset(OLT[:], 0.0), writes=["OLT"])
                P.op("dve", lambda e: e.tensor_copy(pt_f[:], pt_i[:]), reads=["s2c"], writes=["pt_f"])
                P.op("dve", lambda e: e.tensor_scalar(out=pt_f[:], in0=pt_f[:], scalar1=128.0, scalar2=iota_p[:, 0:1], op0=ALU.mult, op1=ALU.add), reads=["pt_f", "s2c"], writes=["pt_f2"])
                P.op("dve", lambda e: e.tensor_copy(idx_i[:], pt_f[:]), reads=["pt_f2"], writes=["idx_i"])

                S2STOP = int(os.environ.get("KS2STOP", 9))

                def rms_tok(src_ap, dim, gain_ap, dst_ap, tag):
                    P.op("act", lambda e: e.activation(jk[:, 0:dim], src_ap, AF.Square), reads=["s2c"], writes=["jk"])
                    P.op("dve", lambda e: e.reduce_sum(out=sts[0:TS, 0:1], in_=jk[:, 0:dim], axis=AX.X), reads=["jk"], writes=["sts0"])
                    P.op("act", lambda e: e.activation(sts[0:TS, 1:2], sts[0:TS, 0:1], AF.Sqrt, bias=EPS, scale=1.0 / dim), reads=["sts0"], writes=["sts1"])
                    P.op("dve", lambda e: e.reciprocal(sts[0:TS, 2:3], sts[0:TS, 1:2]), reads=["sts1"], writes=["sts2"])
                    P.op("dve", lambda e: e.scalar_tensor_tensor(out=dst_ap, in0=src_ap, scalar=sts[0:TS, 2:3], in1=gain_ap, op0=ALU.mult, op1=ALU.mult), reads=["sts2", "s2c"], writes=[tag])
                rms_tok(zt[:, 384:640], 256, gkv16[:], kvs[:, 0:256], "kvs")
                P.op("dve", lambda e: e.tensor_copy(ksw[:, 0:16], zt[:, 656:672]), reads=["s2c"], writes=["ksw"])
                P.op("dve", lambda e: e.tensor_copy(ksw[:, 16:32], zt[:, 640:656]), reads=["s2c"], writes=["ksw"])
                P.op("dve", lambda e: e.tensor_tensor(out=ksw[:], in0=ksw[:], in1=c8[:, 1, 0, :], op=ALU.mult), reads=["ksw", "s2c"], writes=["ksw"])
                P.op("dve", lambda e: e.tensor_tensor(out=kvs[:, 256:288], in0=zt[:, 640:672], in1=c8[:, 0, 0, :], op=ALU.mult), reads=["s2c", "kvs"], writes=["kvs"])
                P.op("dve", lambda e: e.tensor_tensor(out=kvs[:, 256:288], in0=kvs[:, 256:288], in1=ksw[:], op=ALU.add), reads=["kvs", "ksw"], writes=["kvs"])
                P.dma("sp", lambda e: [e.dma_start(out=ckv_s_out, in_=kvs[:, 0:256]), e.dma_start(out=kr_s_out, in_=kvs[:, 256:288]),
                                       e.dma_start(out=Gnew[0:1, :, :], in_=kvs[:, :])], "kvs_st", reads=["kvs", "Gnew"], writes=["Gnew2"], n=3)
                if S2STOP < 2:
                    return
                rms_tok(zt[:, 0:384], 384, gq_bc[:], qlat_b[:], "qlat_b")
                for c in range(3):
                    P.op("pe", lambda e, c=c: e.transpose(pT_b[:, c * TS:(c + 1) * TS], qlat_b[:, c * 128:(c + 1) * 128], ident_bf[0:TS, 0:TS]), reads=["qlat_b", "ident_bf"], writes=["PSA"])
                P.op("act", lambda e: e.activation(qlatT[:], pT_b[:, 0:3 * TS].rearrange("p (c n) -> p c n", c=3), AF.Copy), reads=["PSA"], writes=["qlatT"])
                for hf in range(2):
                    for c in range(3):
                        P.op("pe", lambda e, c=c, hf=hf: e.matmul(pb[X0 + hf][0:TS, 0:384], qlatT[:, c, :], wuq_t[:, c, hf * 384:(hf + 1) * 384], start=(c == 0), stop=(c == 2)),
                             reads=["qlatT", "wuq"], writes=[("acc", X0 + hf)])
                    P.op("act", lambda e, hf=hf: e.activation(q_sb[:, hf * 4:(hf + 1) * 4, :], pb[X0 + hf][0:TS, 0:384].rearrange("p (h d) -> p h d", h=4), AF.Copy),
                         reads=[("acc", X0 + hf)], writes=["q_sb"])
                S2SUB = os.environ.get("KS2SUB", "z")
                if S2SUB < "b":
                    return
                P.op("dve", lambda e: e.tensor_copy(qsw[:, :, 0:16], q_sb[:, :, 80:96]), reads=["q_sb"], writes=["qsw"])
                P.op("dve", lambda e: e.tensor_copy(qsw[:, :, 16:32], q_sb[:, :, 64:80]), reads=["q_sb"], writes=["qsw"])
                P.op("dve", lambda e: e.tensor_tensor(out=qsw[:], in0=qsw[:], in1=c8[:, 1, :, :], op=ALU.mult), reads=["qsw", "s2c"], writes=["qsw"])
                P.op("dve", lambda e: e.tensor_tensor(out=q_sb[:, :, 64:96], in0=q_sb[:, :, 64:96], in1=c8[:, 0, :, :], op=ALU.mult), reads=["q_sb", "s2c", "qsw"], writes=["q_sb"])
                P.op("dve", lambda e: e.tensor_tensor(out=qr_b[:], in0=q_sb[:, :, 64:96], in1=qsw[:], op=ALU.add), reads=["q_sb", "qsw"], writes=["qr_b"])
                P.op("dve", lambda e: e.tensor_copy(qn_b[:], q_sb[:, :, 0:64]), reads=["q_sb"], writes=["qn_b"])
                if S2SUB < "c":
                    return
                for i in range(4):
                    P.op("pe", lambda e, i=i: e.transpose(pT_b[:, 64 + i * TS:64 + (i + 1) * TS], qn_b[:, 2 * i:2 * i + 2, :], ident_bf[0:TS, 0:TS]), reads=["qn_b", "ident_bf"], writes=["PSA"])
                P.op("act", lambda e: e.activation(qnT[:], pT_b[:, 64:64 + 4 * TS].rearrange("p (i n) -> p i n", i=4), AF.Copy), reads=["PSA"], writes=["qnT"])
                if S2SUB < "d":
                    return
                for h in range(8):
                    P.op("pe", lambda e, h=h: e.transpose(pTr_b[0:32, h * TS:(h + 1) * TS], qr_b[:, h, :], ident_bf[0:TS, 0:TS]), reads=["qr_b", "ident_bf"], writes=["PSB"])
                P.op("act", lambda e: e.activation(QRT[:], pTr_b[0:32, 0:8 * TS].rearrange("p (h b) -> p b h", h=8), AF.Copy), reads=["PSB"], writes=["QRT"])
                if S2SUB < "e":
                    return
                for cc in range(2):
                    for h in range(8):
                        i, e_ = h // 2, h % 2
                        P.op("pe", lambda e, cc=cc, h=h, i=i, e_=e_: e.matmul(pb[X0 + cc][:, h * TS:(h + 1) * TS], wukT_t[:, h, cc * 128:(cc + 1) * 128],
                                                                            qnT[:, i, :], start=True, stop=True),
                             reads=["qnT", "wukT"], writes=[("acc", X0 + cc)])
                    P.op("act", lambda e, cc=cc: e.activation(QAT[:, cc, :, :], pb[X0 + cc][:, 0:8 * TS].rearrange("p (h b) -> p b h", h=8), AF.Copy), reads=[("acc", X0 + cc)], writes=["QAT"])

                if S2STOP < 3:
                    return

                def do_pages(b, bs, pages, slot_fn, sc_bank, sc_col0, sckey, mask_new):
                    npg = len(pages)
                    if S2STOP < 4:
                        return
                    for cc in range(2):
                        for jj, pg in enumerate(pages):
                            P.op("pe", lambda e, cc=cc, jj=jj, pg=pg: e.transpose(pT_b[:, cc * 512 + jj * 128:cc * 512 + (jj + 1) * 128], Gb[:, bs, pg, cc * 128:(cc + 1) * 128], ident_bf[:]),
                                 reads=[("Gb", bs), "ident_bf"], writes=["PSA"])
                    for jj, pg in enumerate(pages):
                        P.op("pe", lambda e, jj=jj, pg=pg: e.transpose(pTr_b[0:32, jj * 128:(jj + 1) * 128], Gb[:, bs, pg, 256:288], ident_bf[:]), reads=[("Gb", bs), "ident_bf"], writes=["PSB"])
                    P.op("act", lambda e: e.activation(KT[:, :, 0:npg * 128], pT_b[:, :].rearrange("p (c n) -> p c n", c=2)[:, :, 0:npg * 128], AF.Copy), reads=["PSA"], writes=["KT"])
                    P.op("dve", lambda e: e.tensor_copy(KTr[:, 0:npg * 128], pTr_b[0:32, 0:npg * 128]), reads=["PSB"], writes=["KTr"])
                    for jj, pg in enumerate(pages):
                        cols = slice(sc_col0 + jj * 8, sc_col0 + jj * 8 + 8)
                        P.op("pe", lambda e, jj=jj, cols=cols: e.matmul(pb[sc_bank][:, cols], KT[:, 0, jj * 128:(jj + 1) * 128], QAT[:, 0, b, :], start=True, stop=False), reads=["KT", "QAT"], writes=[sckey])
                        P.op("pe", lambda e, jj=jj, cols=cols: e.matmul(pb[sc_bank][:, cols], KT[:, 1, jj * 128:(jj + 1) * 128], QAT[:, 1, b, :], start=False, stop=False), reads=["KT", "QAT"], writes=[sckey])
                        P.op("pe", lambda e, jj=jj, cols=cols: e.matmul(pb[sc_bank][:, cols], KTr[:, jj * 128:(jj + 1) * 128], QRT[:, b, :], start=False, stop=(not mask_new)), reads=["KTr", "QRT"], writes=[sckey])
                        if mask_new:
                            P.op("pe", lambda e, cols=cols: e.matmul(pb[sc_bank][:, cols], negrow[0:1, :], ones_f[0:1, 0:8], start=False, stop=True), reads=["s2c", "ones_f"], writes=[sckey])

                def sample(b):
                    bs = b % 2
                    for g in range(16):
                        gs, gkey = g_ring.next()
                        P.dma("pool", lambda e, gs=gs, g=g: [e.indirect_dma_start(out=G[:, gs, jj, :], out_offset=None, in_=cache_d,
                                                                                 in_offset=bass.IndirectOffsetOnAxis(ap=idx_i[:, b * 64 + g * 4 + jj:b * 64 + g * 4 + jj + 1], axis=0))
                                                            for jj in range(4)], ("G_ld", gs), reads=["idx_i"], writes=[gkey], n=4)
                        P.op("pool", lambda e, gs=gs, g=g: e.tensor_copy(Gb[:, bs, g * 4:(g + 1) * 4, :], G[:, gs, :, :]), reads=[gkey], writes=[("Gb", bs)])
                        do_pages(b, bs, [g * 4 + jj for jj in range(4)], None, C0, g * 32, "scC", False)
                    P.op("pool", lambda e: e.tensor_copy(Gb[:, bs, 64, :], Gnew[:, b, :]), reads=["Gnew2"], writes=[("Gb", bs)])
                    do_pages(b, bs, [64], None, D0, 0, "scD", True)
                    if S2STOP < 5:
                        return
                    P.op("dve", lambda e: e.tensor_reduce(out=mx1[:], in_=pb[C0][:, :].rearrange("p (j h) -> p h j", h=8), axis=AX.X, op=ALU.max), reads=["scC"], writes=["mx1"])
                    P.op("dve", lambda e: e.tensor_tensor(out=mx1[:], in0=mx1[:], in1=pb[D0][:, 0:8], op=ALU.max), reads=["mx1", "scD"], writes=["mx1"])
                    P.op("pe", lambda e: e.transpose(pb[F_][0:8, 0:128], mx1[:], ident_f[:]), reads=["mx1", "ident_f"], writes=["PSF"])
                    P.op("dve", lambda e: e.reduce_max(out=mxh[:, 0:1], in_=pb[F_][0:8, 0:128], axis=AX.X), reads=["PSF"], writes=["mxh"])
                    P.op("dve", lambda e: e.tensor_scalar(out=diag[:], in0=ident_f[0:8, 0:8], scalar1=mxh[:, 0:1], scalar2=None, op0=ALU.mult), reads=["mxh", "ident_f"], writes=["diag"])
                    P.op("pe", lambda e: e.matmul(pb[F_][:, 128:136], ones_f[0:8, :], diag[:], start=True, stop=True), reads=["diag", "ones_f"], writes=["PSF"])
                    P.op("dve", lambda e: e.tensor_scalar(out=negmx[:], in0=pb[F_][:, 128:136], scalar1=-SCALE, scalar2=None, op0=ALU.mult), reads=["PSF"], writes=["negmx"])
                    for h in range(8):
                        P.op("act", lambda e, h=h: e.activation(PT[:, 0:64, h], pb[C0][:, :].rearrange("p (j h) -> p j h", h=8)[:, :, h], AF.Exp, bias=negmx[:, h:h + 1], scale=SCALE),
                             reads=["scC", "negmx"], writes=["PT"])
                    P.op("dve", lambda e: e.tensor_scalar(out=t8[:], in0=pb[D0][:, 0:8], scalar1=SCALE, scalar2=None, op0=ALU.mult), reads=["scD"], writes=["t8"])
                    P.op("dve", lambda e: e.tensor_tensor(out=t8[:], in0=t8[:], in1=negmx[:], op=ALU.add), reads=["t8", "negmx"], writes=["t8"])
                    P.op("act", lambda e: e.activation(PT[:, 64, :], t8[:], AF.Exp), reads=["t8"], writes=["PT"])
                    P.op("dve", lambda e: e.tensor_reduce(out=pr[:], in_=PT[:, :, :].rearrange("p j h -> p h j"), axis=AX.X, op=ALU.add), reads=["PT"], writes=["pr"])
                    P.op("pe", lambda e: e.matmul(pb[E_][:, 16:24], ones_f[:], pr[:], start=True, stop=True), reads=["pr", "ones_f"], writes=["PSE"])
                    for cc in range(2):
                        for pg in range(65):
                            P.op("pe", lambda e, cc=cc, pg=pg: e.matmul(pb[E_][:, cc * 8:(cc + 1) * 8], Gb[:, bs, pg, cc * 128:(cc + 1) * 128], PT[:, pg, :], start=(pg == 0), stop=(pg == 64)),
                                 reads=["PT", ("Gb", bs)], writes=["PSE"])
                    P.op("dve", lambda e: e.reciprocal(rden[:], pb[E_][:, 16:24]), reads=["PSE"], writes=["rden"])
                    for cc in range(2):
                        P.op("dve", lambda e, cc=cc: e.tensor_tensor(out=OLT[:, cc, :, b], in0=pb[E_][:, cc * 8:(cc + 1) * 8], in1=rden[:], op=ALU.mult), reads=["PSE", "rden"], writes=["OLT"])

                nsmp = int(os.environ.get("KNS", TS))
                for b in range(nsmp):
                    sample(b)
                if S2STOP < 6:
                    return
                for h in range(8):
                    for cc in range(2):
                        P.op("pe", lambda e, h=h, cc=cc: e.matmul(pb[X0][0:TS, h * 64:(h + 1) * 64], OLT[:, cc, h, :], wuv_t[:, cc, h * 64:(h + 1) * 64], start=(cc == 0), stop=(cc == 1)),
                             reads=["OLT", "wuv"], writes=[("acc", X0)])
                P.op("act", lambda e: e.activation(amix[:], pb[X0][0:TS, 0:512].rearrange("p (r c) -> p r c", r=4), AF.Copy), reads=[("acc", X0)], writes=["amix"])
                P.dma("sp", lambda e: e.dma_start(out=mixs_d.ap().rearrange("p (r c) -> p r c", r=4)[:, :, 0:128], in_=amix[:]), "amix_st", reads=["amix"], writes=["mixs_a"])

        if "T" in stages:
            phase_S2()

        rank_cache = {}

        def phase_C1():
            P.barrier()
            for i in range(4):
                P.dma("pool", lambda e, i=i: e.collective_compute("AllGather", ALU.bypass, replica_groups=GROUPS,
                                                                  ins=[ag2_in[i].ap()], outs=[ag2_out[i].ap()]),
                      ("cc2", i), reads=["ag2_in"], writes=[("ag2_out", i)], inc=1)
            def grab(e):
                rank = e.partition_id() % 4
                return [e.dma_start(out=mixloc[i].ap(), in_=ag2_out[i].ap().rearrange("r (t c) -> (r t) c", c=256).rearrange("(rr q) c -> q rr c", rr=4)[bass.ds(rank * 512, 512), :, :])
                        for i in range(4)]
            P.dma("pool", grab, "grab", reads=[("ag2_out", i) for i in range(4)], writes=[("mixloc", i) for i in range(4)], n=4)
            ph, pb = new_phase(6)
            with ph:
                pbT = ph.enter_context(nc.psum_tensor("pbT_c1", [128, 1024], BF16))
                stg = sb("stgc", [128, 2, 1408], F32)
                stg_ring = Ring("stg", 2)
                wout_t = sb("wout_t", [128, KC, D], BF16)
                gattn_bc = sb("gattn_bc", [128, 512], F32)
                NT = 256
                h_t = sb("hc_t", [128, 2, KC, NT], F32)
                h_ring = Ring("hc", 2)
                mt = sb("mt", [128, 2, 4, 256], BF16)
                mt_ring = Ring("mt", 2)
                junk = sb("junkc", [128, 512], F32)
                stc = sb("stc", [128, 4], F32)
                mixn = sb("mixn", [128, D], BF16)
                mixT = sb("mixT", [128, KC, NT], BF16)
                acc_ring = Ring("acc", 4)
                for k in range(KC):
                    load_rows(stg, stg_ring, wout_d, k * 128, 0, 1024, wout_t[:, k, :], "wout")
                P.dma("sp", lambda e: e.dma_start(out=gattn_bc[:], in_=gattn_bc_d), "c0", writes=["gattn_bc"])

                def mix_block(np_, src_fn, src_reads, col0):
                    ms, mkey = mt_ring.next()
                    P.dma("sp", lambda e: src_fn(e, ms), ("mt_ld", ms), reads=src_reads, writes=[mkey])
                    P.op("act", lambda e: e.activation(junk[0:np_, :].rearrange("p (r c) -> p r c", r=4), mt[0:np_, ms, :, 0:128], AF.Square), reads=[mkey], writes=["junkc"])
                    P.op("dve", lambda e: e.reduce_sum(out=stc[0:np_, 0:1], in_=junk[0:np_, :], axis=AX.X), reads=["junkc"], writes=["stc0"])
                    P.op("act", lambda e: e.activation(stc[0:np_, 1:2], stc[0:np_, 0:1], AF.Sqrt, bias=EPS, scale=1.0 / 512), reads=["stc0"], writes=["stc1"])
                    P.op("dve", lambda e: e.reciprocal(stc[0:np_, 2:3], stc[0:np_, 1:2]), reads=["stc1"], writes=["stc2"])
                    for rr in range(4):
                        P.op("dve", lambda e, rr=rr: e.scalar_tensor_tensor(out=mixn[0:np_, rr * 128:(rr + 1) * 128], in0=mt[0:np_, ms, rr, 0:128], scalar=stc[0:np_, 2:3],
                                                                           in1=gattn_bc[0:np_, rr * 128:(rr + 1) * 128], op0=ALU.mult, op1=ALU.mult),
                             reads=[mkey, "stc2", "gattn_bc"], writes=["mixn"])
                    P.op("pool", lambda e: e.tensor_copy(mixn[0:np_, 512:1024].rearrange("p (r c) -> p r c", r=4), mt[0:np_, ms, :, 128:256]), reads=[mkey], writes=["mixn"])
                    for k in range(KC):
                        P.op("pe", lambda e, k=k: e.transpose(pbT[:, k * 128:k * 128 + np_], mixn[0:np_, k * 128:(k + 1) * 128], ident_bf[0:np_, 0:np_]),
                             reads=["mixn", "ident_bf"], writes=["PSA"])
                    P.op("act", lambda e: e.activation(mixT[:, :, col0:col0 + np_], pbT[:, :].rearrange("p (k c) -> p k c", k=KC)[:, :, 0:np_], AF.Copy), reads=["PSA"], writes=["mixT"])

                def c1_tile(t0, n, hs, hkey):
                    P.dma("sp", lambda e: e.dma_start(out=h_t[:, hs, :, 0:n], in_=hT_d[:, t0:t0 + n].rearrange("(k p) n -> p k n", p=128)),
                          ("hc_ld", hs), reads=["hT_d"], writes=[hkey])
                    if t0 < TP:
                        for b2 in range(2):
                            blk = t0 // 128 + b2
                            i, j = blk // 4, blk % 4

                            def src(e, ms, i=i, j=j):
                                return e.dma_start(out=mt[:, ms, :, :], in_=mixloc[i].ap()[j * 128:(j + 1) * 128, :, :])
                            mix_block(128, src, [("mixloc", i)], b2 * 128)
                    else:
                        mix_block(TS, lambda e, ms: e.dma_start(out=mt[0:TS, ms, :, :], in_=mixs_d.ap().rearrange("p (r c) -> p r c", r=4)), ["mixs_d"], 0)
                    for o in range(KC):
                        a, akey = acc_ring.next()
                        for k in range(KC):
                            P.op("pe", lambda e, k=k, o=o, a=a: e.matmul(pb[a][:, 0:n], wout_t[:, k, o * 128:(o + 1) * 128], mixT[:, k, 0:n], start=(k == 0), stop=(k == KC - 1)),
                                 reads=["mixT", "wout"], writes=[akey])
                        P.op("dve", lambda e, o=o, a=a: e.tensor_tensor(out=h_t[:, hs, o, 0:n], in0=h_t[:, hs, o, 0:n], in1=pb[a][:, 0:n], op=ALU.add), reads=[akey, hkey], writes=[hkey])
                    P.dma("sp", lambda e: e.dma_start(out=hT_d[:, t0:t0 + n].rearrange("(k p) n -> p k n", p=128), in_=h_t[:, hs, :, 0:n]),
                          ("hc_st", hs), reads=[hkey], writes=["hT_d2"])

                for (t0, n) in token_tiles():
                    hs, hkey = h_ring.next()
                    c1_tile(t0, n, hs, hkey)

        def phase_C3():
            ph, pb = new_phase(6)
            with ph:
                stg = sb("stgp", [128, 2, 1408], F32)
                stg_ring = Ring("stg", 2)
                wpg_t = sb("wpg_t", [128, KC, D], BF16)
                wpp_t = sb("wpp_t", [128, 2, D], BF16)
                NT = 256
                h_t = sb("hp_t", [128, 2, KC, NT], F32)
                h_ring = Ring("hp", 2)
                sq_t = sb("sqp_t", [128, KC, NT], BF16)
                xn_t = sb("xnp_t", [128, KC, NT], BF16)
                y_t = sb("y_t", [128, KC, NT], F32)
                rs_t = sb("rsp_t", [128, 2, NT], F32)
                pe32 = sb("pe32", [128, 2, NT], F32)
                pe16 = sb("pe16", [128, 2, NT], BF16)
                sg_t = sb("sgp_t", [128, 2, NT], F32)
                ss_ps = pb[4]
                for k in range(KC):
                    load_rows(stg, stg_ring, wpg_d, k * 128, 0, 1024, wpg_t[:, k, :], "wpg")
                for k in range(2):
                    load_rows(stg, stg_ring, wpp_d, k * 128, 0, 1024, wpp_t[:, k, :], "wpp")

                def rms_fm(src_ap_fn, src_key, gcol, dst_t, dst_key, n):
                    P.op("act", lambda e: e.activation(sq_t[:, :, 0:n], src_ap_fn(slice(0, KC)), AF.Square), reads=[src_key], writes=["sq"])
                    for k in range(KC):
                        P.op("pe", lambda e, k=k: e.matmul(ss_ps[:, 0:n], ones_t[:], sq_t[:, k, 0:n], start=(k == 0), stop=(k == KC - 1)), reads=["sq", "ones_t"], writes=["ss_ps"])
                    P.op("act", lambda e: e.activation(rs_t[:, 0, 0:n], ss_ps[:, 0:n], AF.Sqrt, bias=EPS, scale=1.0 / D), reads=["ss_ps"], writes=["rs0"])
                    P.op("dve", lambda e: e.reciprocal(rs_t[:, 1, 0:n], rs_t[:, 0, 0:n]), reads=["rs0"], writes=["rs1"])
                    for k in range(KC):
                        P.op("dve", lambda e, k=k: e.scalar_tensor_tensor(out=dst_t[:, k, 0:n], in0=src_ap_fn(k), scalar=gc_t[:, gcol + k:gcol + k + 1],
                                                                         in1=rs_t[:, 1, 0:n], op0=ALU.mult, op1=ALU.mult), reads=[src_key, "rs1", "gc_t"], writes=[dst_key])

                def c3_tile(t0, n, hs, hkey):
                    P.dma("sp", lambda e: [e.dma_start(out=h_t[:, hs, :, 0:n], in_=hT_d[:, t0:t0 + n].rearrange("(k p) n -> p k n", p=128)),
                                           e.dma_start(out=pe32[:, :, 0:n], in_=peT[:, t0:t0 + n].rearrange("(k p) n -> p k n", p=128))],
                          ("hp_ld", hs), reads=["hT_d"], writes=[hkey, "pe32"], n=2)
                    P.op("pool", lambda e: e.tensor_copy(pe16[:, :, 0:n], pe32[:, :, 0:n]), reads=["pe32"], writes=["pe16"])
                    rms_fm(lambda k: h_t[:, hs, k, 0:n], hkey, GC_PLE, xn_t, "xn", n)
                    for o in range(KC):
                        ga, gk_ = o % 2, ("acc", o % 2)
                        pa, pk_ = 2 + o % 2, ("acc", 2 + o % 2)
                        for k in range(KC):
                            P.op("pe", lambda e, k=k, o=o, ga=ga: e.matmul(pb[ga][:, 0:n], wpg_t[:, k, o * 128:(o + 1) * 128], xn_t[:, k, 0:n], start=(k == 0), stop=(k == KC - 1)),
                                 reads=["xn", "wpg"], writes=[gk_])
                        for k in range(2):
                            P.op("pe", lambda e, k=k, o=o, pa=pa: e.matmul(pb[pa][:, 0:n], wpp_t[:, k, o * 128:(o + 1) * 128], pe16[:, k, 0:n], start=(k == 0), stop=(k == 1)),
                                 reads=["pe16", "wpp"], writes=[pk_])
                        s_ = o % 2
                        P.op("act", lambda e, ga=ga, s_=s_: e.activation(sg_t[:, s_, 0:n], pb[ga][:, 0:n], AF.Exp, scale=-1.0), reads=[gk_], writes=[("sgp", s_)])
                        P.op("dve", lambda e, s_=s_: e.tensor_scalar(out=sg_t[:, s_, 0:n], in0=sg_t[:, s_, 0:n], scalar1=1.0, scalar2=None, op0=ALU.add), reads=[("sgp", s_)], writes=[("sgp", s_)])
                        P.op("dve", lambda e, s_=s_: e.reciprocal(sg_t[:, s_, 0:n], sg_t[:, s_, 0:n]), reads=[("sgp", s_)], writes=[("sgp", s_)])
                        P.op("dve", lambda e, s_=s_, pa=pa: e.tensor_tensor(out=sg_t[:, s_, 0:n], in0=sg_t[:, s_, 0:n], in1=pb[pa][:, 0:n], op=ALU.mult), reads=[("sgp", s_), pk_], writes=[("sgp", s_)])
                        P.op("dve", lambda e, s_=s_, o=o: e.tensor_tensor(out=h_t[:, hs, o, 0:n], in0=h_t[:, hs, o, 0:n], in1=sg_t[:, s_, 0:n], op=ALU.add), reads=[("sgp", s_), hkey], writes=[hkey])
                    rms_fm(lambda k: h_t[:, hs, k, 0:n], hkey, GC_FIN, y_t, "y", n)
                    P.dma("sp", lambda e: e.dma_start(out=yT_out[:, t0:t0 + n].rearrange("(k p) n -> p k n", p=128), in_=y_t[:, :, 0:n]), "y_st", reads=["y"])

                for (t0, n) in token_tiles():
                    hs, hkey = h_ring.next()
                    c3_tile(t0, n, hs, hkey)

        if "C" in stages:
            phase_C1()
            ffn_sweep(wg2, wu2, wd2, GC_FF2, hT_d, hT_d, False)
            phase_C3()

        if DBG:
            P.barrier()
            P.dma("sp", lambda e: [e.dma_start(out=dbg_a[rr * 2048 + i * 512:rr * 2048 + (i + 1) * 512, :],
                                       in_=ag2_in[i].ap().rearrange("p (t c) -> (p t) c", c=256)[rr * 512:(rr + 1) * 512, :]) for i in range(4) for rr in range(4)],
                  "dbga", reads=["ag2_in"], n=16)
        P.barrier_final("sp")
        P.emit(st)
        print("plan stats", P.stats, "sems", P.n_sems, "sbuf top", A.top)
    return nc, dram_in, dram_out


def fm_cols(g, nk):
    return np.ascontiguousarray(np.asarray(g, np.float32).reshape(nk, 128).T)


def rope_tables(pos):
    half = 16
    inv = (np.float32(10000.0) ** (-(np.arange(half, dtype=np.float32) / np.float32(half)))).astype(np.float32)
    ang = (pos.astype(np.float32)[:, None] * inv[None, :]).astype(np.float32)
    c = np.cos(ang.astype(np.float64)).astype(np.float32)
    s = np.sin(ang.astype(np.float64)).astype(np.float32)
    C32 = np.concatenate([c, c], axis=1)
    S32 = np.concatenate([-s, s], axis=1)
    return C32, S32


def prep_inputs(inp, dram_in):
    bf = ml_dtypes.bfloat16
    maps = []
    gcols = np.zeros((128, 48), np.float32)
    gcols[:, 0:8] = fm_cols(inp["g_ff1"][0], 8)
    gcols[:, 8:16] = fm_cols(inp["g_mix"][0], 8)
    gcols[:, 16:24] = fm_cols(inp["g_ff2"][0], 8)
    gcols[:, 24:32] = fm_cols(inp["g_ple"][0], 8)
    gcols[:, 32:40] = fm_cols(inp["g_final"], 8)
    gcols[:, 40:43] = fm_cols(inp["g_q"][0], 3)
    gcols[:, 43:45] = fm_cols(inp["g_kv"][0], 2)
    w_in = inp["w_in"][0]
    w_uq = inp["w_uq"][0]
    w_uk = inp["w_uk"][0]
    w_uv = inp["w_uv"][0]
    C32, S32 = rope_tables(np.arange(SEQ))
    Cs, Ss = rope_tables(np.array([SEQ]))
    c8 = np.stack([np.broadcast_to(Cs[0][None, None, :], (TS, 8, 32)), np.broadcast_to(Ss[0][None, None, :], (TS, 8, 32))], axis=1).astype(np.float32)
    cache_cat = None
    wukT_z = np.zeros((128, 8 * 256), np.float32)
    for hh in range(8):
        wukT_z[(hh % 2) * 64:(hh % 2 + 1) * 64, hh * 256:(hh + 1) * 256] = w_uk[:, hh * 64:(hh + 1) * 64].T
    if "cache" in dram_in:
        cache_cat = np.concatenate([inp["cache_ckv"][0].reshape(-1, 256), inp["cache_krope"][0].reshape(-1, 32)], axis=1)
    p = np.arange(128)[:, None, None]
    j = np.arange(4)[None, :, None]
    col = np.arange(512)[None, None, :]
    amask = np.where(j * 128 + p <= col, 0.0, NEG).astype(bf)
    shared = {
        "gc": gcols,
        "wg1": inp["w_ff1_gate"][0], "wu1": inp["w_ff1_up"][0], "wd1": inp["w_ff1_down"][0],
        "ones_bf": np.ones((128, 128), bf),
        "ident_bf": np.eye(128, dtype=np.float32).astype(bf),
        "ident_f": np.eye(128, dtype=np.float32),
        "gkv_bc": np.broadcast_to(inp["g_kv"][0][None, :], (128, 256)),
        "CT": C32.T, "ST": S32.T, "Ctok": C32, "Stok": S32,
        "amask": amask,
        "w_in": w_in,
        "w_uq": w_uq, "w_uv": w_uv,
        "wukT": wukT_z,
        "gq_bc": np.broadcast_to(inp["g_q"][0][None, :], (TS, 384)),
        "c8": c8, "iota": np.arange(128, dtype=np.float32)[:, None],
        "negrow": np.concatenate([[0.0], np.full(127, NEG)]).astype(np.float32)[None, :],
        "cache": cache_cat,
        "wout": inp["w_out"][0], "wpg": inp["w_ple_gate"][0], "wpp": inp["w_ple_proj"][0],
        "wg2": inp["w_ff2_gate"][0], "wu2": inp["w_ff2_up"][0], "wd2": inp["w_ff2_down"][0],
        "gattn_bc": np.broadcast_to(inp["g_attn_out"][0][None, :], (128, 512)),
        "trimask": np.where(np.arange(128)[:, None] <= np.arange(128)[None, :], 0.0, NEG).astype(np.float32),
    }
    sw = np.concatenate([np.arange(16, 32), np.arange(0, 16)])
    for c in range(NCORES):
        b, r = c // 4, c % 4
        m = {}
        xp = inp["x_prompt"][b, r * TP:(r + 1) * TP, :]
        xs = inp["x_sample"][c * TS:(c + 1) * TS, 0, :]
        m["xT"] = np.concatenate([xp, xs], axis=0).T
        m["peT"] = np.concatenate([inp["p_prompt"][0, b, r * TP:(r + 1) * TP, :], inp["p_sample"][0, c * TS:(c + 1) * TS, 0, :]], axis=0).T
        wB = np.zeros((D, 1280), np.float32)
        wB[:, 0:640] = w_in[:, 0:640]
        wB[:, 640:672] = w_in[:, OFF_KR:OFF_KR + 32]
        wB[:, 672:704] = w_in[:, OFF_KR + sw]
        wB[:, 704:832] = w_in[:, OFF_MQ + r * 128:OFF_MQ + (r + 1) * 128]
        wB[:, 832:960] = w_in[:, OFF_MK + r * 128:OFF_MK + (r + 1) * 128]
        wB[:, 960:1088] = w_in[:, OFF_MV + r * 128:OFF_MV + (r + 1) * 128]
        wB[:, 1088:1216] = w_in[:, OFF_MO + r * 128:OFF_MO + (r + 1) * 128]
        wB[:, 1216] = w_in[:, OFF_MI + r]
        wB[:, 1217] = w_in[:, OFF_MF + r]
        m["wB"] = wB
        wQm = np.zeros((384, 320), np.float32)
        wKm = np.zeros((256, 256), np.float32)
        wVm = np.zeros((256, 128), np.float32)
        for jj in range(2):
            hh = 2 * r + jj
            wQm[:, jj * 160 + 0:jj * 160 + 32] = w_uq[:, hh * 96 + 64:hh * 96 + 96]
            wQm[:, jj * 160 + 64:jj * 160 + 128] = w_uq[:, hh * 96:hh * 96 + 64]
            wQm[:, jj * 160 + 128:jj * 160 + 160] = w_uq[:, hh * 96 + 64 + sw]
            wKm[:, jj * 128 + 64:jj * 128 + 128] = w_uk[:, hh * 64:(hh + 1) * 64]
            wVm[:, jj * 64:(jj + 1) * 64] = w_uv[:, hh * 64:(hh + 1) * 64]
        m["wQm"], m["wKm"], m["wVm"] = wQm, wKm, wVm
        sl = slice(c * TS, (c + 1) * TS)
        m["pt"] = inp["page_table"][sl].reshape(1, 1024).astype(np.int32)
        m["stC"] = inp["state_C"][0, sl].transpose(1, 0, 2, 3).reshape(64, 16384)
        m["stn"] = inp["state_n"][0, sl].transpose(1, 0, 2).reshape(64, 128)
        m["stm"] = inp["state_m"][0, sl].transpose(1, 0).reshape(64, 1)
        gbs = np.zeros((64, 4), np.float32)
        gbs[:, 0] = np.repeat(inp["b_gate_i"][0], 16); gbs[:, 1] = np.repeat(inp["b_gate_f"][0], 16); gbs[:, 2] = -gbs[:, 1]
        m["gbs"] = gbs
        m["gml_pm"] = np.repeat(inp["g_mlstm_out"][0], 16, axis=0)
        m["gml_bc"] = np.broadcast_to(inp["g_mlstm_out"][0, r][None, :], (128, 128))
        m["gb"] = np.array([[inp["b_gate_i"][0, r], inp["b_gate_f"][0, r], -inp["b_gate_f"][0, r], 0.0]], np.float32)
        for k_ in dram_in:
            if k_ not in m:
                m[k_] = shared[k_]
        out = {}
        for k_, (shape, dt) in dram_in.items():
            a = np.ascontiguousarray(m[k_])
            assert tuple(a.shape) == shape, (k_, a.shape, shape)
            out[k_] = a
        maps.append(out)
    return maps


_CACHE = {}


def run(inp, stages):
    if stages not in _CACHE:
        _CACHE[stages] = build(stages)
    nc, dram_in, dram_out = _CACHE[stages]
    maps = prep_inputs(inp, dram_in)
    res = run_bass_kernel_spmd(nc, maps, core_ids=list(range(NCORES)))
    return res.results


def assemble(res):
    f32 = np.float32
    y_p = np.zeros((2, SEQ, D), f32); y_s = np.zeros((128, 1, D), f32)
    ckv_p = np.zeros((1, 2, SEQ, 256), f32); kr_p = np.zeros((1, 2, SEQ, 32), f32)
    C_p = np.zeros((1, 2, 4, 128, 128), f32); n_p = np.zeros((1, 2, 4, 128), f32); m_p = np.zeros((1, 2, 4), f32)
    ckv_s = np.zeros((1, 128, 1, 256), f32); kr_s = np.zeros((1, 128, 1, 32), f32)
    C_s = np.zeros((1, 128, 4, 128, 128), f32); n_s = np.zeros((1, 128, 4, 128), f32); m_s = np.zeros((1, 128, 4), f32)
    for c in range(NCORES):
        b, r = c // 4, c % 4
        o = res[c]
        yT = np.asarray(o["yT_out"])
        y_p[b, r * TP:(r + 1) * TP] = yT[:, :TP].T
        sl = slice(c * TS, (c + 1) * TS)
        y_s[sl, 0] = yT[:, TP:].T
        ckv_p[0, b, r * TP:(r + 1) * TP] = np.asarray(o["ckv_p_out"])[r * TP:(r + 1) * TP]
        kr_p[0, b, r * TP:(r + 1) * TP] = np.asarray(o["kr_p_out"])[r * TP:(r + 1) * TP]
        C_p[0, b, r] = np.asarray(o["C_p_out"]); n_p[0, b, r] = np.asarray(o["n_p_out"])[:, 0]; m_p[0, b, r] = np.asarray(o["m_p_out"])[0, 0]
        ckv_s[0, sl, 0] = np.asarray(o["ckv_s_out"]); kr_s[0, sl, 0] = np.asarray(o["kr_s_out"])
        C_s[0, sl] = np.asarray(o["C_s_out"]).reshape(4, TS, 128, 128).transpose(1, 0, 2, 3)
        n_s[0, sl] = np.asarray(o["n_s_out"]).reshape(4, TS, 128).transpose(1, 0, 2)
        m_s[0, sl] = np.asarray(o["m_s_out"]).reshape(4, TS).T
    return (y_p, y_s, ckv_p, kr_p, C_p, n_p, m_p, ckv_s, kr_s, C_s, n_s, m_s)


def kernel(**inputs):
    inp = {k: np.asarray(v) for k, v in inputs.items()}
    res = run(inp, "AGBMSTC")
    return assemble(res)
```

```python
from contextlib import ExitStack
import os
import numpy as np
import ml_dtypes
import concourse.bass as bass
import concourse.mybir as mybir
from concourse.bass_utils import run_bass_kernel_spmd

F32 = mybir.dt.float32
BF16 = mybir.dt.bfloat16
I32 = mybir.dt.int32
ALU = mybir.AluOpType
AF = mybir.ActivationFunctionType
AX = mybir.AxisListType

NCORES = 8
D = 1024
KC = 8
FF = 2816
FC = 22
TP = 2048
TS = 16
TT = TP + TS
SEQ = 8192
EPS = 1e-6
IN_COLS = 2728
OFF_KV, OFF_KR, OFF_MQ, OFF_MK, OFF_MV, OFF_MO, OFF_MI, OFF_MF = 384, 640, 672, 1184, 1696, 2208, 2720, 2724
SCALE = 96.0 ** -0.5
NPAGE = 64
NEG = -30000.0

ENGS = ("pe", "act", "dve", "pool", "sp")


class Rec:
    __slots__ = ("eng", "fn", "deps", "is_dma", "sem", "val", "marked", "inc", "snap")

    def __init__(self, eng, fn):
        self.eng = eng
        self.fn = fn
        self.deps = []
        self.is_dma = False
        self.sem = None
        self.val = None
        self.marked = False
        self.inc = 16
        self.snap = None


class Plan:
    def __init__(self, nc):
        self.nc = nc
        self.streams = {e: [] for e in ENGS}
        self.res = {}
        self.dma_sem_counts = {}
        self.finals = []

    PSUM_STR = {"ss_ps", "ss", "TR", "BC", "BT", "BS", "BU", "PSA", "PSB", "PSC", "PSD", "PSE", "PSF", "PSG", "PSH", "scC", "scD"}
    PSUM_TUP = {"gu", "dn", "acc", "S", "O", "bank", "psr"}

    @staticmethod
    def _is_psum(key):
        if isinstance(key, str):
            return key in Plan.PSUM_STR
        return isinstance(key, tuple) and len(key) > 0 and key[0] in Plan.PSUM_TUP

    def _track(self, rec, reads, writes):
        reads = list(reads)
        writes = list(writes) + [r for r in reads if Plan._is_psum(r) and r not in writes]
        deps = []
        for r in reads:
            st = self.res.get(r)
            if st is not None and st[0] is not None:
                deps.append(st[0])
        for w in writes:
            st = self.res.get(w)
            if st is not None:
                if st[0] is not None:
                    deps.append(st[0])
                deps.extend(st[1])
        seen = set()
        for d in deps:
            if d is rec or id(d) in seen:
                continue
            seen.add(id(d))
            rec.deps.append(d)
        for r in reads:
            st = self.res.setdefault(r, [None, []])
            st[1].append(rec)
        for w in writes:
            self.res[w] = [rec, []]

    def op(self, eng, fn, reads=(), writes=()):
        rec = Rec(eng, fn)
        self.streams[eng].append(rec)
        self._track(rec, reads, writes)
        return rec

    def dma(self, queue, fn, semkey, reads=(), writes=(), final=False, n=1, inc=None):
        rec = Rec(queue, fn)
        rec.is_dma = True
        rec.inc = 16 * n if inc is None else inc
        self.streams[queue].append(rec)
        c = self.dma_sem_counts.get(semkey, 0) + rec.inc
        self.dma_sem_counts[semkey] = c
        rec.sem = semkey
        rec.val = c
        self._track(rec, reads, writes)
        if final:
            self.finals.append(rec)
        return rec

    def barrier_final(self, eng="sp"):
        rec = Rec(eng, None)
        rec.deps = list(self.finals)
        rec.snap = dict(self.dma_sem_counts)
        self.streams[eng].append(rec)

    def barrier(self):
        last = []
        for e in ENGS:
            for r in reversed(self.streams[e]):
                if r.fn is not None and not r.is_dma:
                    last.append(r)
                    break
        snap = dict(self.dma_sem_counts)
        for e in ENGS:
            rec = Rec(e, None)
            rec.deps = [r for r in last if not r.is_dma]
            rec.snap = snap
            self.streams[e].append(rec)
        self.res = {}

    def emit(self, stack):
        nc = self.nc
        for e in ENGS:
            for rec in self.streams[e]:
                for d in rec.deps:
                    if not d.is_dma:
                        if d.eng == rec.eng and d.eng == "pe":
                            continue
                        d.marked = True
        semh = {}
        for e in ENGS:
            semh[("eng", e)] = stack.enter_context(nc.semaphore("sem_" + e))
            cnt = 0
            for rec in self.streams[e]:
                if not rec.is_dma and rec.marked:
                    cnt += 1
                    rec.sem = ("eng", e)
                    rec.val = cnt
        for i, k in enumerate(self.dma_sem_counts):
            semh[k] = stack.enter_context(nc.semaphore("d%d" % i))
        self.n_sems = len(semh)
        block = stack.enter_context(nc.Block())
        stats = {}

        def run_stream(e):
            def body(eng):
                known = {}
                nw = 0
                for rec in self.streams[e]:
                    need = {}
                    for d in rec.deps:
                        if (not d.is_dma) and d.eng == e and e == "pe":
                            continue
                        if d.val is None:
                            continue
                        if need.get(d.sem, 0) < d.val:
                            need[d.sem] = d.val
                    if rec.snap is not None:
                        for k_, v_ in rec.snap.items():
                            if need.get(k_, 0) < v_:
                                need[k_] = v_
                    if rec.snap is not None and e == "sp" and os.environ.get("KDBGB"):
                        print("BARRIER sp need", {str(s_): v_ for s_, v_ in need.items() if isinstance(s_, tuple) and s_[0] == "eng"}, "known", {str(s_): v_ for s_, v_ in known.items() if isinstance(s_, tuple) and s_[0] == "eng"})
                    for s, v in need.items():
                        if known.get(s, 0) >= v:
                            continue
                        eng.wait_ge(semh[s], v)
                        known[s] = v
                        nw += 1
                    if rec.fn is None:
                        continue
                    ins = rec.fn(eng)
                    if rec.is_dma:
                        if isinstance(ins, (list, tuple)):
                            ins = [i_ for i_ in ins if i_ is not None]
                            assert len(ins) * 16 == rec.inc
                            for i_ in ins:
                                i_.then_inc(semh[rec.sem], 16)
                        else:
                            ins.then_inc(semh[rec.sem], rec.inc)
                    elif rec.marked:
                        ins.then_inc(semh[rec.sem], 1)
                stats[e] = (len(self.streams[e]), nw)
            return body

        block.tensor(run_stream("pe"))
        block.scalar(run_stream("act"))
        block.vector(run_stream("dve"))
        block.gpsimd(run_stream("pool"))
        block.sync(run_stream("sp"))
        self.stats = stats


class Ring:
    def __init__(self, name, n):
        self.name, self.n, self.i = name, n, 0

    def next(self):
        s = self.i % self.n
        self.i += 1
        return s, (self.name, s)


class Arena:
    LO, HI = 16512, 229344

    def __init__(self, nc):
        self.nc, self.top, self.n = nc, Arena.LO, 0

    def sb(self, name, shape, dt):
        esz = 2 if dt == BF16 else 4
        nbytes = esz
        for d in shape[1:]:
            nbytes *= d
        off = (self.top + 31) // 32 * 32
        assert off + nbytes <= Arena.HI, ("SBUF overflow", name, off, nbytes)
        self.top = off + nbytes
        self.n += 1
        return self.nc.alloc_sbuf_tensor_at("%s_%d" % (name, self.n), list(shape), dt, offset=off)

    def mark(self):
        return self.top

    def release(self, m):
        self.top = m


def token_tiles():
    tiles = [(i * 256, 256) for i in range(TP // 256)]
    tiles.append((TP, TS))
    return tiles


def build(stages="A"):
    import os
    nc = bass.Bass("TRN2", target_bir_lowering=False)
    st = ExitStack()
    dram_in = {}
    dram_out = {}

    def din(name, shape, dt=F32):
        dram_in[name] = (tuple(shape), dt)
        return nc.dram_tensor(name, list(shape), dt, kind="ExternalInput").ap()

    def dout(name, shape, dt=F32):
        dram_out[name] = (tuple(shape), dt)
        return nc.dram_tensor(name, list(shape), dt, kind="ExternalOutput").ap()

    xT = din("xT", [D, TT])
    gc = din("gc", [128, 48])
    wg1 = din("wg1", [D, FF]); wu1 = din("wu1", [D, FF]); wd1 = din("wd1", [FF, D])
    ones_bf = din("ones_bf", [128, 128], BF16)
    ident_bf_d = din("ident_bf", [128, 128], BF16)
    ident_f_d = din("ident_f", [128, 128], F32)
    wB_d = din("wB", [D, 1280])
    wQm_d = din("wQm", [384, 320])
    wKm_d = din("wKm", [256, 256])
    wVm_d = din("wVm", [256, 128])
    gkv_bc_d = din("gkv_bc", [128, 256])
    CT_d = din("CT", [32, SEQ]); ST_d = din("ST", [32, SEQ])
    Ctok_d = din("Ctok", [SEQ, 32]); Stok_d = din("Stok", [SEQ, 32])
    amask_d = din("amask", [128, 4, 512], BF16)
    gml_bc_d = din("gml_bc", [128, 128])
    trimask_d = din("trimask", [128, 128])
    gb_d = din("gb", [1, 4])
    wout_d = din("wout", [D, D]); wpg_d = din("wpg", [D, D]); wpp_d = din("wpp", [256, D])
    wg2 = din("wg2", [D, FF]); wu2 = din("wu2", [D, FF]); wd2 = din("wd2", [FF, D])
    gattn_bc_d = din("gattn_bc", [128, 512])
    peT = din("peT", [256, TT])
    w_in_d = din("w_in", [D, IN_COLS])
    w_uq_d = din("w_uq", [384, 768]); wukT_d = din("wukT", [128, 2048]); w_uv_d = din("w_uv", [256, 512])
    gq_bc_d = din("gq_bc", [TS, 384]); c8_d = din("c8", [TS, 2, 8, 32]); iota_d = din("iota", [128, 1]); negrow_d = din("negrow", [1, 128])
    pt_d = din("pt", [1, 1024], I32)
    cache_d = din("cache", [int(os.environ.get("KNPHYS", 10240)) * 128, 288])
    stC_d = din("stC", [64, 16384]); stn_d = din("stn", [64, 128]); stm_d = din("stm", [64, 1])
    gbs_d = din("gbs", [64, 4]); gml_pm_d = din("gml_pm", [64, 128])
    hT_d = nc.dram_tensor("hT_d", [D, TT], F32).ap()
    uT_loc = [nc.dram_tensor("uT_loc%d" % i, [128, 2 * TP], BF16) for i in range(4)]
    ag1_out = [nc.dram_tensor("ag1_out%d" % i, [512, 2 * TP], BF16) for i in range(4)]
    ag2_in = [nc.dram_tensor("ag2_in%d" % i, [128, 4096], BF16) for i in range(4)]
    ag2_out = [nc.dram_tensor("ag2_out%d" % i, [512, 4096], BF16) for i in range(4)]
    ckv_p_out = dout("ckv_p_out", [SEQ, 256])
    kr_p_out = dout("kr_p_out", [SEQ, 32])
    C_p_out = dout("C_p_out", [128, 128])
    n_p_out = dout("n_p_out", [128, 1])
    m_p_out = dout("m_p_out", [1, 1])
    yT_out = dout("yT_out", [D, TT])
    mixs_d = nc.dram_tensor("mixs_d", [TS, D], BF16)
    us_d = nc.dram_tensor("us_d", [128, KC * TS], BF16)
    zs_d = nc.dram_tensor("zs_d", [TS, IN_COLS], F32)
    ckv_s_out = dout("ckv_s_out", [TS, 256]); kr_s_out = dout("kr_s_out", [TS, 32])
    C_s_out = dout("C_s_out", [64, 16384]); n_s_out = dout("n_s_out", [64, 128]); m_s_out = dout("m_s_out", [64, 1])
    mixloc = [nc.dram_tensor("mixloc%d" % i, [512, 4, 256], BF16) for i in range(4)]
    DBG = bool(os.environ.get("KDBG"))
    if DBG:
        dbg_h = dout("dbg_h", [D, TT])
        dbg_u = dout("dbg_u", [D, TT], BF16)
        dbg_a = dout("dbg_a", [SEQ, 256], BF16)
        dbg_ub = dout("dbg_ub", [128, KC, 512], BF16)
        dbg_pt = dout("dbg_pt", [128, 324], F32)

    with st:
        P = Plan(nc)
        A = Arena(nc)
        sb = A.sb

        gc_t = sb("gc_t", [128, 48], F32)
        ones_t = sb("ones_t", [128, 128], BF16)
        ident_bf = sb("ident_bf", [128, 128], BF16)
        ident_f = sb("ident_f", [128, 128], F32)
        P.dma("sp", lambda e: e.dma_start(out=gc_t[:], in_=gc), "c0", writes=["gc_t"])
        P.dma("sp", lambda e: e.dma_start(out=ones_t[:], in_=ones_bf), "c1", writes=["ones_t"])
        P.dma("sp", lambda e: e.dma_start(out=ident_bf[:], in_=ident_bf_d), "c2", writes=["ident_bf"])
        P.dma("sp", lambda e: e.dma_start(out=ident_f[:], in_=ident_f_d), "c3", writes=["ident_f"])
        GC_FF1, GC_MIX, GC_FF2, GC_PLE, GC_FIN, GC_Q, GC_KV = 0, 8, 16, 24, 32, 40, 43
        persist_mark = A.mark()

        def new_phase(nbanks_f32=8):
            P.barrier()
            A.release(persist_mark)
            ph = ExitStack()
            banks = [ph.enter_context(nc.psum_tensor("pb%d_%d" % (i, A.n), [128, 512], F32)) for i in range(nbanks_f32)]
            return ph, banks

        def load_rows(stg, stg_ring, src, r0, c0, ncols, dst_ap, dst_key, nrows=128):
            s, skey = stg_ring.next()
            P.dma("sp", lambda e: e.dma_start(out=stg[0:nrows, s, 0:ncols], in_=src[r0:r0 + nrows, c0:c0 + ncols]),
                  ("stg", s), writes=[skey])
            P.op("pool", lambda e: e.tensor_copy(dst_ap, stg[0:nrows, s, 0:ncols]), reads=[skey], writes=[dst_key])

        def ffn_sweep(wg, wu, wd, gcol_in, src_dram, dst_dram, emit_u):
            ph, pb = new_phase(5)
            with ph:
                wg_t = sb("wg_t", [128, KC, FF], BF16)
                wu_t = sb("wu_t", [128, KC, FF], BF16)
                wd_t = sb("wd_t", [128, FC, D], BF16)
                stg = sb("stg", [128, 2, 1408], F32)
                stg_ring = Ring("stg", 2)
                NT = 256
                h_t = sb("h_t", [128, 2, KC, NT], F32)
                h_ring = Ring("h", 2)
                sq_t = sb("sq_t", [128, KC, NT], BF16)
                xn_t = sb("xn_t", [128, KC, NT], BF16)
                hid_t = sb("hid_t", [128, FC, NT], BF16)
                sg_t = sb("sg_t", [128, 2, NT], F32)
                sg_ring = Ring("sg", 2)
                rs_t = sb("rs_t", [128, 2, NT], F32)
                u_t = sb("u_t", [128, KC, NT], BF16)
                gu_ring = Ring("gu", 2)
                ss_ps = pb[2]
                dn_ring = Ring("dn", 2)

                for half in range(2):
                    c0 = half * 1408
                    for k in range(KC):
                        load_rows(stg, stg_ring, wg, k * 128, c0, 1408, wg_t[:, k, c0:c0 + 1408], ("wg", half))
                        load_rows(stg, stg_ring, wu, k * 128, c0, 1408, wu_t[:, k, c0:c0 + 1408], ("wu", half))
                for f in range(FC):
                    load_rows(stg, stg_ring, wd, f * 128, 0, 1024, wd_t[:, f, :], ("wd", f))

                def rms_fm(src_ap_fn, src_key, gcol, dst_t, dst_key, n):
                    P.op("act", lambda e: e.activation(sq_t[:, :, 0:n], src_ap_fn(slice(0, KC)), AF.Square),
                         reads=[src_key], writes=["sq"])
                    for k in range(KC):
                        P.op("pe", lambda e, k=k: e.matmul(ss_ps[:, 0:n], ones_t[:], sq_t[:, k, 0:n], start=(k == 0), stop=(k == KC - 1)),
                             reads=["sq", "ones_t"], writes=["ss_ps"])
                    P.op("act", lambda e: e.activation(rs_t[:, 0, 0:n], ss_ps[:, 0:n], AF.Sqrt, bias=EPS, scale=1.0 / D),
                         reads=["ss_ps"], writes=["rs0"])
                    P.op("dve", lambda e: e.reciprocal(rs_t[:, 1, 0:n], rs_t[:, 0, 0:n]), reads=["rs0"], writes=["rs1"])
                    for k in range(KC):
                        P.op("dve", lambda e, k=k: e.scalar_tensor_tensor(out=dst_t[:, k, 0:n], in0=src_ap_fn(k), scalar=gc_t[:, gcol + k:gcol + k + 1],
                                                                         in1=rs_t[:, 1, 0:n], op0=ALU.mult, op1=ALU.mult),
                             reads=[src_key, "rs1", "gc_t"], writes=[dst_key])

                tiles = token_tiles()
                if os.environ.get("KMAXT"):
                    tiles = tiles[:int(os.environ["KMAXT"])]
                def do_tile(t0, n, hs, hkey):
                    P.dma("sp", lambda e, hs=hs, t0=t0, n=n: e.dma_start(out=h_t[:, hs, :, 0:n], in_=src_dram[:, t0:t0 + n].rearrange("(k p) n -> p k n", p=128)),
                          ("h_ld", hs), reads=["src_dram"], writes=[hkey])
                    rms_fm(lambda k, hs=hs, n=n: h_t[:, hs, k, 0:n], hkey, gcol_in, xn_t, "xn", n)
                    for f in range(FC):
                        g, gkey = gu_ring.next()
                        half = 0 if f < 11 else 1
                        for k in range(KC):
                            P.op("pe", lambda e, k=k, f=f, g=g: e.matmul(pb[g][:, 0:n], wg_t[:, k, f * 128:(f + 1) * 128], xn_t[:, k, 0:n],
                                                                        start=(k == 0), stop=(k == KC - 1)),
                                 reads=["xn", ("wg", half)], writes=[gkey])
                        for k in range(KC):
                            P.op("pe", lambda e, k=k, f=f, g=g: e.matmul(pb[g][:, 256:256 + n], wu_t[:, k, f * 128:(f + 1) * 128], xn_t[:, k, 0:n],
                                                                        start=(k == 0), stop=(k == KC - 1)),
                                 reads=["xn", ("wu", half)], writes=[gkey])
                        s, skey = sg_ring.next()
                        P.op("act", lambda e, g=g, s=s: e.activation(sg_t[:, s, 0:n], pb[g][:, 0:n], AF.Silu), reads=[gkey], writes=[skey])
                        P.op("dve", lambda e, g=g, s=s, f=f: e.tensor_tensor(out=hid_t[:, f, 0:n], in0=sg_t[:, s, 0:n], in1=pb[g][:, 256:256 + n], op=ALU.mult),
                             reads=[skey, gkey], writes=[("hid", f)])
                    for o in range(KC):
                        d, dkey = dn_ring.next()
                        for f in range(FC):
                            P.op("pe", lambda e, f=f, o=o, d=d: e.matmul(pb[3 + d][:, 0:n], wd_t[:, f, o * 128:(o + 1) * 128], hid_t[:, f, 0:n],
                                                                        start=(f == 0), stop=(f == FC - 1)),
                                 reads=[("hid", f), ("wd", f)], writes=[dkey])
                        P.op("dve", lambda e, o=o, d=d, hs=hs: e.scalar_tensor_tensor(out=h_t[:, hs, o, 0:n], in0=pb[3 + d][:, 0:n], scalar=0.5,
                                                                                     in1=h_t[:, hs, o, 0:n], op0=ALU.mult, op1=ALU.add),
                             reads=[dkey, hkey], writes=[hkey])
                    P.dma("sp", lambda e, hs=hs, t0=t0, n=n: e.dma_start(out=dst_dram[:, t0:t0 + n].rearrange("(k p) n -> p k n", p=128), in_=h_t[:, hs, :, 0:n]),
                          ("h_st", hs), reads=[hkey], writes=["dst_dram"])
                    if emit_u:
                        rms_fm(lambda k, hs=hs, n=n: h_t[:, hs, k, 0:n], hkey, GC_MIX, u_t, "u", n)
                        if t0 >= TP:
                            P.dma("sp", lambda e, n=n: e.dma_start(out=us_d.ap().rearrange("p (k n) -> p k n", k=KC), in_=u_t[:, :, 0:n]), "us_st", reads=["u"], writes=["us_d"])
                        if t0 < TP:
                            P.dma("sp", lambda e, t0=t0, n=n: [e.dma_start(out=uT_loc[i].ap().rearrange("p (k n) -> p k n", k=2)[:, :, t0:t0 + n], in_=u_t[:, 2 * i:2 * i + 2, 0:n])
                                                                for i in range(4)],
                                  "u_st", reads=["u"], writes=["uT_loc"], n=4)
                        if DBG:
                            P.dma("sp", lambda e, hs=hs, t0=t0, n=n: e.dma_start(out=dbg_h[:, t0:t0 + n].rearrange("(k p) n -> p k n", p=128), in_=h_t[:, hs, :, 0:n]),
                                  ("h_st2", hs), reads=[hkey])
                            P.dma("sp", lambda e, t0=t0, n=n: e.dma_start(out=dbg_u[:, t0:t0 + n].rearrange("(k p) n -> p k n", p=128), in_=u_t[:, :, 0:n]),
                                  "u_st2", reads=["u"])

                for (t0, n) in tiles:
                    hs, hkey = h_ring.next()
                    do_tile(t0, n, hs, hkey)

        if "A" in stages:
            ffn_sweep(wg1, wu1, wd1, GC_FF1, xT, hT_d, True)

        GROUPS = [[0, 1, 2, 3], [4, 5, 6, 7]]
        if "G" in stages:
            P.barrier()
            for i in range(4):
                P.dma("pool", lambda e, i=i: e.collective_compute("AllGather", ALU.bypass, replica_groups=GROUPS,
                                                                  ins=[uT_loc[i].ap()], outs=[ag1_out[i].ap()]),
                      ("cc1", i), reads=["uT_loc"], writes=[("ag1_out", i)], inc=1)

        def phase_B():
            ph, pb = new_phase(8)
            with ph:
                stg = sb("stgb", [128, 2, 1408], F32)
                stg_ring = Ring("stg", 2)
                wB_t = sb("wB_t", [128, KC, 704], BF16)
                wQ_t = sb("wQ_t", [128, 3, 320], BF16)
                wK_t = sb("wK_t", [128, 2, 256], BF16)
                wV_t = sb("wV_t", [128, 2, 128], BF16)
                gkv_bc = sb("gkv_bc", [128, 256], F32)
                amask = sb("amask", [128, 4, 512], BF16)
                QT = sb("QT", [128, 2, SEQ], BF16)
                KT = sb("KT", [128, 2, SEQ], BF16)
                VA = sb("VA", [128, 2, 64, 65], BF16)
                qkmax = sb("qkmax", [128, 4], F32)
                tmax = sb("tmax", [128, 2], F32)
                negm = sb("negm", [128, 4], F32)
                u_t = sb("ub_t", [128, 2, KC, 512], BF16)
                u_ring = Ring("ub", 2)
                zq_sb = sb("zq_sb", [128, 3, 512], F32)
                sqb = sb("sqb", [128, 3, 512], BF16)
                qlat = sb("qlat", [128, 3, 512], BF16)
                ckvT = sb("ckvT", [128, 2, 512], BF16)
                rsb = sb("rsb", [128, 2, 512], F32)
                cs_t = sb("cs_t", [32, 2, 2, 512], F32)
                cs_ring = Ring("cs", 2)
                cst_t = sb("cst_t", [128, 2, 2, 4, 32], F32)
                rp_t = sb("rp_t", [32, 2, 512], F32)
                otok = sb("otok", [128, 2, 320], F32)
                otok_ring = Ring("otok", 2)
                junk = sb("junk", [128, 256], F32)
                st1 = sb("st1", [128, 4], F32)
                PT = sb("PT", [128, 3, 512], BF16)
                pt_ring = Ring("pt", 3)
                osb = sb("osb", [65, 512], F32)
                atok = sb("atok", [128, 2, 4, 64], BF16)
                atok_ring = Ring("atok", 2)
                rd = sb("rd", [128, 4], F32)
                acc_ring = Ring("acc", 5)
                SS, TR, O0 = 5, 6, 7

                for k in range(KC):
                    load_rows(stg, stg_ring, wB_d, k * 128, 0, 704, wB_t[:, k, :], "wB")
                for c in range(3):
                    load_rows(stg, stg_ring, wQm_d, c * 128, 0, 320, wQ_t[:, c, :], "wQ")
                for c in range(2):
                    load_rows(stg, stg_ring, wKm_d, c * 128, 0, 256, wK_t[:, c, :], "wK")
                    load_rows(stg, stg_ring, wVm_d, c * 128, 0, 128, wV_t[:, c, :], "wV")
                P.dma("sp", lambda e: e.dma_start(out=gkv_bc[:], in_=gkv_bc_d), "c0", writes=["gkv_bc"])
                P.dma("sp", lambda e: e.dma_start(out=amask[:], in_=amask_d), "c1", writes=["amask"])
                P.op("pool", lambda e: e.memset(QT[:, 0, :], 0.0), writes=["QT"])
                P.op("pool", lambda e: e.memset(QT[:, 1, :], 0.0), writes=["QT"])
                P.op("pool", lambda e: e.memset(KT[:, 0, :], 0.0), writes=["KT"])
                P.op("pool", lambda e: e.memset(KT[:, 1, :], 0.0), writes=["KT"])
                P.op("pool", lambda e: e.memset(VA[:], 1.0), writes=["VA"])
                P.op("dve", lambda e: e.memset(qkmax[:], 0.0), writes=["qkmax"])

                def proj_fm(out_bank, lhs_fn, rhs_fn, nk, m, reads, okey, n=512):
                    for k in range(nk):
                        P.op("pe", lambda e, k=k: e.matmul(pb[out_bank][0:m, 0:n], lhs_fn(k), rhs_fn(k), start=(k == 0), stop=(k == nk - 1)),
                             reads=reads, writes=[okey])

                def norm_fm(src_t, nch, dim, gcol, dst_t, dst_key, skey):
                    P.op("act", lambda e: e.activation(sqb[:, 0:nch, :], src_t[:, 0:nch, :], AF.Square), reads=[skey], writes=["sqb"])
                    for c in range(nch):
                        P.op("pe", lambda e, c=c: e.matmul(pb[SS][:, :], ones_t[:], sqb[:, c, :], start=(c == 0), stop=(c == nch - 1)),
                             reads=["sqb", "ones_t"], writes=["ss"])
                    P.op("act", lambda e: e.activation(rsb[:, 0, :], pb[SS][:, :], AF.Sqrt, bias=EPS, scale=1.0 / dim), reads=["ss"], writes=["rsb0"])
                    P.op("dve", lambda e: e.reciprocal(rsb[:, 1, :], rsb[:, 0, :]), reads=["rsb0"], writes=["rsb1"])
                    for c in range(nch):
                        P.op("dve", lambda e, c=c: e.scalar_tensor_tensor(out=dst_t[:, c, :], in0=src_t[:, c, :], scalar=gc_t[:, gcol + c:gcol + c + 1],
                                                                         in1=rsb[:, 1, :], op0=ALU.mult, op1=ALU.mult),
                             reads=[skey, "rsb1", "gc_t"], writes=[dst_key])

                def bound_update(src_ap_fn, skeys, col):
                    P.op("act", lambda e: e.activation(sqb[:, 0, :], src_ap_fn(), AF.Square), reads=list(skeys), writes=["sqb"])
                    P.op("pe", lambda e: e.matmul(pb[SS][:, :], ones_t[:], sqb[:, 0, :], start=True, stop=True), reads=["sqb", "ones_t"], writes=["ss"])
                    P.op("dve", lambda e: e.reduce_max(out=tmax[:, 0:1], in_=pb[SS][:, :], axis=AX.X), reads=["ss"], writes=["tmax"])
                    P.op("dve", lambda e: e.tensor_tensor(out=qkmax[:, col:col + 1], in0=qkmax[:, col:col + 1], in1=tmax[:, 0:1], op=ALU.max),
                         reads=["tmax", "qkmax"], writes=["qkmax"])

                ntt = int(os.environ.get("KNTT", 16))
                for tt in range(ntt):
                    rank, c0 = tt // 4, (tt % 4) * 512
                    tok0 = tt * 512
                    cols = slice(tok0, tok0 + 512)
                    us, ukey = u_ring.next()
                    P.dma("sp", lambda e, us=us, rank=rank, c0=c0: [e.dma_start(
                        out=u_t[:, us, 2 * i:2 * i + 2, :], in_=ag1_out[i].ap()[rank * 128:(rank + 1) * 128, :].rearrange("p (k n) -> p k n", k=2)[:, :, c0:c0 + 512])
                        for i in range(4)],
                        ("ub_ld", us), reads=[("ag1_out", i) for i in range(4)], writes=[ukey], n=4)
                    if DBG and tt == 1:
                        P.dma("sp", lambda e, us=us: e.dma_start(out=dbg_ub, in_=u_t[:, us, :, :]), "dbgub", reads=[ukey])
                    cs, cskey = cs_ring.next()
                    P.dma("sp", lambda e, cs=cs, tok0=tok0: [e.dma_start(out=cs_t[:, cs, 0, :], in_=CT_d[:, tok0:tok0 + 512]),
                                                            e.dma_start(out=cs_t[:, cs, 1, :], in_=ST_d[:, tok0:tok0 + 512]),
                                                            e.dma_start(out=cst_t[:, cs, 0, :, :], in_=Ctok_d[tok0:tok0 + 512, :].rearrange("(j p) r -> p j r", p=128)),
                                                            e.dma_start(out=cst_t[:, cs, 1, :, :], in_=Stok_d[tok0:tok0 + 512, :].rearrange("(j p) r -> p j r", p=128))],
                          ("cs_ld", cs), writes=[cskey], n=4)
                    for c in range(3):
                        a, akey = acc_ring.next()
                        proj_fm(a, lambda k, c=c: wB_t[:, k, c * 128:(c + 1) * 128], lambda k, us=us: u_t[:, us, k, :], KC, 128, [ukey, "wB"], akey)
                        P.op("act", lambda e, a=a, c=c: e.activation(zq_sb[:, c, :], pb[a][:, :], AF.Copy), reads=[akey], writes=["zq_sb"])
                    norm_fm(zq_sb, 3, 384, GC_Q, qlat, "qlat", "zq_sb")
                    for c in range(2):
                        a, akey = acc_ring.next()
                        proj_fm(a, lambda k, c=c: wB_t[:, k, 384 + c * 128:384 + (c + 1) * 128], lambda k, us=us: u_t[:, us, k, :], KC, 128, [ukey, "wB"], akey)
                        P.op("act", lambda e, a=a, c=c: e.activation(zq_sb[:, c, :], pb[a][:, :], AF.Copy), reads=[akey], writes=["zq_sb"])
                    norm_fm(zq_sb, 2, 256, GC_KV, ckvT, "ckvT", "zq_sb")
                    a1, a1key = acc_ring.next()
                    proj_fm(a1, lambda k: wB_t[:, k, 640:672], lambda k, us=us: u_t[:, us, k, :], KC, 32, [ukey, "wB"], a1key)
                    a2, a2key = acc_ring.next()
                    proj_fm(a2, lambda k: wB_t[:, k, 672:704], lambda k, us=us: u_t[:, us, k, :], KC, 32, [ukey, "wB"], a2key)
                    P.op("dve", lambda e, a1=a1, cs=cs: e.tensor_tensor(out=rp_t[:, 0, :], in0=pb[a1][0:32, :], in1=cs_t[:, cs, 0, :], op=ALU.mult),
                         reads=[a1key, cskey], writes=["rp0"])
                    P.op("dve", lambda e, a2=a2, cs=cs: e.tensor_tensor(out=rp_t[:, 1, :], in0=pb[a2][0:32, :], in1=cs_t[:, cs, 1, :], op=ALU.mult),
                         reads=[a2key, cskey], writes=["rp1"])
                    P.op("dve", lambda e, cols=cols: e.tensor_tensor(out=KT[0:32, 0, cols], in0=rp_t[:, 0, :], in1=rp_t[:, 1, :], op=ALU.add),
                         reads=["rp0", "rp1", "KT"], writes=[("KT", tt)])
                    P.op("pool", lambda e, cols=cols: e.tensor_copy(KT[0:32, 1, cols], KT[0:32, 0, cols]), reads=[("KT", tt), "KT"], writes=[("KT1", tt)])
                    for j in range(4):
                        a, akey = acc_ring.next()
                        proj_fm(a, lambda k, us=us, j=j: u_t[:, us, k, j * 128:(j + 1) * 128], lambda k: wB_t[:, k, 384:704], KC, 128, [ukey, "wB"], akey, n=320)
                        os_, okey = otok_ring.next()
                        P.op("act", lambda e, a=a: e.activation(junk[:], pb[a][:, 0:256], AF.Square), reads=[akey], writes=["junk"])
                        P.op("dve", lambda e: e.reduce_sum(out=st1[:, 0:1], in_=junk[:], axis=AX.X), reads=["junk"], writes=["st1a"])
                        P.op("act", lambda e: e.activation(st1[:, 1:2], st1[:, 0:1], AF.Sqrt, bias=EPS, scale=1.0 / 256), reads=["st1a"], writes=["st1b"])
                        P.op("dve", lambda e: e.reciprocal(st1[:, 2:3], st1[:, 1:2]), reads=["st1b"], writes=["st1c"])
                        P.op("dve", lambda e, a=a, os_=os_: e.scalar_tensor_tensor(out=otok[:, os_, 0:256], in0=pb[a][:, 0:256], scalar=st1[:, 2:3], in1=gkv_bc[:],
                                                                                 op0=ALU.mult, op1=ALU.mult), reads=[akey, "st1c", "gkv_bc"], writes=[okey])
                        P.op("dve", lambda e, a=a, os_=os_, cs=cs, j=j: e.tensor_tensor(out=otok[:, os_, 256:288], in0=pb[a][:, 256:288], in1=cst_t[:, cs, 0, j, :], op=ALU.mult),
                             reads=[akey, cskey], writes=[okey])
                        P.op("dve", lambda e, a=a, os_=os_, cs=cs, j=j: e.tensor_tensor(out=otok[:, os_, 288:320], in0=pb[a][:, 288:320], in1=cst_t[:, cs, 1, j, :], op=ALU.mult),
                             reads=[akey, cskey], writes=[okey])
                        P.op("dve", lambda e, os_=os_: e.tensor_tensor(out=otok[:, os_, 256:288], in0=otok[:, os_, 256:288], in1=otok[:, os_, 288:320], op=ALU.add),
                             reads=[okey], writes=[okey])
                        if DBG and tt == 0 and j == 0:
                            dpt = sb("dpt", [128, 324], F32)
                            P.op("dve", lambda e, a=a: e.tensor_copy(dpt[:, 0:320], pb[a][:, 0:320]), reads=[akey], writes=["dpt"])
                            P.op("dve", lambda e: e.tensor_copy(dpt[:, 320:324], st1[:, 0:4]), reads=["st1c"], writes=["dpt"])
                            P.dma("sp", lambda e: e.dma_start(out=dbg_pt, in_=dpt[:]), "dbgpt", reads=["dpt"])
                        r0 = tok0 + j * 128
                        P.dma("sp", lambda e, os_=os_, r0=r0: [e.dma_start(out=ckv_p_out[r0:r0 + 128, :], in_=otok[:, os_, 0:256]),
                                                              e.dma_start(out=kr_p_out[r0:r0 + 128, :], in_=otok[:, os_, 256:288])],
                              ("otok_st", os_), reads=[okey], n=2)
                    for h in range(2):
                        qa, qakey = acc_ring.next()
                        proj_fm(qa, lambda c, h=h: wQ_t[:, c, h * 160:h * 160 + 128], lambda c: qlat[:, c, :], 3, 128, ["qlat", "wQ"], qakey)
                        qb, qbkey = acc_ring.next()
                        proj_fm(qb, lambda c, h=h: wQ_t[:, c, h * 160 + 128:h * 160 + 160], lambda c: qlat[:, c, :], 3, 32, ["qlat", "wQ"], qbkey)
                        P.op("act", lambda e, qa=qa, h=h, cols=cols: e.activation(QT[64:128, h, cols], pb[qa][64:128, :], AF.Copy), reads=[qakey, "QT"], writes=[("QT", h, tt)])
                        P.op("dve", lambda e, qa=qa, cs=cs: e.tensor_tensor(out=rp_t[:, 0, :], in0=pb[qa][0:32, :], in1=cs_t[:, cs, 0, :], op=ALU.mult),
                             reads=[qakey, cskey], writes=["rp0"])
                        P.op("dve", lambda e, qb=qb, cs=cs: e.tensor_tensor(out=rp_t[:, 1, :], in0=pb[qb][0:32, :], in1=cs_t[:, cs, 1, :], op=ALU.mult),
                             reads=[qbkey, cskey], writes=["rp1"])
                        P.op("dve", lambda e, h=h, cols=cols: e.tensor_tensor(out=QT[0:32, h, cols], in0=rp_t[:, 0, :], in1=rp_t[:, 1, :], op=ALU.add),
                             reads=["rp0", "rp1", "QT"], writes=[("QT", h, tt)])
                        bound_update(lambda h=h, cols=cols: QT[:, h, cols], [("QT", h, tt)], h)
                    for h in range(2):
                        a, akey = acc_ring.next()
                        proj_fm(a, lambda c, h=h: wK_t[:, c, h * 128:(h + 1) * 128], lambda c: ckvT[:, c, :], 2, 128, ["ckvT", "wK"], akey)
                        kt_keys = [("KT", tt)] if h == 0 else [("KT1", tt)]
                        P.op("act", lambda e, a=a, h=h, cols=cols: e.activation(KT[64:128, h, cols], pb[a][64:128, :], AF.Copy), reads=[akey, "KT"], writes=[("KTn", h, tt)])
                        bound_update(lambda h=h, cols=cols: KT[:, h, cols], kt_keys + [("KTn", h, tt)], 2 + h)
                    a, akey = acc_ring.next()
                    for j in range(4):
                        for c in range(2):
                            P.op("pe", lambda e, a=a, j=j, c=c: e.matmul(pb[a][:, j * 128:(j + 1) * 128], ckvT[:, c, j * 128:(j + 1) * 128], wV_t[:, c, :],
                                                                        start=(c == 0), stop=(c == 1)), reads=["ckvT", "wV"], writes=[akey])
                    for h in range(2):
                        P.op("act", lambda e, a=a, h=h, tt=tt: e.activation(VA[:, h, tt * 4:(tt + 1) * 4, 0:64],
                                                                           pb[a][:, :].rearrange("p (j h d) -> p j h d", j=4, h=2)[:, :, h, :], AF.Copy),
                             reads=[akey, "VA"], writes=[("VA", h, tt)])

                P.barrier()
                P.op("dve", lambda e: e.tensor_tensor(out=negm[:, 0:2], in0=qkmax[:, 0:2], in1=qkmax[:, 2:4], op=ALU.mult), writes=["negm"])
                P.op("act", lambda e: e.activation(negm[:, 2:4], negm[:, 0:2], AF.Sqrt), reads=["negm"], writes=["negm2"])
                P.op("dve", lambda e: e.tensor_scalar(out=negm[:, 0:2], in0=negm[:, 2:4], scalar1=-SCALE * 1.02, scalar2=None, op0=ALU.mult),
                     reads=["negm2"], writes=["negm3"])

                s_ring = Ring("S", 5)
                o_banks = [O0]
                nqt = int(os.environ.get("KNQT", 16))
                for h in range(2):
                    for qt in range(nqt):
                        nkb = 4 * (qt + 1)
                        okey = ("O", 0)
                        for kb in range(nkb):
                            sbk, skey = s_ring.next()
                            diag = kb >= 4 * qt
                            P.op("pe", lambda e, sbk=sbk, h=h, kb=kb, qt=qt, diag=diag: e.matmul(pb[sbk][:, :], KT[:, h, kb * 128:(kb + 1) * 128], QT[:, h, qt * 512:(qt + 1) * 512],
                                                                                               start=True, stop=(not diag)), writes=[skey])
                            if diag:
                                P.op("pe", lambda e, sbk=sbk, j=kb - 4 * qt: e.matmul(pb[sbk][:, :], ident_bf[:], amask[:, j, :], start=False, stop=True),
                                     reads=["amask", "ident_bf"], writes=[skey])
                            ps_, pkey = pt_ring.next()
                            P.op("act", lambda e, sbk=sbk, ps_=ps_, h=h: e.activation(PT[:, ps_, :], pb[sbk][:, :], AF.Exp, bias=negm[:, h:h + 1], scale=SCALE),
                                 reads=[skey, "negm3"], writes=[pkey])
                            P.op("pe", lambda e, ps_=ps_, h=h, kb=kb, nkb=nkb: e.matmul(pb[O0][0:65, :], VA[:, h, kb, :], PT[:, ps_, :], start=(kb == 0), stop=(kb == nkb - 1)),
                                 reads=[pkey], writes=[okey])
                        P.op("act", lambda e: e.activation(osb[:, :], pb[O0][0:65, :], AF.Copy), reads=[okey], writes=["osb"])
                        for j in range(4):
                            P.op("pe", lambda e, j=j: e.transpose(pb[TR][:, j * 65:(j + 1) * 65], osb[0:65, j * 128:(j + 1) * 128], ident_f[0:65, 0:65]),
                                 reads=["osb", "ident_f"], writes=["TR"])
                        P.op("dve", lambda e: e.reciprocal(rd[:, :], pb[TR][:, 0:260].rearrange("p (j c) -> p j c", j=4)[:, :, 64]), reads=["TR"], writes=["rd"])
                        as_, akey = atok_ring.next()
                        for j in range(4):
                            P.op("dve", lambda e, j=j, as_=as_: e.tensor_scalar(out=atok[:, as_, j, :], in0=pb[TR][:, j * 65:j * 65 + 64], scalar1=rd[:, j:j + 1], scalar2=None, op0=ALU.mult),
                                 reads=["TR", "rd"], writes=[akey])
                        P.dma("sp", lambda e, as_=as_, qt=qt, h=h: e.dma_start(
                            out=ag2_in[qt % 4].ap().rearrange("p (t c) -> (p t) c", c=256)[(qt // 4) * 512:(qt // 4 + 1) * 512, h * 64:(h + 1) * 64].rearrange("(j p) d -> p j d", p=128),
                            in_=atok[:, as_, :, :]),
                            ("atok_st", as_), reads=[akey], writes=["ag2_in"])

        if "B" in stages:
            phase_B()

        def phase_M():
            ph, pb = new_phase(8)
            with ph:
                stg = sb("stgm", [128, 2, 1408], F32)
                stg_ring = Ring("stg", 2)
                wM_t = sb("wM_t", [128, KC, 576], BF16)
                gml_bc = sb("gml_bc", [128, 128], F32)
                trimask = sb("trimask", [128, 128], F32)
                gb = sb("gb", [1, 4], F32)
                onesrow = sb("onesrow", [1, 128], F32)
                u_t = sb("um_t", [128, 2, KC, 512], BF16)
                u_ring = Ring("um", 2)
                QmT = sb("QmT", [128, 512], BF16)
                KmT = sb("KmT", [128, 512], BF16)
                rw = sb("rw", [1, 14, 512], F32)
                R_IG, R_LF, R_B, R_GI, R_CM, R_M, R_NM, R_W, R_T, R_EMR, R_WS, R_Z, R_E = range(13)
                mst = sb("mst", [1, 4], F32)
                mst2 = sb("mst2", [1, 2], F32)
                MSTOP = int(os.environ.get("KMSTOP", 9))
                cols_sb = sb("cols_sb", [128, 2, 8], F32)
                cols_ring = Ring("colsb", 2)
                ET = sb("ET", [128, 128], F32)
                AT = sb("AT", [128, 128], BF16)
                kw = sb("kw", [128, 128], BF16)
                vaug = sb("vaug", [128, 2, 130], BF16)
                v_ring = Ring("vaug", 2)
                sgo = sb("sgo", [128, 2, 128], F32)
                Cst = sb("Cst", [128, 129], F32)
                Cb = sb("Cb", [128, 129], BF16)
                tmpn = sb("tmpn", [128, 129], F32)
                num = sb("num", [128, 129], F32)
                dn = sb("dn", [128, 4], F32)
                hm = sb("hm", [128, 128], F32)
                junkm = sb("junkm", [128, 128], F32)
                hout = sb("hout", [128, 2, 128], BF16)
                ho_ring = Ring("hout", 2)
                BQ, BK, BI, BF_, BT, BC, BS, BU = range(8)

                for k in range(KC):
                    load_rows(stg, stg_ring, wB_d, k * 128, 704, 576, wM_t[:, k, :], "wM")
                P.dma("sp", lambda e: [e.dma_start(out=gml_bc[:], in_=gml_bc_d), e.dma_start(out=trimask[:], in_=trimask_d),
                                       e.dma_start(out=gb[:], in_=gb_d)], "c0", writes=["mconst"], n=3)
                P.op("dve", lambda e: e.memset(onesrow[:], 1.0), writes=["onesrow"])
                P.op("dve", lambda e: e.memset(rw[:, R_Z, :], 0.0), writes=["rwz"])
                P.op("dve", lambda e: e.memset(mst[:], 0.0), writes=["mst"])
                P.op("dve", lambda e: e.memset(Cst[:], 0.0), writes=["Cst"])
                P.op("pool", lambda e: e.memset(Cb[:], 0.0), writes=["Cb"])
                P.op("pool", lambda e: e.memset(vaug[:], 1.0), writes=["vaug_init"])

                def m_chunk(tt, c, us, ukey):
                    cc = tt * 4 + c
                    cs_ = slice(c * 128, (c + 1) * 128)
                    mp, mn = cc % 2, (cc + 1) % 2
                    row = lambda r: rw[0:1, r, cs_]
                    P.op("dve", lambda e: e.tensor_tensor_scan(row(R_B), onesrow[0:1, :], row(R_LF), 0.0, ALU.mult, ALU.add), reads=["lf", "onesrow"], writes=["b"])
                    P.op("dve", lambda e: e.tensor_tensor(out=row(R_GI), in0=row(R_IG), in1=row(R_B), op=ALU.subtract), reads=["ig", "b"], writes=["gi"])
                    P.op("dve", lambda e: e.tensor_tensor_scan(row(R_CM), row(R_GI), row(R_Z), -1e30, ALU.max, ALU.add), reads=["gi", "rwz"], writes=["cm"])
                    P.op("dve", lambda e: e.tensor_scalar(out=row(R_M), in0=row(R_CM), scalar1=mst[0:1, mp:mp + 1], scalar2=None, op0=ALU.max), reads=["cm", "mst"], writes=["M"])
                    P.op("dve", lambda e: e.tensor_tensor(out=mst[0:1, mn:mn + 1], in0=rw[0:1, R_M, c * 128 + 127:c * 128 + 128], in1=rw[0:1, R_B, c * 128 + 127:c * 128 + 128], op=ALU.add),
                         reads=["M", "b", "mst"], writes=["mstn"])
                    P.op("dve", lambda e: e.tensor_scalar(out=row(R_NM), in0=row(R_M), scalar1=-1.0, scalar2=None, op0=ALU.mult), reads=["M"], writes=["nM"])
                    P.op("act", lambda e: e.activation(row(R_W), row(R_M), AF.Exp, bias=mst[0:1, mp:mp + 1], scale=-1.0), reads=["M", "mst"], writes=["w"])
                    P.op("dve", lambda e: e.tensor_tensor(out=row(R_T), in0=row(R_B), in1=row(R_M), op=ALU.add), reads=["b", "M"], writes=["t"])
                    P.op("act", lambda e: e.activation(row(R_EMR), row(R_T), AF.Exp, scale=-1.0), reads=["t"], writes=["emr"])
                    P.op("dve", lambda e: e.tensor_tensor(out=mst[0:1, 2:3], in0=rw[0:1, R_B, c * 128 + 127:c * 128 + 128], in1=mst[0:1, mn:mn + 1], op=ALU.subtract),
                         reads=["b", "mstn"], writes=["dlt"])
                    P.op("act", lambda e: e.activation(row(R_WS), row(R_GI), AF.Exp, bias=mst[0:1, 2:3]), reads=["gi", "dlt"], writes=["ws"])
                    P.op("act", lambda e: e.activation(mst[0:1, 3:4], mst[0:1, mp:mp + 1], AF.Exp, bias=mst[0:1, 2:3]), reads=["mst", "dlt"], writes=["a11"])
                    if MSTOP < 2:
                        return
                    for ci, (r, key) in enumerate([(R_GI, "gi"), (R_W, "w"), (R_EMR, "emr"), (R_WS, "ws")]):
                        P.op("pe", lambda e, ci=ci, r=r: e.matmul(pb[BC][:, 2 * ci:2 * ci + 2], row(r), onesrow[0:1, 0:2], start=True, stop=True), reads=[key, "onesrow"], writes=["BC"])
                    P.op("dve", lambda e: e.tensor_copy(mst2[0:1, 0:1], mst[0:1, 3:4]), reads=["a11"], writes=["a11b"])
                    P.op("dve", lambda e: e.tensor_copy(mst2[0:1, 1:2], mst[0:1, 3:4]), reads=["a11"], writes=["a11b"])
                    P.op("pe", lambda e: e.matmul(pb[BC][:, 8:10], onesrow[0:1, :], mst2[0:1, 0:2], start=True, stop=True), reads=["a11b", "onesrow"], writes=["BC"])
                    P.op("pe", lambda e: e.matmul(pb[BC][:, 128:256], onesrow[0:1, :], row(R_NM), start=True, stop=False), reads=["nM", "onesrow"], writes=["BC"])
                    P.op("pe", lambda e: e.matmul(pb[BC][:, 128:256], ident_f[:], trimask[:], start=False, stop=True), reads=["mconst", "ident_f"], writes=["BC"])
                    cb_, cbkey = cols_ring.next()
                    P.op("dve", lambda e: e.tensor_copy(cols_sb[:, cb_, 0:5], pb[BC][:, 0:10].rearrange("p (c two) -> p c two", two=2)[:, :, 0]), reads=["BC"], writes=[cbkey])
                    P.op("act", lambda e: e.activation(ET[:], pb[BC][:, 128:256], AF.Exp, bias=cols_sb[:, cb_, 0:1]), reads=["BC", cbkey], writes=["ET"])
                    if MSTOP < 3:
                        return
                    for k in range(KC):
                        P.op("pe", lambda e, k=k: e.matmul(pb[BT][:, 0:384], u_t[:, us, k, cs_], wM_t[:, k, 128:512], start=(k == 0), stop=(k == KC - 1)),
                             reads=[ukey, "wM"], writes=["BT"])
                    P.op("dve", lambda e: e.tensor_scalar(out=cols_sb[:, cb_, 5:6], in0=cols_sb[:, cb_, 3:4], scalar1=128.0 ** -0.5, scalar2=None, op0=ALU.mult), reads=[cbkey], writes=[cbkey])
                    P.op("dve", lambda e: e.tensor_scalar(out=kw[:], in0=pb[BT][:, 0:128], scalar1=cols_sb[:, cb_, 5:6], scalar2=None, op0=ALU.mult),
                         reads=["BT", cbkey], writes=["kw"])
                    vs, vkey = v_ring.next()
                    if MSTOP == 3 and os.environ.get("KMSUB") == "a":
                        return
                    P.op("act", lambda e: e.activation(vaug[:, vs, 0:128], pb[BT][:, 128:256], AF.Copy), reads=["BT", "vaug_init", "kw"], writes=[vkey])
                    if MSTOP == 3 and os.environ.get("KMSUB") == "b":
                        return
                    P.op("act", lambda e: e.activation(sgo[:, 0, :], pb[BT][:, 256:384], AF.Exp, scale=-1.0), reads=["BT"], writes=["sgo0"])
                    P.op("dve", lambda e: e.tensor_scalar(out=sgo[:, 0, :], in0=sgo[:, 0, :], scalar1=1.0, scalar2=None, op0=ALU.add), reads=["sgo0"], writes=["sgo0"])
                    P.op("dve", lambda e: e.reciprocal(sgo[:, 1, :], sgo[:, 0, :]), reads=["sgo0"], writes=["sgo1"])
                    if MSTOP < 4:
                        return
                    P.op("pe", lambda e: e.matmul(pb[BS][:, 0:128], KmT[:, cs_], QmT[:, cs_], start=True, stop=True), reads=["KmT", "QmT"], writes=["BS"])
                    P.op("dve", lambda e: e.tensor_tensor(out=AT[:], in0=ET[:], in1=pb[BS][:, 0:128], op=ALU.mult), reads=["ET", "BS"], writes=["AT"])
                    P.op("pe", lambda e: e.matmul(pb[BS][:, 128:257], AT[:], vaug[:, vs, 0:129], start=True, stop=True), reads=["AT", vkey], writes=["BS"])
                    P.op("pe", lambda e: e.matmul(pb[BU][:, 0:129], QmT[:, cs_], Cb[:], start=True, stop=True), reads=["QmT", "Cb"], writes=["BU"])
                    P.op("dve", lambda e: e.tensor_scalar(out=tmpn[:], in0=pb[BU][:, 0:129], scalar1=cols_sb[:, cb_, 1:2], scalar2=None, op0=ALU.mult), reads=["BU", cbkey], writes=["tmpn"])
                    P.op("dve", lambda e: e.tensor_tensor(out=num[:], in0=tmpn[:], in1=pb[BS][:, 128:257], op=ALU.add), reads=["tmpn", "BS"], writes=["num"])
                    P.op("dve", lambda e: e.tensor_scalar(out=dn[:, 1:2], in0=num[:, 128:129], scalar1=-1.0, scalar2=None, op0=ALU.mult), reads=["num"], writes=["dn1"])
                    P.op("dve", lambda e: e.tensor_tensor(out=dn[:, 0:1], in0=num[:, 128:129], in1=dn[:, 1:2], op=ALU.max), reads=["num", "dn1"], writes=["dn0"])
                    P.op("dve", lambda e: e.tensor_scalar(out=dn[:, 0:1], in0=dn[:, 0:1], scalar1=cols_sb[:, cb_, 2:3], scalar2=None, op0=ALU.max), reads=["dn0", cbkey], writes=["dn0"])
                    P.op("dve", lambda e: e.reciprocal(dn[:, 1:2], dn[:, 0:1]), reads=["dn0"], writes=["dn1"])
                    P.op("dve", lambda e: e.scalar_tensor_tensor(out=hm[:], in0=num[:, 0:128], scalar=dn[:, 1:2], in1=sgo[:, 1, :], op0=ALU.mult, op1=ALU.mult),
                         reads=["num", "dn1", "sgo1"], writes=["hm"])
                    P.op("act", lambda e: e.activation(junkm[:], hm[:], AF.Square), reads=["hm"], writes=["junkm"])
                    P.op("dve", lambda e: e.reduce_sum(out=dn[:, 2:3], in_=junkm[:], axis=AX.X), reads=["junkm"], writes=["dn2"])
                    P.op("act", lambda e: e.activation(dn[:, 3:4], dn[:, 2:3], AF.Sqrt, bias=EPS, scale=1.0 / 128), reads=["dn2"], writes=["dn3"])
                    P.op("dve", lambda e: e.reciprocal(dn[:, 2:3], dn[:, 3:4]), reads=["dn3"], writes=["dn4"])
                    hs_, hkey_ = ho_ring.next()
                    P.op("dve", lambda e: e.scalar_tensor_tensor(out=hout[:, hs_, :], in0=hm[:], scalar=dn[:, 2:3], in1=gml_bc[:], op0=ALU.mult, op1=ALU.mult),
                         reads=["hm", "dn4", "mconst"], writes=[hkey_])
                    qt, j = cc // 4, cc % 4
                    r0 = (qt // 4) * 512 + j * 128
                    P.dma("sp", lambda e: e.dma_start(out=ag2_in[qt % 4].ap().rearrange("p (t c) -> (p t) c", c=256)[r0:r0 + 128, 128:256], in_=hout[:, hs_, :]),
                          ("hout_st", hs_), reads=[hkey_], writes=["ag2_in"])
                    P.op("pe", lambda e: e.matmul(pb[BU][:, 256:385], kw[:], vaug[:, vs, 0:129], start=True, stop=True), reads=["kw", vkey], writes=["BU"])
                    P.op("dve", lambda e: e.scalar_tensor_tensor(out=Cst[:], in0=Cst[:], scalar=cols_sb[:, cb_, 4:5], in1=pb[BU][:, 256:385], op0=ALU.mult, op1=ALU.add),
                         reads=["BU", cbkey, "Cst"], writes=["Cst"])
                    P.op("act", lambda e: e.activation(Cb[:], Cst[:], AF.Copy), reads=["Cst"], writes=["Cb"])

                def m_tile(tt, us, ukey):
                    rank, c0 = tt // 4, (tt % 4) * 512
                    P.dma("sp", lambda e: [e.dma_start(
                        out=u_t[:, us, 2 * i:2 * i + 2, :], in_=ag1_out[i].ap()[rank * 128:(rank + 1) * 128, :].rearrange("p (k n) -> p k n", k=2)[:, :, c0:c0 + 512])
                        for i in range(4)], ("um_ld", us), reads=[("ag1_out", i) for i in range(4)], writes=[ukey], n=4)
                    for (bank, c0w, dst, key, scl) in [(BQ, 0, QmT, "QmT", 1.0), (BK, 128, KmT, "KmT", 128.0 ** -0.5)]:
                        for k in range(KC):
                            P.op("pe", lambda e, k=k, bank=bank, c0w=c0w: e.matmul(pb[bank][:, :], wM_t[:, k, c0w:c0w + 128], u_t[:, us, k, :], start=(k == 0), stop=(k == KC - 1)),
                                 reads=[ukey, "wM"], writes=[("bank", bank)])
                        P.op("act", lambda e, bank=bank, dst=dst, scl=scl: e.activation(dst[:], pb[bank][:, :], AF.Copy, scale=scl), reads=[("bank", bank)], writes=[key])
                    for (bank, col) in [(BI, 512), (BF_, 513)]:
                        for k in range(KC):
                            P.op("pe", lambda e, k=k, bank=bank, col=col: e.matmul(pb[bank][0:1, :], wM_t[:, k, col:col + 1], u_t[:, us, k, :], start=(k == 0), stop=(k == KC - 1)),
                                 reads=[ukey, "wM"], writes=[("bank", bank)])
                    P.op("dve", lambda e: e.tensor_scalar(out=rw[0:1, R_IG, :], in0=pb[BI][0:1, :], scalar1=gb[0:1, 0:1], scalar2=None, op0=ALU.add), reads=[("bank", BI), "mconst"], writes=["ig"])
                    P.op("act", lambda e: e.activation(rw[0:1, R_E, :], pb[BF_][0:1, :], AF.Exp, bias=gb[0:1, 2:3], scale=-1.0), reads=[("bank", BF_), "mconst"], writes=["rwe"])
                    P.op("act", lambda e: e.activation(rw[0:1, R_E, :], rw[0:1, R_E, :], AF.Ln, bias=1.0), reads=["rwe"], writes=["rwe"])
                    P.op("dve", lambda e: e.tensor_scalar(out=rw[0:1, R_LF, :], in0=rw[0:1, R_E, :], scalar1=-1.0, scalar2=None, op0=ALU.mult), reads=["rwe"], writes=["lf"])
                    for c in range(4):
                        m_chunk(tt, c, us, ukey)

                nmt = int(os.environ.get("KNMT", 16))
                for tt in range(nmt):
                    us, ukey = u_ring.next()
                    m_tile(tt, us, ukey)
                mfin = (nmt * 4) % 2
                P.dma("sp", lambda e: [e.dma_start(out=C_p_out, in_=Cst[:, 0:128]), e.dma_start(out=n_p_out, in_=Cst[:, 128:129]),
                                       e.dma_start(out=m_p_out, in_=mst[0:1, mfin:mfin + 1])], "mout", reads=["Cst", "mstn"], n=3)

        if "M" in stages:
            phase_M()

        def phase_S1():
            ph, pb = new_phase(6)
            with ph:
                stg = sb("stgs", [128, 2, 1408], F32)
                stg_ring = Ring("stg", 2)
                wIn_t = sb("wIn_t", [128, KC, IN_COLS], BF16)
                us_t = sb("us_t", [128, KC, TS], BF16)
                z_sb = sb("z_sb", [TS, IN_COLS], F32)
                Cpm = sb("Cpm", [64, 128, 128], F32)
                pmq = sb("pmq", [64, 4, 128], F32)
                pmg = sb("pmg", [64, 16], F32)
                npm = sb("npm", [64, 128], F32)
                gbs = sb("gbs", [64, 4], F32)
                gml_pm = sb("gml_pm", [64, 128], F32)
                ks = sb("ks", [64, 128], F32)
                kwv = sb("kwv", [64, 128], F32)
                qCacc = sb("qCacc", [64, 128], F32)
                tv = sb("tv", [64, 2, 128], F32)
                jk = sb("jk", [64, 128], F32)
                hs_ = sb("hs_", [64, 128], F32)
                hob = sb("hob", [64, 128], BF16)
                for half in range(2):
                    c0 = half * 1364
                    for k in range(KC):
                        load_rows(stg, stg_ring, w_in_d, k * 128, c0, 1364, wIn_t[:, k, c0:c0 + 1364], "wIn")
                P.dma("sp", lambda e: [e.dma_start(out=us_t[:], in_=us_d.ap().rearrange("p (k n) -> p k n", k=KC)),
                                       e.dma_start(out=Cpm[:], in_=stC_d.rearrange("p (d e) -> p d e", d=128)),
                                       e.dma_start(out=npm[:], in_=stn_d), e.dma_start(out=pmg[:, 0:1], in_=stm_d),
                                       e.dma_start(out=gbs[:], in_=gbs_d), e.dma_start(out=gml_pm[:], in_=gml_pm_d)],
                      "s1ld", reads=["us_d"], writes=["us_t", "Cpm", "pmconst"], n=6)
                col = 0
                gi_ = 0
                while col < IN_COLS:
                    w_ = min(512, IN_COLS - col)
                    a = gi_ % 4
                    for k in range(KC):
                        P.op("pe", lambda e, k=k, a=a, col=col, w_=w_: e.matmul(pb[a][0:TS, 0:w_], us_t[:, k, :], wIn_t[:, k, col:col + w_], start=(k == 0), stop=(k == KC - 1)),
                             reads=["us_t", "wIn"], writes=[("acc", a)])
                    P.op("act", lambda e, a=a, col=col, w_=w_: e.activation(z_sb[:, col:col + w_], pb[a][0:TS, 0:w_], AF.Copy), reads=[("acc", a)], writes=["z_sb"])
                    col += w_
                    gi_ += 1
                P.dma("sp", lambda e: e.dma_start(out=zs_d.ap(), in_=z_sb[:]), "zs_st", reads=["z_sb"], writes=["zs_d"])
                def pm_load(e):
                    outs = []
                    for h in range(4):
                        for qi, off in enumerate([OFF_MQ, OFF_MK, OFF_MV, OFF_MO]):
                            outs.append(e.dma_start(out=pmq[h * 16:(h + 1) * 16, qi, :], in_=zs_d.ap()[:, off + h * 128:off + (h + 1) * 128]))
                        outs.append(e.dma_start(out=pmg[h * 16:(h + 1) * 16, 1:2], in_=zs_d.ap()[:, OFF_MI + h:OFF_MI + h + 1], allow_slow_non_contiguous=True))
                        outs.append(e.dma_start(out=pmg[h * 16:(h + 1) * 16, 2:3], in_=zs_d.ap()[:, OFF_MF + h:OFF_MF + h + 1], allow_slow_non_contiguous=True))
                    return outs
                P.dma("sp", pm_load, "pmld", reads=["zs_d"], writes=["pm"], n=24)
                G = lambda c: pmg[:, c:c + 1]
                P.op("dve", lambda e: e.tensor_tensor(out=G(3), in0=G(1), in1=gbs[:, 0:1], op=ALU.add), reads=["pm", "pmconst"], writes=["g3"])
                P.op("act", lambda e: e.activation(G(15), G(2), AF.Exp, bias=gbs[:, 2:3], scale=-1.0), reads=["pm", "pmconst"], writes=["g15"])
                P.op("act", lambda e: e.activation(G(15), G(15), AF.Ln, bias=1.0), reads=["g15"], writes=["g15"])
                P.op("dve", lambda e: e.tensor_scalar(out=G(4), in0=G(15), scalar1=-1.0, scalar2=None, op0=ALU.mult), reads=["g15"], writes=["g4"])
                P.op("dve", lambda e: e.tensor_tensor(out=G(5), in0=G(4), in1=G(0), op=ALU.add), reads=["g4", "pmconst"], writes=["g5"])
                P.op("dve", lambda e: e.tensor_tensor(out=G(6), in0=G(5), in1=G(3), op=ALU.max), reads=["g5", "g3"], writes=["g6"])
                P.op("dve", lambda e: e.tensor_tensor(out=G(15), in0=G(5), in1=G(6), op=ALU.subtract), reads=["g5", "g6", "g15"], writes=["g15b"])
                P.op("act", lambda e: e.activation(G(7), G(15), AF.Exp), reads=["g15b"], writes=["g7"])
                P.op("dve", lambda e: e.tensor_tensor(out=G(15), in0=G(3), in1=G(6), op=ALU.subtract), reads=["g3", "g6", "g7"], writes=["g15c"])
                P.op("act", lambda e: e.activation(G(8), G(15), AF.Exp), reads=["g15c"], writes=["g8"])
                P.op("act", lambda e: e.activation(G(9), G(6), AF.Exp, scale=-1.0), reads=["g6"], writes=["g9"])
                P.op("dve", lambda e: e.tensor_scalar(out=ks[:], in0=pmq[:, 1, :], scalar1=128.0 ** -0.5, scalar2=None, op0=ALU.mult), reads=["pm"], writes=["ks"])
                P.op("dve", lambda e: e.tensor_tensor(out=jk[:], in0=pmq[:, 0, :], in1=ks[:], op=ALU.mult), reads=["pm", "ks"], writes=["jk"])
                P.op("dve", lambda e: e.reduce_sum(out=G(10), in_=jk[:], axis=AX.X), reads=["jk"], writes=["g10"])
                P.op("dve", lambda e: e.tensor_tensor(out=jk[:], in0=pmq[:, 0, :], in1=npm[:], op=ALU.mult), reads=["pm", "Cpm", "g10"], writes=["jk"])
                P.op("dve", lambda e: e.reduce_sum(out=G(11), in_=jk[:], axis=AX.X), reads=["jk"], writes=["g11"])
                P.op("dve", lambda e: e.tensor_tensor(out=G(12), in0=G(10), in1=G(8), op=ALU.mult), reads=["g10", "g8"], writes=["g12"])
                P.op("dve", lambda e: e.tensor_scalar(out=kwv[:], in0=ks[:], scalar1=G(8), scalar2=None, op0=ALU.mult), reads=["ks", "g8"], writes=["kwv"])
                P.op("dve", lambda e: e.memset(qCacc[:], 0.0), writes=["qCacc"])
                for d in range(128):
                    P.op("dve", lambda e, d=d: e.scalar_tensor_tensor(out=qCacc[:], in0=Cpm[:, d, :], scalar=pmq[:, 0, d:d + 1], in1=qCacc[:], op0=ALU.mult, op1=ALU.add),
                         reads=["Cpm", "pm", "qCacc"], writes=["qCacc"])
                    P.op("dve", lambda e, d=d: e.tensor_scalar(out=tv[:, d % 2, :], in0=pmq[:, 2, :], scalar1=kwv[:, d:d + 1], scalar2=None, op0=ALU.mult),
                         reads=["pm", "kwv"], writes=[("tv", d % 2)])
                    P.op("dve", lambda e, d=d: e.scalar_tensor_tensor(out=Cpm[:, d, :], in0=Cpm[:, d, :], scalar=G(7), in1=tv[:, d % 2, :], op0=ALU.mult, op1=ALU.add),
                         reads=[("tv", d % 2), "g7", "qCacc"], writes=["Cpm"])
                P.dma("sp", lambda e: e.dma_start(out=C_s_out.rearrange("p (d e) -> p d e", d=128), in_=Cpm[:]), "cs_st", reads=["Cpm"])
                P.op("dve", lambda e: e.scalar_tensor_tensor(out=jk[:], in0=npm[:], scalar=G(7), in1=kwv[:], op0=ALU.mult, op1=ALU.add), reads=["g7", "kwv", "g11"], writes=["jk2"])
                P.dma("sp", lambda e: [e.dma_start(out=n_s_out, in_=jk[:]), e.dma_start(out=m_s_out, in_=pmg[:, 6:7])], "ns_st", reads=["jk2", "g6"], n=2)
                P.op("dve", lambda e: e.tensor_scalar(out=hs_[:], in0=qCacc[:], scalar1=G(7), scalar2=None, op0=ALU.mult), reads=["qCacc", "g7"], writes=["hs"])
                P.op("dve", lambda e: e.scalar_tensor_tensor(out=hs_[:], in0=pmq[:, 2, :], scalar=G(12), in1=hs_[:], op0=ALU.mult, op1=ALU.add), reads=["hs", "g12", "pm"], writes=["hs"])
                P.op("dve", lambda e: e.scalar_tensor_tensor(out=G(13), in0=G(11), scalar=G(7), in1=G(12), op0=ALU.mult, op1=ALU.add), reads=["g11", "g7", "g12"], writes=["g13"])
                P.op("dve", lambda e: e.tensor_scalar(out=G(15), in0=G(13), scalar1=-1.0, scalar2=None, op0=ALU.mult), reads=["g13", "g8"], writes=["g15d"])
                P.op("dve", lambda e: e.tensor_tensor(out=G(13), in0=G(13), in1=G(15), op=ALU.max), reads=["g15d"], writes=["g13b"])
                P.op("dve", lambda e: e.tensor_tensor(out=G(13), in0=G(13), in1=G(9), op=ALU.max), reads=["g13b", "g9"], writes=["g13c"])
                P.op("dve", lambda e: e.reciprocal(G(14), G(13)), reads=["g13c"], writes=["g14"])
                P.op("act", lambda e: e.activation(jk[:], pmq[:, 3, :], AF.Exp, scale=-1.0), reads=["pm", "jk2"], writes=["jk2", "jk3"])
                P.op("dve", lambda e: e.tensor_scalar(out=jk[:], in0=jk[:], scalar1=1.0, scalar2=None, op0=ALU.add), reads=["jk3"], writes=["jk3"])
                P.op("dve", lambda e: e.reciprocal(jk[:], jk[:]), reads=["jk3"], writes=["jk3"])
                P.op("dve", lambda e: e.scalar_tensor_tensor(out=hs_[:], in0=hs_[:], scalar=G(14), in1=jk[:], op0=ALU.mult, op1=ALU.mult), reads=["hs", "g14", "jk3"], writes=["hs2"])
                P.op("act", lambda e: e.activation(jk[:], hs_[:], AF.Square), reads=["hs2"], writes=["jk4"])
                P.op("dve", lambda e: e.reduce_sum(out=G(15), in_=jk[:], axis=AX.X), reads=["jk4"], writes=["g15e"])
                P.op("act", lambda e: e.activation(G(13), G(15), AF.Sqrt, bias=EPS, scale=1.0 / 128), reads=["g15e"], writes=["g13d"])
                P.op("dve", lambda e: e.reciprocal(G(14), G(13)), reads=["g13d"], writes=["g14b"])
                P.op("dve", lambda e: e.scalar_tensor_tensor(out=hob[:], in0=hs_[:], scalar=G(14), in1=gml_pm[:], op0=ALU.mult, op1=ALU.mult), reads=["hs2", "g14b", "pmconst"], writes=["hob"])
                P.dma("sp", lambda e: [e.dma_start(out=mixs_d.ap().rearrange("p (r c) -> p r c", r=4)[:, h, 128:256], in_=hob[h * 16:(h + 1) * 16, :]) for h in range(4)],
                      "hob_st", reads=["hob"], writes=["mixs_hm"], n=4)

        if "S" in stages:
            phase_S1()

        def phase_S2():
            ph, pb = new_phase(6)
            with ph:
                pT_b = ph.enter_context(nc.psum_tensor("pT_b", [128, 1024], BF16))
                pTr_b = ph.enter_context(nc.psum_tensor("pTr_b", [128, 1024], BF16))
                C0, D0, E_, F_, X0, X1 = range(6)
                stg = sb("stg2", [128, 2, 1408], F32)
                stg_ring = Ring("stg", 2)
                wuq_t = sb("wuq_t", [128, 3, 768], BF16)
                wukT_t = sb("wukT_t", [128, 8, 256], BF16)
                wuv_t = sb("wuv_t", [128, 2, 512], BF16)
                zt = sb("zt", [TS, 672], F32)
                gq_bc = sb("gq_bc", [TS, 384], F32)
                gkv16 = sb("gkv16", [TS, 256], F32)
                c8 = sb("c8", [TS, 2, 8, 32], F32)
                iota_p = sb("iota_p", [128, 1], F32)
                pt_i = sb("pt_i", [128, 1024], I32)
                pt_f = sb("pt_f", [128, 1024], F32)
                idx_i = sb("idx_i", [128, 1024], I32)
                jk = sb("jk2", [TS, 768], F32)
                sts = sb("sts", [128, 8], F32)
                qlat_b = sb("qlat_b", [TS, 384], BF16)
                qlatT = sb("qlatT", [128, 3, TS], BF16)
                q_sb = sb("q_sb", [TS, 8, 96], F32)
                qsw = sb("qsw", [TS, 8, 32], F32)
                qn_b = sb("qn_b", [TS, 8, 64], BF16)
                qr_b = sb("qr_b", [TS, 8, 32], BF16)
                qnT = sb("qnT", [128, 4, TS], BF16)
                QAT = sb("QAT", [128, 2, TS, 8], BF16)
                QRT = sb("QRT", [32, TS, 8], BF16)
                kvs = sb("kvs", [TS, 288], F32)
                ksw = sb("ksw", [TS, 32], F32)
                Gnew = sb("Gnew", [128, TS, 288], F32)
                ones_f = sb("ones_f", [128, 128], F32)
                negrow = sb("negrow", [1, 128], F32)
                G = sb("Gst", [128, 3, 4, 288], F32)
                g_ring = Ring("G", 3)
                Gb = sb("Gb", [128, 2, 65, 288], BF16)
                KT = sb("KTs", [128, 2, 512], BF16)
                KTr = sb("KTrs", [32, 512], BF16)
                PT = sb("PTs", [128, 65, 8], BF16)
                mx1 = sb("mx1", [128, 8], F32)
                mxh = sb("mxh", [8, 2], F32)
                diag = sb("diag", [8, 8], F32)
                negmx = sb("negmx", [128, 8], F32)
                t8 = sb("t8", [128, 8], F32)
                pr = sb("pr", [128, 8], F32)
                rden = sb("rden", [128, 8], F32)
                OLT = sb("OLT", [128, 2, 8, TS], BF16)
                amix = sb("amix", [TS, 4, 128], BF16)

                for c in range(3):
                    load_rows(stg, stg_ring, w_uq_d, c * 128, 0, 768, wuq_t[:, c, :], "wuq")
                for i in range(8):
                    load_rows(stg, stg_ring, wukT_d, 0, i * 256, 256, wukT_t[:, i, :], "wukT")
                for c in range(2):
                    load_rows(stg, stg_ring, w_uv_d, c * 128, 0, 512, wuv_t[:, c, :], "wuv")
                P.dma("sp", lambda e: [e.dma_start(out=zt[:], in_=zs_d.ap()[:, 0:672]), e.dma_start(out=gq_bc[:], in_=gq_bc_d), e.dma_start(out=gkv16[:], in_=gkv_bc_d[0:TS, :]),
                                       e.dma_start(out=c8[:], in_=c8_d), e.dma_start(out=iota_p[:], in_=iota_d), e.dma_start(out=negrow[:], in_=negrow_d),
                                       e.dma_start(out=pt_i[:], in_=pt_d.partition_broadcast(128))],
                      "s2ld", reads=["zs_d"], writes=["s2c"], n=7)
                P.op("dve", lambda e: e.memset(ones_f[:], 1.0), writes=["ones_f"])
                P.op("pool", lambda e: e.memset(Gnew[:], 0.0), writes=["Gnew"])
                P.op("pool", lambda e: e.memset(OLT[:], 0.0), writes=["OLT"])
                P.op("dve", lambda e: e.tensor_copy(pt_f[:], pt_i[:]), reads=["s2c"], writes=["pt_f"])
                P.op("dve", lambda e: e.tensor_scalar(out=pt_f[:], in0=pt_f[:], scalar1=128.0, scalar2=iota_p[:, 0:1], op0=ALU.mult, op1=ALU.add), reads=["pt_f", "s2c"], writes=["pt_f2"])
                P.op("dve", lambda e: e.tensor_copy(idx_i[:], pt_f[:]), reads=["pt_f2"], writes=["idx_i"])

                S2STOP = int(os.environ.get("KS2STOP", 9))

                def rms_tok(src_ap, dim, gain_ap, dst_ap, tag):
                    P.op("act", lambda e: e.activation(jk[:, 0:dim], src_ap, AF.Square), reads=["s2c"], writes=["jk"])
                    P.op("dve", lambda e: e.reduce_sum(out=sts[0:TS, 0:1], in_=jk[:, 0:dim], axis=AX.X), reads=["jk"], writes=["sts0"])
                    P.op("act", lambda e: e.activation(sts[0:TS, 1:2], sts[0:TS, 0:1], AF.Sqrt, bias=EPS, scale=1.0 / dim), reads=["sts0"], writes=["sts1"])
                    P.op("dve", lambda e: e.reciprocal(sts[0:TS, 2:3], sts[0:TS, 1:2]), reads=["sts1"], writes=["sts2"])
                    P.op("dve", lambda e: e.scalar_tensor_tensor(out=dst_ap, in0=src_ap, scalar=sts[0:TS, 2:3], in1=gain_ap, op0=ALU.mult, op1=ALU.mult), reads=["sts2", "s2c"], writes=[tag])
                rms_tok(zt[:, 384:640], 256, gkv16[:], kvs[:, 0:256], "kvs")
                P.op("dve", lambda e: e.tensor_copy(ksw[:, 0:16], zt[:, 656:672]), reads=["s2c"], writes=["ksw"])
                P.op("dve", lambda e: e.tensor_copy(ksw[:, 16:32], zt[:, 640:656]), reads=["s2c"], writes=["ksw"])
                P.op("dve", lambda e: e.tensor_tensor(out=ksw[:], in0=ksw[:], in1=c8[:, 1, 0, :], op=ALU.mult), reads=["ksw", "s2c"], writes=["ksw"])
                P.op("dve", lambda e: e.tensor_tensor(out=kvs[:, 256:288], in0=zt[:, 640:672], in1=c8[:, 0, 0, :], op=ALU.mult), reads=["s2c", "kvs"], writes=["kvs"])
                P.op("dve", lambda e: e.tensor_tensor(out=kvs[:, 256:288], in0=kvs[:, 256:288], in1=ksw[:], op=ALU.add), reads=["kvs", "ksw"], writes=["kvs"])
                P.dma("sp", lambda e: [e.dma_start(out=ckv_s_out, in_=kvs[:, 0:256]), e.dma_start(out=kr_s_out, in_=kvs[:, 256:288]),
                                       e.dma_start(out=Gnew[0:1, :, :], in_=kvs[:, :])], "kvs_st", reads=["kvs", "Gnew"], writes=["Gnew2"], n=3)
                if S2STOP < 2:
                    return
                rms_tok(zt[:, 0:384], 384, gq_bc[:], qlat_b[:], "qlat_b")
                for c in range(3):
                    P.op("pe", lambda e, c=c: e.transpose(pT_b[:, c * TS:(c + 1) * TS], qlat_b[:, c * 128:(c + 1) * 128], ident_bf[0:TS, 0:TS]), reads=["qlat_b", "ident_bf"], writes=["PSA"])
                P.op("act", lambda e: e.activation(qlatT[:], pT_b[:, 0:3 * TS].rearrange("p (c n) -> p c n", c=3), AF.Copy), reads=["PSA"], writes=["qlatT"])
                for hf in range(2):
                    for c in range(3):
                        P.op("pe", lambda e, c=c, hf=hf: e.matmul(pb[X0 + hf][0:TS, 0:384], qlatT[:, c, :], wuq_t[:, c, hf * 384:(hf + 1) * 384], start=(c == 0), stop=(c == 2)),
                             reads=["qlatT", "wuq"], writes=[("acc", X0 + hf)])
                    P.op("act", lambda e, hf=hf: e.activation(q_sb[:, hf * 4:(hf + 1) * 4, :], pb[X0 + hf][0:TS, 0:384].rearrange("p (h d) -> p h d", h=4), AF.Copy),
                         reads=[("acc", X0 + hf)], writes=["q_sb"])
                S2SUB = os.environ.get("KS2SUB", "z")
                if S2SUB < "b":
                    return
                P.op("dve", lambda e: e.tensor_copy(qsw[:, :, 0:16], q_sb[:, :, 80:96]), reads=["q_sb"], writes=["qsw"])
                P.op("dve", lambda e: e.tensor_copy(qsw[:, :, 16:32], q_sb[:, :, 64:80]), reads=["q_sb"], writes=["qsw"])
                P.op("dve", lambda e: e.tensor_tensor(out=qsw[:], in0=qsw[:], in1=c8[:, 1, :, :], op=ALU.mult), reads=["qsw", "s2c"], writes=["qsw"])
                P.op("dve", lambda e: e.tensor_tensor(out=q_sb[:, :, 64:96], in0=q_sb[:, :, 64:96], in1=c8[:, 0, :, :], op=ALU.mult), reads=["q_sb", "s2c", "qsw"], writes=["q_sb"])
                P.op("dve", lambda e: e.tensor_tensor(out=qr_b[:], in0=q_sb[:, :, 64:96], in1=qsw[:], op=ALU.add), reads=["q_sb", "qsw"], writes=["qr_b"])
                P.op("dve", lambda e: e.tensor_copy(qn_b[:], q_sb[:, :, 0:64]), reads=["q_sb"], writes=["qn_b"])
                if S2SUB < "c":
                    return
                for i in range(4):
                    P.op("pe", lambda e, i=i: e.transpose(pT_b[:, 64 + i * TS:64 + (i + 1) * TS], qn_b[:, 2 * i:2 * i + 2, :], ident_bf[0:TS, 0:TS]), reads=["qn_b", "ident_bf"], writes=["PSA"])
                P.op("act", lambda e: e.activation(qnT[:], pT_b[:, 64:64 + 4 * TS].rearrange("p (i n) -> p i n", i=4), AF.Copy), reads=["PSA"], writes=["qnT"])
                if S2SUB < "d":
                    return
                for h in range(8):
                    P.op("pe", lambda e, h=h: e.transpose(pTr_b[0:32, h * TS:(h + 1) * TS], qr_b[:, h, :], ident_bf[0:TS, 0:TS]), reads=["qr_b", "ident_bf"], writes=["PSB"])
                P.op("act", lambda e: e.activation(QRT[:], pTr_b[0:32, 0:8 * TS].rearrange("p (h b) -> p b h", h=8), AF.Copy), reads=["PSB"], writes=["QRT"])
                if S2SUB < "e":
                    return
                for cc in range(2):
                    for h in range(8):
                        i, e_ = h // 2, h % 2
                        P.op("pe", lambda e, cc=cc, h=h, i=i, e_=e_: e.matmul(pb[X0 + cc][:, h * TS:(h + 1) * TS], wukT_t[:, h, cc * 128:(cc + 1) * 128],
                                                                            qnT[:, i, :], start=True, stop=True),
                             reads=["qnT", "wukT"], writes=[("acc", X0 + cc)])
                    P.op("act", lambda e, cc=cc: e.activation(QAT[:, cc, :, :], pb[X0 + cc][:, 0:8 * TS].rearrange("p (h b) -> p b h", h=8), AF.Copy), reads=[("acc", X0 + cc)], writes=["QAT"])

                if S2STOP < 3:
                    return

                def do_pages(b, bs, pages, slot_fn, sc_bank, sc_col0, sckey, mask_new):
                    npg = len(pages)
                    if S2STOP < 4:
                        return
                    for cc in range(2):
                        for jj, pg in enumerate(pages):
                            P.op("pe", lambda e, cc=cc, jj=jj, pg=pg: e.transpose(pT_b[:, cc * 512 + jj * 128:cc * 512 + (jj + 1) * 128], Gb[:, bs, pg, cc * 128:(cc + 1) * 128], ident_bf[:]),
                                 reads=[("Gb", bs), "ident_bf"], writes=["PSA"])
                    for jj, pg in enumerate(pages):
                        P.op("pe", lambda e, jj=jj, pg=pg: e.transpose(pTr_b[0:32, jj * 128:(jj + 1) * 128], Gb[:, bs, pg, 256:288], ident_bf[:]), reads=[("Gb", bs), "ident_bf"], writes=["PSB"])
                    P.op("act", lambda e: e.activation(KT[:, :, 0:npg * 128], pT_b[:, :].rearrange("p (c n) -> p c n", c=2)[:, :, 0:npg * 128], AF.Copy), reads=["PSA"], writes=["KT"])
                    P.op("dve", lambda e: e.tensor_copy(KTr[:, 0:npg * 128], pTr_b[0:32, 0:npg * 128]), reads=["PSB"], writes=["KTr"])
                    for jj, pg in enumerate(pages):
                        cols = slice(sc_col0 + jj * 8, sc_col0 + jj * 8 + 8)
                        P.op("pe", lambda e, jj=jj, cols=cols: e.matmul(pb[sc_bank][:, cols], KT[:, 0, jj * 128:(jj + 1) * 128], QAT[:, 0, b, :], start=True, stop=False), reads=["KT", "QAT"], writes=[sckey])
                        P.op("pe", lambda e, jj=jj, cols=cols: e.matmul(pb[sc_bank][:, cols], KT[:, 1, jj * 128:(jj + 1) * 128], QAT[:, 1, b, :], start=False, stop=False), reads=["KT", "QAT"], writes=[sckey])
                        P.op("pe", lambda e, jj=jj, cols=cols: e.matmul(pb[sc_bank][:, cols], KTr[:, jj * 128:(jj + 1) * 128], QRT[:, b, :], start=False, stop=(not mask_new)), reads=["KTr", "QRT"], writes=[sckey])
                        if mask_new:
                            P.op("pe", lambda e, cols=cols: e.matmul(pb[sc_bank][:, cols], negrow[0:1, :], ones_f[0:1, 0:8], start=False, stop=True), reads=["s2c", "ones_f"], writes=[sckey])

                def sample(b):
                    bs = b % 2
                    def consume(g, gs, gkey):
                        P.op("pool", lambda e: e.tensor_copy(Gb[:, bs, g * 4:(g + 1) * 4, :], G[:, gs, :, :]), reads=[gkey], writes=[("Gb", bs)])
                        do_pages(b, bs, [g * 4 + jj for jj in range(4)], None, C0, g * 32, "scC", False)
                    prev = None
                    for g in range(16):
                        gs, gkey = g_ring.next()
                        P.dma("pool", lambda e, gs=gs, g=g: [e.indirect_dma_start(out=G[:, gs, jj, :], out_offset=None, in_=cache_d,
                                                                                 in_offset=bass.IndirectOffsetOnAxis(ap=idx_i[:, b * 64 + g * 4 + jj:b * 64 + g * 4 + jj + 1], axis=0))
                                                            for jj in range(4)], ("G_ld", gs), reads=["idx_i"], writes=[gkey], n=4)
                        if prev is not None:
                            consume(*prev)
                        prev = (g, gs, gkey)
                    consume(*prev)
                    P.op("pool", lambda e: e.tensor_copy(Gb[:, bs, 64, :], Gnew[:, b, :]), reads=["Gnew2"], writes=[("Gb", bs)])
                    do_pages(b, bs, [64], None, D0, 0, "scD", True)
                    if S2STOP < 5:
                        return
                    P.op("dve", lambda e: e.tensor_reduce(out=mx1[:], in_=pb[C0][:, :].rearrange("p (j h) -> p h j", h=8), axis=AX.X, op=ALU.max), reads=["scC"], writes=["mx1"])
                    P.op("dve", lambda e: e.tensor_tensor(out=mx1[:], in0=mx1[:], in1=pb[D0][:, 0:8], op=ALU.max), reads=["mx1", "scD"], writes=["mx1"])
                    P.op("pe", lambda e: e.transpose(pb[F_][0:8, 0:128], mx1[:], ident_f[:]), reads=["mx1", "ident_f"], writes=["PSF"])
                    P.op("dve", lambda e: e.reduce_max(out=mxh[:, 0:1], in_=pb[F_][0:8, 0:128], axis=AX.X), reads=["PSF"], writes=["mxh"])
                    P.op("dve", lambda e: e.tensor_scalar(out=diag[:], in0=ident_f[0:8, 0:8], scalar1=mxh[:, 0:1], scalar2=None, op0=ALU.mult), reads=["mxh", "ident_f"], writes=["diag"])
                    P.op("pe", lambda e: e.matmul(pb[F_][:, 128:136], ones_f[0:8, :], diag[:], start=True, stop=True), reads=["diag", "ones_f"], writes=["PSF"])
                    P.op("dve", lambda e: e.tensor_scalar(out=negmx[:], in0=pb[F_][:, 128:136], scalar1=-SCALE, scalar2=None, op0=ALU.mult), reads=["PSF"], writes=["negmx"])
                    for h in range(8):
                        P.op("act", lambda e, h=h: e.activation(PT[:, 0:64, h], pb[C0][:, :].rearrange("p (j h) -> p j h", h=8)[:, :, h], AF.Exp, bias=negmx[:, h:h + 1], scale=SCALE),
                             reads=["scC", "negmx"], writes=["PT"])
                    P.op("dve", lambda e: e.tensor_scalar(out=t8[:], in0=pb[D0][:, 0:8], scalar1=SCALE, scalar2=None, op0=ALU.mult), reads=["scD"], writes=["t8"])
                    P.op("dve", lambda e: e.tensor_tensor(out=t8[:], in0=t8[:], in1=negmx[:], op=ALU.add), reads=["t8", "negmx"], writes=["t8"])
                    P.op("act", lambda e: e.activation(PT[:, 64, :], t8[:], AF.Exp), reads=["t8"], writes=["PT"])
                    P.op("dve", lambda e: e.tensor_reduce(out=pr[:], in_=PT[:, :, :].rearrange("p j h -> p h j"), axis=AX.X, op=ALU.add), reads=["PT"], writes=["pr"])
                    P.op("pe", lambda e: e.matmul(pb[E_][:, 16:24], ones_f[:], pr[:], start=True, stop=True), reads=["pr", "ones_f"], writes=["PSE"])
                    for cc in range(2):
                        for pg in range(65):
                            P.op("pe", lambda e, cc=cc, pg=pg: e.matmul(pb[E_][:, cc * 8:(cc + 1) * 8], Gb[:, bs, pg, cc * 128:(cc + 1) * 128], PT[:, pg, :], start=(pg == 0), stop=(pg == 64)),
                                 reads=["PT", ("Gb", bs)], writes=["PSE"])
                    P.op("dve", lambda e: e.reciprocal(rden[:], pb[E_][:, 16:24]), reads=["PSE"], writes=["rden"])
                    for cc in range(2):
                        P.op("dve", lambda e, cc=cc: e.tensor_tensor(out=OLT[:, cc, :, b], in0=pb[E_][:, cc * 8:(cc + 1) * 8], in1=rden[:], op=ALU.mult), reads=["PSE", "rden"], writes=["OLT"])

                nsmp = int(os.environ.get("KNS", TS))
                for b in range(nsmp):
                    sample(b)
                if S2STOP < 6:
                    return
                for h in range(8):
                    for cc in range(2):
                        P.op("pe", lambda e, h=h, cc=cc: e.matmul(pb[X0][0:TS, h * 64:(h + 1) * 64], OLT[:, cc, h, :], wuv_t[:, cc, h * 64:(h + 1) * 64], start=(cc == 0), stop=(cc == 1)),
                             reads=["OLT", "wuv"], writes=[("acc", X0)])
                P.op("act", lambda e: e.activation(amix[:], pb[X0][0:TS, 0:512].rearrange("p (r c) -> p r c", r=4), AF.Copy), reads=[("acc", X0)], writes=["amix"])
                P.dma("sp", lambda e: e.dma_start(out=mixs_d.ap().rearrange("p (r c) -> p r c", r=4)[:, :, 0:128], in_=amix[:]), "amix_st", reads=["amix"], writes=["mixs_a"])

        if "T" in stages:
            phase_S2()

        rank_cache = {}

        def phase_C1():
            P.barrier()
            for i in range(4):
                P.dma("pool", lambda e, i=i: e.collective_compute("AllGather", ALU.bypass, replica_groups=GROUPS,
                                                                  ins=[ag2_in[i].ap()], outs=[ag2_out[i].ap()]),
                      ("cc2", i), reads=["ag2_in"], writes=[("ag2_out", i)], inc=1)
            def grab(e):
                rank = e.partition_id() % 4
                return [e.dma_start(out=mixloc[i].ap(), in_=ag2_out[i].ap().rearrange("r (t c) -> (r t) c", c=256).rearrange("(rr q) c -> q rr c", rr=4)[bass.ds(rank * 512, 512), :, :])
                        for i in range(4)]
            P.dma("pool", grab, "grab", reads=[("ag2_out", i) for i in range(4)], writes=[("mixloc", i) for i in range(4)], n=4)
            ph, pb = new_phase(6)
            with ph:
                pbT = ph.enter_context(nc.psum_tensor("pbT_c1", [128, 1024], BF16))
                stg = sb("stgc", [128, 2, 1408], F32)
                stg_ring = Ring("stg", 2)
                wout_t = sb("wout_t", [128, KC, D], BF16)
                gattn_bc = sb("gattn_bc", [128, 512], F32)
                NT = 256
                h_t = sb("hc_t", [128, 2, KC, NT], F32)
                h_ring = Ring("hc", 2)
                mt = sb("mt", [128, 2, 4, 256], BF16)
                mt_ring = Ring("mt", 2)
                junk = sb("junkc", [128, 512], F32)
                stc = sb("stc", [128, 4], F32)
                mixn = sb("mixn", [128, D], BF16)
                mixT = sb("mixT", [128, KC, NT], BF16)
                acc_ring = Ring("acc", 4)
                for k in range(KC):
                    load_rows(stg, stg_ring, wout_d, k * 128, 0, 1024, wout_t[:, k, :], "wout")
                P.dma("sp", lambda e: e.dma_start(out=gattn_bc[:], in_=gattn_bc_d), "c0", writes=["gattn_bc"])

                def mix_block(np_, src_fn, src_reads, col0):
                    ms, mkey = mt_ring.next()
                    P.dma("sp", lambda e: src_fn(e, ms), ("mt_ld", ms), reads=src_reads, writes=[mkey])
                    P.op("act", lambda e: e.activation(junk[0:np_, :].rearrange("p (r c) -> p r c", r=4), mt[0:np_, ms, :, 0:128], AF.Square), reads=[mkey], writes=["junkc"])
                    P.op("dve", lambda e: e.reduce_sum(out=stc[0:np_, 0:1], in_=junk[0:np_, :], axis=AX.X), reads=["junkc"], writes=["stc0"])
                    P.op("act", lambda e: e.activation(stc[0:np_, 1:2], stc[0:np_, 0:1], AF.Sqrt, bias=EPS, scale=1.0 / 512), reads=["stc0"], writes=["stc1"])
                    P.op("dve", lambda e: e.reciprocal(stc[0:np_, 2:3], stc[0:np_, 1:2]), reads=["stc1"], writes=["stc2"])
                    for rr in range(4):
                        P.op("dve", lambda e, rr=rr: e.scalar_tensor_tensor(out=mixn[0:np_, rr * 128:(rr + 1) * 128], in0=mt[0:np_, ms, rr, 0:128], scalar=stc[0:np_, 2:3],
                                                                           in1=gattn_bc[0:np_, rr * 128:(rr + 1) * 128], op0=ALU.mult, op1=ALU.mult),
                             reads=[mkey, "stc2", "gattn_bc"], writes=["mixn"])
                    P.op("pool", lambda e: e.tensor_copy(mixn[0:np_, 512:1024].rearrange("p (r c) -> p r c", r=4), mt[0:np_, ms, :, 128:256]), reads=[mkey], writes=["mixn"])
                    for k in range(KC):
                        P.op("pe", lambda e, k=k: e.transpose(pbT[:, k * 128:k * 128 + np_], mixn[0:np_, k * 128:(k + 1) * 128], ident_bf[0:np_, 0:np_]),
                             reads=["mixn", "ident_bf"], writes=["PSA"])
                    P.op("act", lambda e: e.activation(mixT[:, :, col0:col0 + np_], pbT[:, :].rearrange("p (k c) -> p k c", k=KC)[:, :, 0:np_], AF.Copy), reads=["PSA"], writes=["mixT"])

                def c1_tile(t0, n, hs, hkey):
                    P.dma("sp", lambda e: e.dma_start(out=h_t[:, hs, :, 0:n], in_=hT_d[:, t0:t0 + n].rearrange("(k p) n -> p k n", p=128)),
                          ("hc_ld", hs), reads=["hT_d"], writes=[hkey])
                    if t0 < TP:
                        for b2 in range(2):
                            blk = t0 // 128 + b2
                            i, j = blk // 4, blk % 4

                            def src(e, ms, i=i, j=j):
                                return e.dma_start(out=mt[:, ms, :, :], in_=mixloc[i].ap()[j * 128:(j + 1) * 128, :, :])
                            mix_block(128, src, [("mixloc", i)], b2 * 128)
                    else:
                        mix_block(TS, lambda e, ms: e.dma_start(out=mt[0:TS, ms, :, :], in_=mixs_d.ap().rearrange("p (r c) -> p r c", r=4)), ["mixs_d"], 0)
                    for o in range(KC):
                        a, akey = acc_ring.next()
                        for k in range(KC):
                            P.op("pe", lambda e, k=k, o=o, a=a: e.matmul(pb[a][:, 0:n], wout_t[:, k, o * 128:(o + 1) * 128], mixT[:, k, 0:n], start=(k == 0), stop=(k == KC - 1)),
                                 reads=["mixT", "wout"], writes=[akey])
                        P.op("dve", lambda e, o=o, a=a: e.tensor_tensor(out=h_t[:, hs, o, 0:n], in0=h_t[:, hs, o, 0:n], in1=pb[a][:, 0:n], op=ALU.add), reads=[akey, hkey], writes=[hkey])
                    P.dma("sp", lambda e: e.dma_start(out=hT_d[:, t0:t0 + n].rearrange("(k p) n -> p k n", p=128), in_=h_t[:, hs, :, 0:n]),
                          ("hc_st", hs), reads=[hkey], writes=["hT_d2"])

                for (t0, n) in token_tiles():
                    hs, hkey = h_ring.next()
                    c1_tile(t0, n, hs, hkey)

        def phase_C3():
            ph, pb = new_phase(6)
            with ph:
                stg = sb("stgp", [128, 2, 1408], F32)
                stg_ring = Ring("stg", 2)
                wpg_t = sb("wpg_t", [128, KC, D], BF16)
                wpp_t = sb("wpp_t", [128, 2, D], BF16)
                NT = 256
                h_t = sb("hp_t", [128, 2, KC, NT], F32)
                h_ring = Ring("hp", 2)
                sq_t = sb("sqp_t", [128, KC, NT], BF16)
                xn_t = sb("xnp_t", [128, KC, NT], BF16)
                y_t = sb("y_t", [128, KC, NT], F32)
                rs_t = sb("rsp_t", [128, 2, NT], F32)
                pe32 = sb("pe32", [128, 2, NT], F32)
                pe16 = sb("pe16", [128, 2, NT], BF16)
                sg_t = sb("sgp_t", [128, 2, NT], F32)
                ss_ps = pb[4]
                for k in range(KC):
                    load_rows(stg, stg_ring, wpg_d, k * 128, 0, 1024, wpg_t[:, k, :], "wpg")
                for k in range(2):
                    load_rows(stg, stg_ring, wpp_d, k * 128, 0, 1024, wpp_t[:, k, :], "wpp")

                def rms_fm(src_ap_fn, src_key, gcol, dst_t, dst_key, n):
                    P.op("act", lambda e: e.activation(sq_t[:, :, 0:n], src_ap_fn(slice(0, KC)), AF.Square), reads=[src_key], writes=["sq"])
                    for k in range(KC):
                        P.op("pe", lambda e, k=k: e.matmul(ss_ps[:, 0:n], ones_t[:], sq_t[:, k, 0:n], start=(k == 0), stop=(k == KC - 1)), reads=["sq", "ones_t"], writes=["ss_ps"])
                    P.op("act", lambda e: e.activation(rs_t[:, 0, 0:n], ss_ps[:, 0:n], AF.Sqrt, bias=EPS, scale=1.0 / D), reads=["ss_ps"], writes=["rs0"])
                    P.op("dve", lambda e: e.reciprocal(rs_t[:, 1, 0:n], rs_t[:, 0, 0:n]), reads=["rs0"], writes=["rs1"])
                    for k in range(KC):
                        P.op("dve", lambda e, k=k: e.scalar_tensor_tensor(out=dst_t[:, k, 0:n], in0=src_ap_fn(k), scalar=gc_t[:, gcol + k:gcol + k + 1],
                                                                         in1=rs_t[:, 1, 0:n], op0=ALU.mult, op1=ALU.mult), reads=[src_key, "rs1", "gc_t"], writes=[dst_key])

                def c3_tile(t0, n, hs, hkey):
                    P.dma("sp", lambda e: [e.dma_start(out=h_t[:, hs, :, 0:n], in_=hT_d[:, t0:t0 + n].rearrange("(k p) n -> p k n", p=128)),
                                           e.dma_start(out=pe32[:, :, 0:n], in_=peT[:, t0:t0 + n].rearrange("(k p) n -> p k n", p=128))],
                          ("hp_ld", hs), reads=["hT_d"], writes=[hkey, "pe32"], n=2)
                    P.op("pool", lambda e: e.tensor_copy(pe16[:, :, 0:n], pe32[:, :, 0:n]), reads=["pe32"], writes=["pe16"])
                    rms_fm(lambda k: h_t[:, hs, k, 0:n], hkey, GC_PLE, xn_t, "xn", n)
                    for o in range(KC):
                        ga, gk_ = o % 2, ("acc", o % 2)
                        pa, pk_ = 2 + o % 2, ("acc", 2 + o % 2)
                        for k in range(KC):
                            P.op("pe", lambda e, k=k, o=o, ga=ga: e.matmul(pb[ga][:, 0:n], wpg_t[:, k, o * 128:(o + 1) * 128], xn_t[:, k, 0:n], start=(k == 0), stop=(k == KC - 1)),
                                 reads=["xn", "wpg"], writes=[gk_])
                        for k in range(2):
                            P.op("pe", lambda e, k=k, o=o, pa=pa: e.matmul(pb[pa][:, 0:n], wpp_t[:, k, o * 128:(o + 1) * 128], pe16[:, k, 0:n], start=(k == 0), stop=(k == 1)),
                                 reads=["pe16", "wpp"], writes=[pk_])
                        s_ = o % 2
                        P.op("act", lambda e, ga=ga, s_=s_: e.activation(sg_t[:, s_, 0:n], pb[ga][:, 0:n], AF.Exp, scale=-1.0), reads=[gk_], writes=[("sgp", s_)])
                        P.op("dve", lambda e, s_=s_: e.tensor_scalar(out=sg_t[:, s_, 0:n], in0=sg_t[:, s_, 0:n], scalar1=1.0, scalar2=None, op0=ALU.add), reads=[("sgp", s_)], writes=[("sgp", s_)])
                        P.op("dve", lambda e, s_=s_: e.reciprocal(sg_t[:, s_, 0:n], sg_t[:, s_, 0:n]), reads=[("sgp", s_)], writes=[("sgp", s_)])
                        P.op("dve", lambda e, s_=s_, pa=pa: e.tensor_tensor(out=sg_t[:, s_, 0:n], in0=sg_t[:, s_, 0:n], in1=pb[pa][:, 0:n], op=ALU.mult), reads=[("sgp", s_), pk_], writes=[("sgp", s_)])
                        P.op("dve", lambda e, s_=s_, o=o: e.tensor_tensor(out=h_t[:, hs, o, 0:n], in0=h_t[:, hs, o, 0:n], in1=sg_t[:, s_, 0:n], op=ALU.add), reads=[("sgp", s_), hkey], writes=[hkey])
                    rms_fm(lambda k: h_t[:, hs, k, 0:n], hkey, GC_FIN, y_t, "y", n)
                    P.dma("sp", lambda e: e.dma_start(out=yT_out[:, t0:t0 + n].rearrange("(k p) n -> p k n", p=128), in_=y_t[:, :, 0:n]), "y_st", reads=["y"])

                for (t0, n) in token_tiles():
                    hs, hkey = h_ring.next()
                    c3_tile(t0, n, hs, hkey)

        if "C" in stages:
            phase_C1()
            ffn_sweep(wg2, wu2, wd2, GC_FF2, hT_d, hT_d, False)
            phase_C3()

        if DBG:
            P.barrier()
            P.dma("sp", lambda e: [e.dma_start(out=dbg_a[rr * 2048 + i * 512:rr * 2048 + (i + 1) * 512, :],
                                       in_=ag2_in[i].ap().rearrange("p (t c) -> (p t) c", c=256)[rr * 512:(rr + 1) * 512, :]) for i in range(4) for rr in range(4)],
                  "dbga", reads=["ag2_in"], n=16)
        P.barrier_final("sp")
        P.emit(st)
        print("plan stats", P.stats, "sems", P.n_sems, "sbuf top", A.top)
    return nc, dram_in, dram_out


def fm_cols(g, nk):
    return np.ascontiguousarray(np.asarray(g, np.float32).reshape(nk, 128).T)


def rope_tables(pos):
    half = 16
    inv = (np.float32(10000.0) ** (-(np.arange(half, dtype=np.float32) / np.float32(half)))).astype(np.float32)
    ang = (pos.astype(np.float32)[:, None] * inv[None, :]).astype(np.float32)
    c = np.cos(ang.astype(np.float64)).astype(np.float32)
    s = np.sin(ang.astype(np.float64)).astype(np.float32)
    C32 = np.concatenate([c, c], axis=1)
    S32 = np.concatenate([-s, s], axis=1)
    return C32, S32


def prep_inputs(inp, dram_in):
    bf = ml_dtypes.bfloat16
    maps = []
    gcols = np.zeros((128, 48), np.float32)
    gcols[:, 0:8] = fm_cols(inp["g_ff1"][0], 8)
    gcols[:, 8:16] = fm_cols(inp["g_mix"][0], 8)
    gcols[:, 16:24] = fm_cols(inp["g_ff2"][0], 8)
    gcols[:, 24:32] = fm_cols(inp["g_ple"][0], 8)
    gcols[:, 32:40] = fm_cols(inp["g_final"], 8)
    gcols[:, 40:43] = fm_cols(inp["g_q"][0], 3)
    gcols[:, 43:45] = fm_cols(inp["g_kv"][0], 2)
    w_in = inp["w_in"][0]
    w_uq = inp["w_uq"][0]
    w_uk = inp["w_uk"][0]
    w_uv = inp["w_uv"][0]
    C32, S32 = rope_tables(np.arange(SEQ))
    Cs, Ss = rope_tables(np.array([SEQ]))
    c8 = np.stack([np.broadcast_to(Cs[0][None, None, :], (TS, 8, 32)), np.broadcast_to(Ss[0][None, None, :], (TS, 8, 32))], axis=1).astype(np.float32)
    cache_cat = None
    wukT_z = np.zeros((128, 8 * 256), np.float32)
    for hh in range(8):
        wukT_z[(hh % 2) * 64:(hh % 2 + 1) * 64, hh * 256:(hh + 1) * 256] = w_uk[:, hh * 64:(hh + 1) * 64].T
    if "cache" in dram_in:
        cache_cat = np.concatenate([inp["cache_ckv"][0].reshape(-1, 256), inp["cache_krope"][0].reshape(-1, 32)], axis=1)
    p = np.arange(128)[:, None, None]
    j = np.arange(4)[None, :, None]
    col = np.arange(512)[None, None, :]
    amask = np.where(j * 128 + p <= col, 0.0, NEG).astype(bf)
    shared = {
        "gc": gcols,
        "wg1": inp["w_ff1_gate"][0], "wu1": inp["w_ff1_up"][0], "wd1": inp["w_ff1_down"][0],
        "ones_bf": np.ones((128, 128), bf),
        "ident_bf": np.eye(128, dtype=np.float32).astype(bf),
        "ident_f": np.eye(128, dtype=np.float32),
        "gkv_bc": np.broadcast_to(inp["g_kv"][0][None, :], (128, 256)),
        "CT": C32.T, "ST": S32.T, "Ctok": C32, "Stok": S32,
        "amask": amask,
        "w_in": w_in,
        "w_uq": w_uq, "w_uv": w_uv,
        "wukT": wukT_z,
        "gq_bc": np.broadcast_to(inp["g_q"][0][None, :], (TS, 384)),
        "c8": c8, "iota": np.arange(128, dtype=np.float32)[:, None],
        "negrow": np.concatenate([[0.0], np.full(127, NEG)]).astype(np.float32)[None, :],
        "cache": cache_cat,
        "wout": inp["w_out"][0], "wpg": inp["w_ple_gate"][0], "wpp": inp["w_ple_proj"][0],
        "wg2": inp["w_ff2_gate"][0], "wu2": inp["w_ff2_up"][0], "wd2": inp["w_ff2_down"][0],
        "gattn_bc": np.broadcast_to(inp["g_attn_out"][0][None, :], (128, 512)),
        "trimask": np.where(np.arange(128)[:, None] <= np.arange(128)[None, :], 0.0, NEG).astype(np.float32),
    }
    sw = np.concatenate([np.arange(16, 32), np.arange(0, 16)])
    for c in range(NCORES):
        b, r = c // 4, c % 4
        m = {}
        xp = inp["x_prompt"][b, r * TP:(r + 1) * TP, :]
        xs = inp["x_sample"][c * TS:(c + 1) * TS, 0, :]
        m["xT"] = np.concatenate([xp, xs], axis=0).T
        m["peT"] = np.concatenate([inp["p_prompt"][0, b, r * TP:(r + 1) * TP, :], inp["p_sample"][0, c * TS:(c + 1) * TS, 0, :]], axis=0).T
        wB = np.zeros((D, 1280), np.float32)
        wB[:, 0:640] = w_in[:, 0:640]
        wB[:, 640:672] = w_in[:, OFF_KR:OFF_KR + 32]
        wB[:, 672:704] = w_in[:, OFF_KR + sw]
        wB[:, 704:832] = w_in[:, OFF_MQ + r * 128:OFF_MQ + (r + 1) * 128]
        wB[:, 832:960] = w_in[:, OFF_MK + r * 128:OFF_MK + (r + 1) * 128]
        wB[:, 960:1088] = w_in[:, OFF_MV + r * 128:OFF_MV + (r + 1) * 128]
        wB[:, 1088:1216] = w_in[:, OFF_MO + r * 128:OFF_MO + (r + 1) * 128]
        wB[:, 1216] = w_in[:, OFF_MI + r]
        wB[:, 1217] = w_in[:, OFF_MF + r]
        m["wB"] = wB
        wQm = np.zeros((384, 320), np.float32)
        wKm = np.zeros((256, 256), np.float32)
        wVm = np.zeros((256, 128), np.float32)
        for jj in range(2):
            hh = 2 * r + jj
            wQm[:, jj * 160 + 0:jj * 160 + 32] = w_uq[:, hh * 96 + 64:hh * 96 + 96]
            wQm[:, jj * 160 + 64:jj * 160 + 128] = w_uq[:, hh * 96:hh * 96 + 64]
            wQm[:, jj * 160 + 128:jj * 160 + 160] = w_uq[:, hh * 96 + 64 + sw]
            wKm[:, jj * 128 + 64:jj * 128 + 128] = w_uk[:, hh * 64:(hh + 1) * 64]
            wVm[:, jj * 64:(jj + 1) * 64] = w_uv[:, hh * 64:(hh + 1) * 64]
        m["wQm"], m["wKm"], m["wVm"] = wQm, wKm, wVm
        sl = slice(c * TS, (c + 1) * TS)
        m["pt"] = inp["page_table"][sl].reshape(1, 1024).astype(np.int32)
        m["stC"] = inp["state_C"][0, sl].transpose(1, 0, 2, 3).reshape(64, 16384)
        m["stn"] = inp["state_n"][0, sl].transpose(1, 0, 2).reshape(64, 128)
        m["stm"] = inp["state_m"][0, sl].transpose(1, 0).reshape(64, 1)
        gbs = np.zeros((64, 4), np.float32)
        gbs[:, 0] = np.repeat(inp["b_gate_i"][0], 16); gbs[:, 1] = np.repeat(inp["b_gate_f"][0], 16); gbs[:, 2] = -gbs[:, 1]
        m["gbs"] = gbs
        m["gml_pm"] = np.repeat(inp["g_mlstm_out"][0], 16, axis=0)
        m["gml_bc"] = np.broadcast_to(inp["g_mlstm_out"][0, r][None, :], (128, 128))
        m["gb"] = np.array([[inp["b_gate_i"][0, r], inp["b_gate_f"][0, r], -inp["b_gate_f"][0, r], 0.0]], np.float32)
        for k_ in dram_in:
            if k_ not in m:
                m[k_] = shared[k_]
        out = {}
        for k_, (shape, dt) in dram_in.items():
            a = np.ascontiguousarray(m[k_])
            assert tuple(a.shape) == shape, (k_, a.shape, shape)
            out[k_] = a
        maps.append(out)
    return maps


_CACHE = {}


def run(inp, stages):
    if stages not in _CACHE:
        _CACHE[stages] = build(stages)
    nc, dram_in, dram_out = _CACHE[stages]
    maps = prep_inputs(inp, dram_in)
    res = run_bass_kernel_spmd(nc, maps, core_ids=list(range(NCORES)))
    return res.results


def assemble(res):
    f32 = np.float32
    y_p = np.zeros((2, SEQ, D), f32); y_s = np.zeros((128, 1, D), f32)
    ckv_p = np.zeros((1, 2, SEQ, 256), f32); kr_p = np.zeros((1, 2, SEQ, 32), f32)
    C_p = np.zeros((1, 2, 4, 128, 128), f32); n_p = np.zeros((1, 2, 4, 128), f32); m_p = np.zeros((1, 2, 4), f32)
    ckv_s = np.zeros((1, 128, 1, 256), f32); kr_s = np.zeros((1, 128, 1, 32), f32)
    C_s = np.zeros((1, 128, 4, 128, 128), f32); n_s = np.zeros((1, 128, 4, 128), f32); m_s = np.zeros((1, 128, 4), f32)
    for c in range(NCORES):
        b, r = c // 4, c % 4
        o = res[c]
        yT = np.asarray(o["yT_out"])
        y_p[b, r * TP:(r + 1) * TP] = yT[:, :TP].T
        sl = slice(c * TS, (c + 1) * TS)
        y_s[sl, 0] = yT[:, TP:].T
        ckv_p[0, b, r * TP:(r + 1) * TP] = np.asarray(o["ckv_p_out"])[r * TP:(r + 1) * TP]
        kr_p[0, b, r * TP:(r + 1) * TP] = np.asarray(o["kr_p_out"])[r * TP:(r + 1) * TP]
        C_p[0, b, r] = np.asarray(o["C_p_out"]); n_p[0, b, r] = np.asarray(o["n_p_out"])[:, 0]; m_p[0, b, r] = np.asarray(o["m_p_out"])[0, 0]
        ckv_s[0, sl, 0] = np.asarray(o["ckv_s_out"]); kr_s[0, sl, 0] = np.asarray(o["kr_s_out"])
        C_s[0, sl] = np.asarray(o["C_s_out"]).reshape(4, TS, 128, 128).transpose(1, 0, 2, 3)
        n_s[0, sl] = np.asarray(o["n_s_out"]).reshape(4, TS, 128).transpose(1, 0, 2)
        m_s[0, sl] = np.asarray(o["m_s_out"]).reshape(4, TS).T
    return (y_p, y_s, ckv_p, kr_p, C_p, n_p, m_p, ckv_s, kr_s, C_s, n_s, m_s)


def kernel(**inputs):
    inp = {k: np.asarray(v) for k, v in inputs.items()}
    res = run(inp, "AGBMSTC")
    return assemble(res)
```

```python
from contextlib import ExitStack
import os
import numpy as np
import ml_dtypes
import concourse.bass as bass
import concourse.mybir as mybir
from concourse.bass_utils import run_bass_kernel_spmd

F32 = mybir.dt.float32
BF16 = mybir.dt.bfloat16
I32 = mybir.dt.int32
ALU = mybir.AluOpType
AF = mybir.ActivationFunctionType
AX = mybir.AxisListType

NCORES = 8
D = 1024
KC = 8
FF = 2816
FC = 22
TP = 2048
TS = 16
TT = TP + TS
SEQ = 8192
EPS = 1e-6
IN_COLS = 2728
OFF_KV, OFF_KR, OFF_MQ, OFF_MK, OFF_MV, OFF_MO, OFF_MI, OFF_MF = 384, 640, 672, 1184, 1696, 2208, 2720, 2724
SCALE = 96.0 ** -0.5
NPAGE = 64
NEG = -30000.0

ENGS = ("pe", "act", "dve", "pool", "sp")


class Rec:
    __slots__ = ("eng", "fn", "deps", "is_dma", "sem", "val", "marked", "inc", "snap")

    def __init__(self, eng, fn):
        self.eng = eng
        self.fn = fn
        self.deps = []
        self.is_dma = False
        self.sem = None
        self.val = None
        self.marked = False
        self.inc = 16
        self.snap = None


class Plan:
    def __init__(self, nc):
        self.nc = nc
        self.streams = {e: [] for e in ENGS}
        self.res = {}
        self.dma_sem_counts = {}
        self.finals = []

    PSUM_STR = {"ss_ps", "ss", "TR", "BC", "BT", "BS", "BU", "PSA", "PSB", "PSC", "PSD", "PSE", "PSF", "PSG", "PSH", "scC", "scD"}
    PSUM_TUP = {"gu", "dn", "acc", "S", "O", "bank", "psr"}

    @staticmethod
    def _is_psum(key):
        if isinstance(key, str):
            return key in Plan.PSUM_STR
        return isinstance(key, tuple) and len(key) > 0 and key[0] in Plan.PSUM_TUP

    def _track(self, rec, reads, writes):
        reads = list(reads)
        writes = list(writes) + [r for r in reads if Plan._is_psum(r) and r not in writes]
        deps = []
        for r in reads:
            st = self.res.get(r)
            if st is not None and st[0] is not None:
                deps.append(st[0])
        for w in writes:
            st = self.res.get(w)
            if st is not None:
                if st[0] is not None:
                    deps.append(st[0])
                deps.extend(st[1])
        seen = set()
        for d in deps:
            if d is rec or id(d) in seen:
                continue
            seen.add(id(d))
            rec.deps.append(d)
        for r in reads:
            st = self.res.setdefault(r, [None, []])
            st[1].append(rec)
        for w in writes:
            self.res[w] = [rec, []]

    def op(self, eng, fn, reads=(), writes=()):
        rec = Rec(eng, fn)
        self.streams[eng].append(rec)
        self._track(rec, reads, writes)
        return rec

    def dma(self, queue, fn, semkey, reads=(), writes=(), final=False, n=1, inc=None):
        rec = Rec(queue, fn)
        rec.is_dma = True
        rec.inc = 16 * n if inc is None else inc
        self.streams[queue].append(rec)
        c = self.dma_sem_counts.get(semkey, 0) + rec.inc
        self.dma_sem_counts[semkey] = c
        rec.sem = semkey
        rec.val = c
        self._track(rec, reads, writes)
        if final:
            self.finals.append(rec)
        return rec

    def barrier_final(self, eng="sp"):
        rec = Rec(eng, None)
        rec.deps = list(self.finals)
        rec.snap = dict(self.dma_sem_counts)
        self.streams[eng].append(rec)

    def barrier(self):
        last = []
        for e in ENGS:
            for r in reversed(self.streams[e]):
                if r.fn is not None and not r.is_dma:
                    last.append(r)
                    break
        snap = dict(self.dma_sem_counts)
        for e in ENGS:
            rec = Rec(e, None)
            rec.deps = [r for r in last if not r.is_dma]
            rec.snap = snap
            self.streams[e].append(rec)
        self.res = {}

    def emit(self, stack):
        nc = self.nc
        for e in ENGS:
            for rec in self.streams[e]:
                for d in rec.deps:
                    if not d.is_dma:
                        if d.eng == rec.eng and d.eng == "pe":
                            continue
                        d.marked = True
        semh = {}
        for e in ENGS:
            semh[("eng", e)] = stack.enter_context(nc.semaphore("sem_" + e))
            cnt = 0
            for rec in self.streams[e]:
                if not rec.is_dma and rec.marked:
                    cnt += 1
                    rec.sem = ("eng", e)
                    rec.val = cnt
        for i, k in enumerate(self.dma_sem_counts):
            semh[k] = stack.enter_context(nc.semaphore("d%d" % i))
        self.n_sems = len(semh)
        block = stack.enter_context(nc.Block())
        stats = {}

        def run_stream(e):
            def body(eng):
                known = {}
                nw = 0
                for rec in self.streams[e]:
                    need = {}
                    for d in rec.deps:
                        if (not d.is_dma) and d.eng == e and e == "pe":
                            continue
                        if d.val is None:
                            continue
                        if need.get(d.sem, 0) < d.val:
                            need[d.sem] = d.val
                    if rec.snap is not None:
                        for k_, v_ in rec.snap.items():
                            if need.get(k_, 0) < v_:
                                need[k_] = v_
                    if rec.snap is not None and e == "sp" and os.environ.get("KDBGB"):
                        print("BARRIER sp need", {str(s_): v_ for s_, v_ in need.items() if isinstance(s_, tuple) and s_[0] == "eng"}, "known", {str(s_): v_ for s_, v_ in known.items() if isinstance(s_, tuple) and s_[0] == "eng"})
                    for s, v in need.items():
                        if known.get(s, 0) >= v:
                            continue
                        eng.wait_ge(semh[s], v)
                        known[s] = v
                        nw += 1
                    if rec.fn is None:
                        continue
                    ins = rec.fn(eng)
                    if rec.is_dma:
                        if isinstance(ins, (list, tuple)):
                            ins = [i_ for i_ in ins if i_ is not None]
                            assert len(ins) * 16 == rec.inc
                            for i_ in ins:
                                i_.then_inc(semh[rec.sem], 16)
                        else:
                            ins.then_inc(semh[rec.sem], rec.inc)
                    elif rec.marked:
                        ins.then_inc(semh[rec.sem], 1)
                stats[e] = (len(self.streams[e]), nw)
            return body

        block.tensor(run_stream("pe"))
        block.scalar(run_stream("act"))
        block.vector(run_stream("dve"))
        block.gpsimd(run_stream("pool"))
        block.sync(run_stream("sp"))
        self.stats = stats


class Ring:
    def __init__(self, name, n):
        self.name, self.n, self.i = name, n, 0

    def next(self):
        s = self.i % self.n
        self.i += 1
        return s, (self.name, s)


class Arena:
    LO, HI = 16512, 229344

    def __init__(self, nc):
        self.nc, self.top, self.n = nc, Arena.LO, 0

    def sb(self, name, shape, dt):
        esz = 2 if dt == BF16 else 4
        nbytes = esz
        for d in shape[1:]:
            nbytes *= d
        off = (self.top + 31) // 32 * 32
        assert off + nbytes <= Arena.HI, ("SBUF overflow", name, off, nbytes)
        self.top = off + nbytes
        self.n += 1
        return self.nc.alloc_sbuf_tensor_at("%s_%d" % (name, self.n), list(shape), dt, offset=off)

    def mark(self):
        return self.top

    def release(self, m):
        self.top = m


def token_tiles():
    tiles = [(i * 256, 256) for i in range(TP // 256)]
    tiles.append((TP, TS))
    return tiles


def build(stages="A"):
    import os
    nc = bass.Bass("TRN2", target_bir_lowering=False)
    st = ExitStack()
    dram_in = {}
    dram_out = {}

    def din(name, shape, dt=F32):
        dram_in[name] = (tuple(shape), dt)
        return nc.dram_tensor(name, list(shape), dt, kind="ExternalInput").ap()

    def dout(name, shape, dt=F32):
        dram_out[name] = (tuple(shape), dt)
        return nc.dram_tensor(name, list(shape), dt, kind="ExternalOutput").ap()

    xT = din("xT", [D, TT])
    gc = din("gc", [128, 48])
    wg1 = din("wg1", [D, FF]); wu1 = din("wu1", [D, FF]); wd1 = din("wd1", [FF, D])
    ones_bf = din("ones_bf", [128, 128], BF16)
    ident_bf_d = din("ident_bf", [128, 128], BF16)
    ident_f_d = din("ident_f", [128, 128], F32)
    wB_d = din("wB", [D, 1280])
    wQm_d = din("wQm", [384, 320])
    wKm_d = din("wKm", [256, 256])
    wVm_d = din("wVm", [256, 128])
    gkv_bc_d = din("gkv_bc", [128, 256])
    CT_d = din("CT", [32, SEQ]); ST_d = din("ST", [32, SEQ])
    Ctok_d = din("Ctok", [SEQ, 32]); Stok_d = din("Stok", [SEQ, 32])
    amask_d = din("amask", [128, 4, 512], BF16)
    gml_bc_d = din("gml_bc", [128, 128])
    trimask_d = din("trimask", [128, 128])
    gb_d = din("gb", [1, 4])
    wout_d = din("wout", [D, D]); wpg_d = din("wpg", [D, D]); wpp_d = din("wpp", [256, D])
    wg2 = din("wg2", [D, FF]); wu2 = din("wu2", [D, FF]); wd2 = din("wd2", [FF, D])
    gattn_bc_d = din("gattn_bc", [128, 512])
    peT = din("peT", [256, TT])
    w_in_d = din("w_in", [D, IN_COLS])
    w_uq_d = din("w_uq", [384, 768]); wukT_d = din("wukT", [128, 2048]); w_uv_d = din("w_uv", [256, 512])
    gq_bc_d = din("gq_bc", [TS, 384]); c8_d = din("c8", [TS, 2, 8, 32]); iota_d = din("iota", [128, 1]); negrow_d = din("negrow", [1, 128])
    pt_d = din("pt", [1, 1024], I32)
    cache_d = din("cache", [int(os.environ.get("KNPHYS", 10240)) * 128, 288])
    stC_d = din("stC", [64, 16384]); stn_d = din("stn", [64, 128]); stm_d = din("stm", [64, 1])
    gbs_d = din("gbs", [64, 4]); gml_pm_d = din("gml_pm", [64, 128])
    hT_d = nc.dram_tensor("hT_d", [D, TT], F32).ap()
    uT_loc = [nc.dram_tensor("uT_loc%d" % i, [128, 2 * TP], BF16) for i in range(4)]
    ag1_out = [nc.dram_tensor("ag1_out%d" % i, [512, 2 * TP], BF16) for i in range(4)]
    ag2_in = [nc.dram_tensor("ag2_in%d" % i, [128, 4096], BF16) for i in range(4)]
    ag2_out = [nc.dram_tensor("ag2_out%d" % i, [512, 4096], BF16) for i in range(4)]
    ckv_p_out = dout("ckv_p_out", [SEQ, 256])
    kr_p_out = dout("kr_p_out", [SEQ, 32])
    C_p_out = dout("C_p_out", [128, 128])
    n_p_out = dout("n_p_out", [128, 1])
    m_p_out = dout("m_p_out", [1, 1])
    yT_out = dout("yT_out", [D, TT])
    mixs_d = nc.dram_tensor("mixs_d", [TS, D], BF16)
    us_d = nc.dram_tensor("us_d", [128, KC * TS], BF16)
    zs_d = nc.dram_tensor("zs_d", [TS, IN_COLS], F32)
    ckv_s_out = dout("ckv_s_out", [TS, 256]); kr_s_out = dout("kr_s_out", [TS, 32])
    C_s_out = dout("C_s_out", [64, 16384]); n_s_out = dout("n_s_out", [64, 128]); m_s_out = dout("m_s_out", [64, 1])
    mixloc = [nc.dram_tensor("mixloc%d" % i, [512, 4, 256], BF16) for i in range(4)]
    DBG = bool(os.environ.get("KDBG"))
    if DBG:
        dbg_h = dout("dbg_h", [D, TT])
        dbg_u = dout("dbg_u", [D, TT], BF16)
        dbg_a = dout("dbg_a", [SEQ, 256], BF16)
        dbg_ub = dout("dbg_ub", [128, KC, 512], BF16)
        dbg_pt = dout("dbg_pt", [128, 324], F32)

    with st:
        P = Plan(nc)
        A = Arena(nc)
        sb = A.sb

        gc_t = sb("gc_t", [128, 48], F32)
        ones_t = sb("ones_t", [128, 128], BF16)
        ident_bf = sb("ident_bf", [128, 128], BF16)
        ident_f = sb("ident_f", [128, 128], F32)
        P.dma("sp", lambda e: e.dma_start(out=gc_t[:], in_=gc), "c0", writes=["gc_t"])
        P.dma("sp", lambda e: e.dma_start(out=ones_t[:], in_=ones_bf), "c1", writes=["ones_t"])
        P.dma("sp", lambda e: e.dma_start(out=ident_bf[:], in_=ident_bf_d), "c2", writes=["ident_bf"])
        P.dma("sp", lambda e: e.dma_start(out=ident_f[:], in_=ident_f_d), "c3", writes=["ident_f"])
        GC_FF1, GC_MIX, GC_FF2, GC_PLE, GC_FIN, GC_Q, GC_KV = 0, 8, 16, 24, 32, 40, 43
        persist_mark = A.mark()

        def new_phase(nbanks_f32=8):
            P.barrier()
            A.release(persist_mark)
            ph = ExitStack()
            banks = [ph.enter_context(nc.psum_tensor("pb%d_%d" % (i, A.n), [128, 512], F32)) for i in range(nbanks_f32)]
            return ph, banks

        def load_rows(stg, stg_ring, src, r0, c0, ncols, dst_ap, dst_key, nrows=128):
            s, skey = stg_ring.next()
            P.dma("sp", lambda e: e.dma_start(out=stg[0:nrows, s, 0:ncols], in_=src[r0:r0 + nrows, c0:c0 + ncols]),
                  ("stg", s), writes=[skey])
            P.op("pool", lambda e: e.tensor_copy(dst_ap, stg[0:nrows, s, 0:ncols]), reads=[skey], writes=[dst_key])

        def ffn_sweep(wg, wu, wd, gcol_in, src_dram, dst_dram, emit_u):
            ph, pb = new_phase(5)
            with ph:
                wg_t = sb("wg_t", [128, KC, FF], BF16)
                wu_t = sb("wu_t", [128, KC, FF], BF16)
                wd_t = sb("wd_t", [128, FC, D], BF16)
                stg = sb("stg", [128, 2, 1408], F32)
                stg_ring = Ring("stg", 2)
                NT = 256
                h_t = sb("h_t", [128, 2, KC, NT], F32)
                h_ring = Ring("h", 2)
                sq_t = sb("sq_t", [128, KC, NT], BF16)
                xn_t = sb("xn_t", [128, KC, NT], BF16)
                hid_t = sb("hid_t", [128, FC, NT], BF16)
                sg_t = sb("sg_t", [128, 2, NT], F32)
                sg_ring = Ring("sg", 2)
                rs_t = sb("rs_t", [128, 2, NT], F32)
                u_t = sb("u_t", [128, KC, NT], BF16)
                gu_ring = Ring("gu", 2)
                ss_ps = pb[2]
                dn_ring = Ring("dn", 2)

                def load_tile(t0, n, hs, hkey):
                    P.dma("sp", lambda e: e.dma_start(out=h_t[:, hs, :, 0:n], in_=src_dram[:, t0:t0 + n].rearrange("(k p) n -> p k n", p=128)),
                          ("h_ld", hs), reads=["src_dram"], writes=[hkey])

                tiles = token_tiles()
                if os.environ.get("KMAXT"):
                    tiles = tiles[:int(os.environ["KMAXT"])]
                slots = [h_ring.next() for _ in tiles]
                for i in range(min(2, len(tiles))):
                    load_tile(tiles[i][0], tiles[i][1], slots[i][0], slots[i][1])

                for half in range(2):
                    c0 = half * 1408
                    for k in range(KC):
                        load_rows(stg, stg_ring, wg, k * 128, c0, 1408, wg_t[:, k, c0:c0 + 1408], ("wg", half))
                        load_rows(stg, stg_ring, wu, k * 128, c0, 1408, wu_t[:, k, c0:c0 + 1408], ("wu", half))
                for f in range(FC):
                    load_rows(stg, stg_ring, wd, f * 128, 0, 1024, wd_t[:, f, :], ("wd", f))

                def rms_fm(src_ap_fn, src_key, gcol, dst_t, dst_key, n):
                    P.op("act", lambda e: e.activation(sq_t[:, :, 0:n], src_ap_fn(slice(0, KC)), AF.Square),
                         reads=[src_key], writes=["sq"])
                    for k in range(KC):
                        P.op("pe", lambda e, k=k: e.matmul(ss_ps[:, 0:n], ones_t[:], sq_t[:, k, 0:n], start=(k == 0), stop=(k == KC - 1)),
                             reads=["sq", "ones_t"], writes=["ss_ps"])
                    P.op("act", lambda e: e.activation(rs_t[:, 0, 0:n], ss_ps[:, 0:n], AF.Sqrt, bias=EPS, scale=1.0 / D),
                         reads=["ss_ps"], writes=["rs0"])
                    P.op("dve", lambda e: e.reciprocal(rs_t[:, 1, 0:n], rs_t[:, 0, 0:n]), reads=["rs0"], writes=["rs1"])
                    for k in range(KC):
                        P.op("dve", lambda e, k=k: e.scalar_tensor_tensor(out=dst_t[:, k, 0:n], in0=src_ap_fn(k), scalar=gc_t[:, gcol + k:gcol + k + 1],
                                                                         in1=rs_t[:, 1, 0:n], op0=ALU.mult, op1=ALU.mult),
                             reads=[src_key, "rs1", "gc_t"], writes=[dst_key])

                def do_tile(t0, n, hs, hkey):
                    rms_fm(lambda k, hs=hs, n=n: h_t[:, hs, k, 0:n], hkey, gcol_in, xn_t, "xn", n)
                    for f in range(FC):
                        g, gkey = gu_ring.next()
                        half = 0 if f < 11 else 1
                        for k in range(KC):
                            P.op("pe", lambda e, k=k, f=f, g=g: e.matmul(pb[g][:, 0:n], wg_t[:, k, f * 128:(f + 1) * 128], xn_t[:, k, 0:n],
                                                                        start=(k == 0), stop=(k == KC - 1)),
                                 reads=["xn", ("wg", half)], writes=[gkey])
                        for k in range(KC):
                            P.op("pe", lambda e, k=k, f=f, g=g: e.matmul(pb[g][:, 256:256 + n], wu_t[:, k, f * 128:(f + 1) * 128], xn_t[:, k, 0:n],
                                                                        start=(k == 0), stop=(k == KC - 1)),
                                 reads=["xn", ("wu", half)], writes=[gkey])
                        s, skey = sg_ring.next()
                        P.op("act", lambda e, g=g, s=s: e.activation(sg_t[:, s, 0:n], pb[g][:, 0:n], AF.Silu), reads=[gkey], writes=[skey])
                        P.op("dve", lambda e, g=g, s=s, f=f: e.tensor_tensor(out=hid_t[:, f, 0:n], in0=sg_t[:, s, 0:n], in1=pb[g][:, 256:256 + n], op=ALU.mult),
                             reads=[skey, gkey], writes=[("hid", f)])
                    for o in range(KC):
                        d, dkey = dn_ring.next()
                        for f in range(FC):
                            P.op("pe", lambda e, f=f, o=o, d=d: e.matmul(pb[3 + d][:, 0:n], wd_t[:, f, o * 128:(o + 1) * 128], hid_t[:, f, 0:n],
                                                                        start=(f == 0), stop=(f == FC - 1)),
                                 reads=[("hid", f), ("wd", f)], writes=[dkey])
                        P.op("dve", lambda e, o=o, d=d, hs=hs: e.scalar_tensor_tensor(out=h_t[:, hs, o, 0:n], in0=pb[3 + d][:, 0:n], scalar=0.5,
                                                                                     in1=h_t[:, hs, o, 0:n], op0=ALU.mult, op1=ALU.add),
                             reads=[dkey, hkey], writes=[hkey])
                    P.dma("sp", lambda e, hs=hs, t0=t0, n=n: e.dma_start(out=dst_dram[:, t0:t0 + n].rearrange("(k p) n -> p k n", p=128), in_=h_t[:, hs, :, 0:n]),
                          ("h_st", hs), reads=[hkey], writes=["dst_dram"])
                    if emit_u:
                        rms_fm(lambda k, hs=hs, n=n: h_t[:, hs, k, 0:n], hkey, GC_MIX, u_t, "u", n)
                        if t0 >= TP:
                            P.dma("sp", lambda e, n=n: e.dma_start(out=us_d.ap().rearrange("p (k n) -> p k n", k=KC), in_=u_t[:, :, 0:n]), "us_st", reads=["u"], writes=["us_d"])
                        if t0 < TP:
                            P.dma("sp", lambda e, t0=t0, n=n: [e.dma_start(out=uT_loc[i].ap().rearrange("p (k n) -> p k n", k=2)[:, :, t0:t0 + n], in_=u_t[:, 2 * i:2 * i + 2, 0:n])
                                                                for i in range(4)],
                                  "u_st", reads=["u"], writes=["uT_loc"], n=4)
                        if DBG:
                            P.dma("sp", lambda e, hs=hs, t0=t0, n=n: e.dma_start(out=dbg_h[:, t0:t0 + n].rearrange("(k p) n -> p k n", p=128), in_=h_t[:, hs, :, 0:n]),
                                  ("h_st2", hs), reads=[hkey])
                            P.dma("sp", lambda e, t0=t0, n=n: e.dma_start(out=dbg_u[:, t0:t0 + n].rearrange("(k p) n -> p k n", p=128), in_=u_t[:, :, 0:n]),
                                  "u_st2", reads=["u"])

                for i, (t0, n) in enumerate(tiles):
                    do_tile(t0, n, slots[i][0], slots[i][1])
                    if i + 2 < len(tiles):
                        load_tile(tiles[i + 2][0], tiles[i + 2][1], slots[i + 2][0], slots[i + 2][1])

        if "A" in stages:
            ffn_sweep(wg1, wu1, wd1, GC_FF1, xT, hT_d, True)

        GROUPS = [[0, 1, 2, 3], [4, 5, 6, 7]]
        if "G" in stages:
            P.barrier()
            for i in range(4):
                P.dma("pool", lambda e, i=i: e.collective_compute("AllGather", ALU.bypass, replica_groups=GROUPS,
                                                                  ins=[uT_loc[i].ap()], outs=[ag1_out[i].ap()]),
                      ("cc1", i), reads=["uT_loc"], writes=[("ag1_out", i)], inc=1)

        def phase_B():
            ph, pb = new_phase(8)
            with ph:
                stg = sb("stgb", [128, 2, 1408], F32)
                stg_ring = Ring("stg", 2)
                wB_t = sb("wB_t", [128, KC, 704], BF16)
                wQ_t = sb("wQ_t", [128, 3, 320], BF16)
                wK_t = sb("wK_t", [128, 2, 256], BF16)
                wV_t = sb("wV_t", [128, 2, 128], BF16)
                gkv_bc = sb("gkv_bc", [128, 256], F32)
                amask = sb("amask", [128, 4, 512], BF16)
                QT = sb("QT", [128, 2, SEQ], BF16)
                KT = sb("KT", [128, 2, SEQ], BF16)
                VA = sb("VA", [128, 2, 64, 65], BF16)
                qkmax = sb("qkmax", [128, 4], F32)
                tmax = sb("tmax", [128, 2], F32)
                negm = sb("negm", [128, 4], F32)
                u_t = sb("ub_t", [128, 2, KC, 512], BF16)
                u_ring = Ring("ub", 2)
                zq_sb = sb("zq_sb", [128, 3, 512], F32)
                sqb = sb("sqb", [128, 3, 512], BF16)
                qlat = sb("qlat", [128, 3, 512], BF16)
                ckvT = sb("ckvT", [128, 2, 512], BF16)
                rsb = sb("rsb", [128, 2, 512], F32)
                cs_t = sb("cs_t", [32, 2, 2, 512], F32)
                cs_ring = Ring("cs", 2)
                cst_t = sb("cst_t", [128, 2, 2, 4, 32], F32)
                rp_t = sb("rp_t", [32, 2, 512], F32)
                otok = sb("otok", [128, 2, 320], F32)
                otok_ring = Ring("otok", 2)
                junk = sb("junk", [128, 256], F32)
                st1 = sb("st1", [128, 4], F32)
                PT = sb("PT", [128, 3, 512], BF16)
                pt_ring = Ring("pt", 3)
                osb = sb("osb", [65, 512], F32)
                atok = sb("atok", [128, 2, 4, 64], BF16)
                atok_ring = Ring("atok", 2)
                rd = sb("rd", [128, 4], F32)
                acc_ring = Ring("acc", 5)
                SS, TR, O0 = 5, 6, 7

                for k in range(KC):
                    load_rows(stg, stg_ring, wB_d, k * 128, 0, 704, wB_t[:, k, :], "wB")
                for c in range(3):
                    load_rows(stg, stg_ring, wQm_d, c * 128, 0, 320, wQ_t[:, c, :], "wQ")
                for c in range(2):
                    load_rows(stg, stg_ring, wKm_d, c * 128, 0, 256, wK_t[:, c, :], "wK")
                    load_rows(stg, stg_ring, wVm_d, c * 128, 0, 128, wV_t[:, c, :], "wV")
                P.dma("sp", lambda e: e.dma_start(out=gkv_bc[:], in_=gkv_bc_d), "c0", writes=["gkv_bc"])
                P.dma("sp", lambda e: e.dma_start(out=amask[:], in_=amask_d), "c1", writes=["amask"])
                P.op("pool", lambda e: e.memset(QT[:, 0, :], 0.0), writes=["QT"])
                P.op("pool", lambda e: e.memset(QT[:, 1, :], 0.0), writes=["QT"])
                P.op("pool", lambda e: e.memset(KT[:, 0, :], 0.0), writes=["KT"])
                P.op("pool", lambda e: e.memset(KT[:, 1, :], 0.0), writes=["KT"])
                P.op("pool", lambda e: e.memset(VA[:], 1.0), writes=["VA"])
                P.op("dve", lambda e: e.memset(qkmax[:], 0.0), writes=["qkmax"])

                def proj_fm(out_bank, lhs_fn, rhs_fn, nk, m, reads, okey, n=512):
                    for k in range(nk):
                        P.op("pe", lambda e, k=k: e.matmul(pb[out_bank][0:m, 0:n], lhs_fn(k), rhs_fn(k), start=(k == 0), stop=(k == nk - 1)),
                             reads=reads, writes=[okey])

                def norm_fm(src_t, nch, dim, gcol, dst_t, dst_key, skey):
                    P.op("act", lambda e: e.activation(sqb[:, 0:nch, :], src_t[:, 0:nch, :], AF.Square), reads=[skey], writes=["sqb"])
                    for c in range(nch):
                        P.op("pe", lambda e, c=c: e.matmul(pb[SS][:, :], ones_t[:], sqb[:, c, :], start=(c == 0), stop=(c == nch - 1)),
                             reads=["sqb", "ones_t"], writes=["ss"])
                    P.op("act", lambda e: e.activation(rsb[:, 0, :], pb[SS][:, :], AF.Sqrt, bias=EPS, scale=1.0 / dim), reads=["ss"], writes=["rsb0"])
                    P.op("dve", lambda e: e.reciprocal(rsb[:, 1, :], rsb[:, 0, :]), reads=["rsb0"], writes=["rsb1"])
                    for c in range(nch):
                        P.op("dve", lambda e, c=c: e.scalar_tensor_tensor(out=dst_t[:, c, :], in0=src_t[:, c, :], scalar=gc_t[:, gcol + c:gcol + c + 1],
                                                                         in1=rsb[:, 1, :], op0=ALU.mult, op1=ALU.mult),
                             reads=[skey, "rsb1", "gc_t"], writes=[dst_key])

                def bound_update(src_ap_fn, skeys, col):
                    P.op("act", lambda e: e.activation(sqb[:, 0, :], src_ap_fn(), AF.Square), reads=list(skeys), writes=["sqb"])
                    P.op("pe", lambda e: e.matmul(pb[SS][:, :], ones_t[:], sqb[:, 0, :], start=True, stop=True), reads=["sqb", "ones_t"], writes=["ss"])
                    P.op("dve", lambda e: e.reduce_max(out=tmax[:, 0:1], in_=pb[SS][:, :], axis=AX.X), reads=["ss"], writes=["tmax"])
                    P.op("dve", lambda e: e.tensor_tensor(out=qkmax[:, col:col + 1], in0=qkmax[:, col:col + 1], in1=tmax[:, 0:1], op=ALU.max),
                         reads=["tmax", "qkmax"], writes=["qkmax"])

                ntt = int(os.environ.get("KNTT", 16))
                for tt in range(ntt):
                    rank, c0 = tt // 4, (tt % 4) * 512
                    tok0 = tt * 512
                    cols = slice(tok0, tok0 + 512)
                    us, ukey = u_ring.next()
                    P.dma("sp", lambda e, us=us, rank=rank, c0=c0: [e.dma_start(
                        out=u_t[:, us, 2 * i:2 * i + 2, :], in_=ag1_out[i].ap()[rank * 128:(rank + 1) * 128, :].rearrange("p (k n) -> p k n", k=2)[:, :, c0:c0 + 512])
                        for i in range(4)],
                        ("ub_ld", us), reads=[("ag1_out", i) for i in range(4)], writes=[ukey], n=4)
                    if DBG and tt == 1:
                        P.dma("sp", lambda e, us=us: e.dma_start(out=dbg_ub, in_=u_t[:, us, :, :]), "dbgub", reads=[ukey])
                    cs, cskey = cs_ring.next()
                    P.dma("sp", lambda e, cs=cs, tok0=tok0: [e.dma_start(out=cs_t[:, cs, 0, :], in_=CT_d[:, tok0:tok0 + 512]),
                                                            e.dma_start(out=cs_t[:, cs, 1, :], in_=ST_d[:, tok0:tok0 + 512]),
                                                            e.dma_start(out=cst_t[:, cs, 0, :, :], in_=Ctok_d[tok0:tok0 + 512, :].rearrange("(j p) r -> p j r", p=128)),
                                                            e.dma_start(out=cst_t[:, cs, 1, :, :], in_=Stok_d[tok0:tok0 + 512, :].rearrange("(j p) r -> p j r", p=128))],
                          ("cs_ld", cs), writes=[cskey], n=4)
                    for c in range(3):
                        a, akey = acc_ring.next()
                        proj_fm(a, lambda k, c=c: wB_t[:, k, c * 128:(c + 1) * 128], lambda k, us=us: u_t[:, us, k, :], KC, 128, [ukey, "wB"], akey)
                        P.op("act", lambda e, a=a, c=c: e.activation(zq_sb[:, c, :], pb[a][:, :], AF.Copy), reads=[akey], writes=["zq_sb"])
                    norm_fm(zq_sb, 3, 384, GC_Q, qlat, "qlat", "zq_sb")
                    for c in range(2):
                        a, akey = acc_ring.next()
                        proj_fm(a, lambda k, c=c: wB_t[:, k, 384 + c * 128:384 + (c + 1) * 128], lambda k, us=us: u_t[:, us, k, :], KC, 128, [ukey, "wB"], akey)
                        P.op("act", lambda e, a=a, c=c: e.activation(zq_sb[:, c, :], pb[a][:, :], AF.Copy), reads=[akey], writes=["zq_sb"])
                    norm_fm(zq_sb, 2, 256, GC_KV, ckvT, "ckvT", "zq_sb")
                    a1, a1key = acc_ring.next()
                    proj_fm(a1, lambda k: wB_t[:, k, 640:672], lambda k, us=us: u_t[:, us, k, :], KC, 32, [ukey, "wB"], a1key)
                    a2, a2key = acc_ring.next()
                    proj_fm(a2, lambda k: wB_t[:, k, 672:704], lambda k, us=us: u_t[:, us, k, :], KC, 32, [ukey, "wB"], a2key)
                    P.op("dve", lambda e, a1=a1, cs=cs: e.tensor_tensor(out=rp_t[:, 0, :], in0=pb[a1][0:32, :], in1=cs_t[:, cs, 0, :], op=ALU.mult),
                         reads=[a1key, cskey], writes=["rp0"])
                    P.op("dve", lambda e, a2=a2, cs=cs: e.tensor_tensor(out=rp_t[:, 1, :], in0=pb[a2][0:32, :], in1=cs_t[:, cs, 1, :], op=ALU.mult),
                         reads=[a2key, cskey], writes=["rp1"])
                    P.op("dve", lambda e, cols=cols: e.tensor_tensor(out=KT[0:32, 0, cols], in0=rp_t[:, 0, :], in1=rp_t[:, 1, :], op=ALU.add),
                         reads=["rp0", "rp1", "KT"], writes=[("KT", tt)])
                    P.op("pool", lambda e, cols=cols: e.tensor_copy(KT[0:32, 1, cols], KT[0:32, 0, cols]), reads=[("KT", tt), "KT"], writes=[("KT1", tt)])
                    for j in range(4):
                        a, akey = acc_ring.next()
                        proj_fm(a, lambda k, us=us, j=j: u_t[:, us, k, j * 128:(j + 1) * 128], lambda k: wB_t[:, k, 384:704], KC, 128, [ukey, "wB"], akey, n=320)
                        os_, okey = otok_ring.next()
                        P.op("act", lambda e, a=a: e.activation(junk[:], pb[a][:, 0:256], AF.Square), reads=[akey], writes=["junk"])
                        P.op("dve", lambda e: e.reduce_sum(out=st1[:, 0:1], in_=junk[:], axis=AX.X), reads=["junk"], writes=["st1a"])
                        P.op("act", lambda e: e.activation(st1[:, 1:2], st1[:, 0:1], AF.Sqrt, bias=EPS, scale=1.0 / 256), reads=["st1a"], writes=["st1b"])
                        P.op("dve", lambda e: e.reciprocal(st1[:, 2:3], st1[:, 1:2]), reads=["st1b"], writes=["st1c"])
                        P.op("dve", lambda e, a=a, os_=os_: e.scalar_tensor_tensor(out=otok[:, os_, 0:256], in0=pb[a][:, 0:256], scalar=st1[:, 2:3], in1=gkv_bc[:],
                                                                                 op0=ALU.mult, op1=ALU.mult), reads=[akey, "st1c", "gkv_bc"], writes=[okey])
                        P.op("dve", lambda e, a=a, os_=os_, cs=cs, j=j: e.tensor_tensor(out=otok[:, os_, 256:288], in0=pb[a][:, 256:288], in1=cst_t[:, cs, 0, j, :], op=ALU.mult),
                             reads=[akey, cskey], writes=[okey])
                        P.op("dve", lambda e, a=a, os_=os_, cs=cs, j=j: e.tensor_tensor(out=otok[:, os_, 288:320], in0=pb[a][:, 288:320], in1=cst_t[:, cs, 1, j, :], op=ALU.mult),
                             reads=[akey, cskey], writes=[okey])
                        P.op("dve", lambda e, os_=os_: e.tensor_tensor(out=otok[:, os_, 256:288], in0=otok[:, os_, 256:288], in1=otok[:, os_, 288:320], op=ALU.add),
                             reads=[okey], writes=[okey])
                        if DBG and tt == 0 and j == 0:
                            dpt = sb("dpt", [128, 324], F32)
                            P.op("dve", lambda e, a=a: e.tensor_copy(dpt[:, 0:320], pb[a][:, 0:320]), reads=[akey], writes=["dpt"])
                            P.op("dve", lambda e: e.tensor_copy(dpt[:, 320:324], st1[:, 0:4]), reads=["st1c"], writes=["dpt"])
                            P.dma("sp", lambda e: e.dma_start(out=dbg_pt, in_=dpt[:]), "dbgpt", reads=["dpt"])
                        r0 = tok0 + j * 128
                        P.dma("sp", lambda e, os_=os_, r0=r0: [e.dma_start(out=ckv_p_out[r0:r0 + 128, :], in_=otok[:, os_, 0:256]),
                                                              e.dma_start(out=kr_p_out[r0:r0 + 128, :], in_=otok[:, os_, 256:288])],
                              ("otok_st", os_), reads=[okey], n=2)
                    for h in range(2):
                        qa, qakey = acc_ring.next()
                        proj_fm(qa, lambda c, h=h: wQ_t[:, c, h * 160:h * 160 + 128], lambda c: qlat[:, c, :], 3, 128, ["qlat", "wQ"], qakey)
                        qb, qbkey = acc_ring.next()
                        proj_fm(qb, lambda c, h=h: wQ_t[:, c, h * 160 + 128:h * 160 + 160], lambda c: qlat[:, c, :], 3, 32, ["qlat", "wQ"], qbkey)
                        P.op("act", lambda e, qa=qa, h=h, cols=cols: e.activation(QT[64:128, h, cols], pb[qa][64:128, :], AF.Copy), reads=[qakey, "QT"], writes=[("QT", h, tt)])
                        P.op("dve", lambda e, qa=qa, cs=cs: e.tensor_tensor(out=rp_t[:, 0, :], in0=pb[qa][0:32, :], in1=cs_t[:, cs, 0, :], op=ALU.mult),
                             reads=[qakey, cskey], writes=["rp0"])
                        P.op("dve", lambda e, qb=qb, cs=cs: e.tensor_tensor(out=rp_t[:, 1, :], in0=pb[qb][0:32, :], in1=cs_t[:, cs, 1, :], op=ALU.mult),
                             reads=[qbkey, cskey], writes=["rp1"])
                        P.op("dve", lambda e, h=h, cols=cols: e.tensor_tensor(out=QT[0:32, h, cols], in0=rp_t[:, 0, :], in1=rp_t[:, 1, :], op=ALU.add),
                             reads=["rp0", "rp1", "QT"], writes=[("QT", h, tt)])
                        bound_update(lambda h=h, cols=cols: QT[:, h, cols], [("QT", h, tt)], h)
                    for h in range(2):
                        a, akey = acc_ring.next()
                        proj_fm(a, lambda c, h=h: wK_t[:, c, h * 128:(h + 1) * 128], lambda c: ckvT[:, c, :], 2, 128, ["ckvT", "wK"], akey)
                        kt_keys = [("KT", tt)] if h == 0 else [("KT1", tt)]
                        P.op("act", lambda e, a=a, h=h, cols=cols: e.activation(KT[64:128, h, cols], pb[a][64:128, :], AF.Copy), reads=[akey, "KT"], writes=[("KTn", h, tt)])
                        bound_update(lambda h=h, cols=cols: KT[:, h, cols], kt_keys + [("KTn", h, tt)], 2 + h)
                    a, akey = acc_ring.next()
                    for j in range(4):
                        for c in range(2):
                            P.op("pe", lambda e, a=a, j=j, c=c: e.matmul(pb[a][:, j * 128:(j + 1) * 128], ckvT[:, c, j * 128:(j + 1) * 128], wV_t[:, c, :],
                                                                        start=(c == 0), stop=(c == 1)), reads=["ckvT", "wV"], writes=[akey])
                    for h in range(2):
                        P.op("act", lambda e, a=a, h=h, tt=tt: e.activation(VA[:, h, tt * 4:(tt + 1) * 4, 0:64],
                                                                           pb[a][:, :].rearrange("p (j h d) -> p j h d", j=4, h=2)[:, :, h, :], AF.Copy),
                             reads=[akey, "VA"], writes=[("VA", h, tt)])

                P.barrier()
                P.op("dve", lambda e: e.tensor_tensor(out=negm[:, 0:2], in0=qkmax[:, 0:2], in1=qkmax[:, 2:4], op=ALU.mult), writes=["negm"])
                P.op("act", lambda e: e.activation(negm[:, 2:4], negm[:, 0:2], AF.Sqrt), reads=["negm"], writes=["negm2"])
                P.op("dve", lambda e: e.tensor_scalar(out=negm[:, 0:2], in0=negm[:, 2:4], scalar1=-SCALE * 1.02, scalar2=None, op0=ALU.mult),
                     reads=["negm2"], writes=["negm3"])

                s_ring = Ring("S", 5)
                o_banks = [O0]
                nqt = int(os.environ.get("KNQT", 16))
                for h in range(2):
                    for qt in range(nqt):
                        nkb = 4 * (qt + 1)
                        okey = ("O", 0)
                        for kb in range(nkb):
                            sbk, skey = s_ring.next()
                            diag = kb >= 4 * qt
                            P.op("pe", lambda e, sbk=sbk, h=h, kb=kb, qt=qt, diag=diag: e.matmul(pb[sbk][:, :], KT[:, h, kb * 128:(kb + 1) * 128], QT[:, h, qt * 512:(qt + 1) * 512],
                                                                                               start=True, stop=(not diag)), writes=[skey])
                            if diag:
                                P.op("pe", lambda e, sbk=sbk, j=kb - 4 * qt: e.matmul(pb[sbk][:, :], ident_bf[:], amask[:, j, :], start=False, stop=True),
                                     reads=["amask", "ident_bf"], writes=[skey])
                            ps_, pkey = pt_ring.next()
                            P.op("act", lambda e, sbk=sbk, ps_=ps_, h=h: e.activation(PT[:, ps_, :], pb[sbk][:, :], AF.Exp, bias=negm[:, h:h + 1], scale=SCALE),
                                 reads=[skey, "negm3"], writes=[pkey])
                            P.op("pe", lambda e, ps_=ps_, h=h, kb=kb, nkb=nkb: e.matmul(pb[O0][0:65, :], VA[:, h, kb, :], PT[:, ps_, :], start=(kb == 0), stop=(kb == nkb - 1)),
                                 reads=[pkey], writes=[okey])
                        P.op("act", lambda e: e.activation(osb[:, :], pb[O0][0:65, :], AF.Copy), reads=[okey], writes=["osb"])
                        for j in range(4):
                            P.op("pe", lambda e, j=j: e.transpose(pb[TR][:, j * 65:(j + 1) * 65], osb[0:65, j * 128:(j + 1) * 128], ident_f[0:65, 0:65]),
                                 reads=["osb", "ident_f"], writes=["TR"])
                        P.op("dve", lambda e: e.reciprocal(rd[:, :], pb[TR][:, 0:260].rearrange("p (j c) -> p j c", j=4)[:, :, 64]), reads=["TR"], writes=["rd"])
                        as_, akey = atok_ring.next()
                        for j in range(4):
                            P.op("dve", lambda e, j=j, as_=as_: e.tensor_scalar(out=atok[:, as_, j, :], in0=pb[TR][:, j * 65:j * 65 + 64], scalar1=rd[:, j:j + 1], scalar2=None, op0=ALU.mult),
                                 reads=["TR", "rd"], writes=[akey])
                        P.dma("sp", lambda e, as_=as_, qt=qt, h=h: e.dma_start(
                            out=ag2_in[qt % 4].ap().rearrange("p (t c) -> (p t) c", c=256)[(qt // 4) * 512:(qt // 4 + 1) * 512, h * 64:(h + 1) * 64].rearrange("(j p) d -> p j d", p=128),
                            in_=atok[:, as_, :, :]),
                            ("atok_st", as_), reads=[akey], writes=["ag2_in"])

        if "B" in stages:
            phase_B()

        def phase_M():
            ph, pb = new_phase(8)
            with ph:
                stg = sb("stgm", [128, 2, 1408], F32)
                stg_ring = Ring("stg", 2)
                wM_t = sb("wM_t", [128, KC, 576], BF16)
                gml_bc = sb("gml_bc", [128, 128], F32)
                trimask = sb("trimask", [128, 128], F32)
                gb = sb("gb", [1, 4], F32)
                onesrow = sb("onesrow", [1, 128], F32)
                u_t = sb("um_t", [128, 2, KC, 512], BF16)
                u_ring = Ring("um", 2)
                QmT = sb("QmT", [128, 512], BF16)
                KmT = sb("KmT", [128, 512], BF16)
                rw = sb("rw", [1, 14, 512], F32)
                R_IG, R_LF, R_B, R_GI, R_CM, R_M, R_NM, R_W, R_T, R_EMR, R_WS, R_Z, R_E = range(13)
                mst = sb("mst", [1, 4], F32)
                mst2 = sb("mst2", [1, 2], F32)
                MSTOP = int(os.environ.get("KMSTOP", 9))
                cols_sb = sb("cols_sb", [128, 2, 8], F32)
                cols_ring = Ring("colsb", 2)
                ET = sb("ET", [128, 128], F32)
                AT = sb("AT", [128, 128], BF16)
                kw = sb("kw", [128, 128], BF16)
                vaug = sb("vaug", [128, 2, 130], BF16)
                v_ring = Ring("vaug", 2)
                sgo = sb("sgo", [128, 2, 128], F32)
                Cst = sb("Cst", [128, 129], F32)
                Cb = sb("Cb", [128, 129], BF16)
                tmpn = sb("tmpn", [128, 129], F32)
                num = sb("num", [128, 129], F32)
                dn = sb("dn", [128, 4], F32)
                hm = sb("hm", [128, 128], F32)
                junkm = sb("junkm", [128, 128], F32)
                hout = sb("hout", [128, 2, 128], BF16)
                ho_ring = Ring("hout", 2)
                BQ, BK, BI, BF_, BT, BC, BS, BU = range(8)

                for k in range(KC):
                    load_rows(stg, stg_ring, wB_d, k * 128, 704, 576, wM_t[:, k, :], "wM")
                P.dma("sp", lambda e: [e.dma_start(out=gml_bc[:], in_=gml_bc_d), e.dma_start(out=trimask[:], in_=trimask_d),
                                       e.dma_start(out=gb[:], in_=gb_d)], "c0", writes=["mconst"], n=3)
                P.op("dve", lambda e: e.memset(onesrow[:], 1.0), writes=["onesrow"])
                P.op("dve", lambda e: e.memset(rw[:, R_Z, :], 0.0), writes=["rwz"])
                P.op("dve", lambda e: e.memset(mst[:], 0.0), writes=["mst"])
                P.op("dve", lambda e: e.memset(Cst[:], 0.0), writes=["Cst"])
                P.op("pool", lambda e: e.memset(Cb[:], 0.0), writes=["Cb"])
                P.op("pool", lambda e: e.memset(vaug[:], 1.0), writes=["vaug_init"])

                def m_chunk(tt, c, us, ukey):
                    cc = tt * 4 + c
                    cs_ = slice(c * 128, (c + 1) * 128)
                    mp, mn = cc % 2, (cc + 1) % 2
                    row = lambda r: rw[0:1, r, cs_]
                    P.op("dve", lambda e: e.tensor_tensor_scan(row(R_B), onesrow[0:1, :], row(R_LF), 0.0, ALU.mult, ALU.add), reads=["lf", "onesrow"], writes=["b"])
                    P.op("dve", lambda e: e.tensor_tensor(out=row(R_GI), in0=row(R_IG), in1=row(R_B), op=ALU.subtract), reads=["ig", "b"], writes=["gi"])
                    P.op("dve", lambda e: e.tensor_tensor_scan(row(R_CM), row(R_GI), row(R_Z), -1e30, ALU.max, ALU.add), reads=["gi", "rwz"], writes=["cm"])
                    P.op("dve", lambda e: e.tensor_scalar(out=row(R_M), in0=row(R_CM), scalar1=mst[0:1, mp:mp + 1], scalar2=None, op0=ALU.max), reads=["cm", "mst"], writes=["M"])
                    P.op("dve", lambda e: e.tensor_tensor(out=mst[0:1, mn:mn + 1], in0=rw[0:1, R_M, c * 128 + 127:c * 128 + 128], in1=rw[0:1, R_B, c * 128 + 127:c * 128 + 128], op=ALU.add),
                         reads=["M", "b", "mst"], writes=["mstn"])
                    P.op("dve", lambda e: e.tensor_scalar(out=row(R_NM), in0=row(R_M), scalar1=-1.0, scalar2=None, op0=ALU.mult), reads=["M"], writes=["nM"])
                    P.op("act", lambda e: e.activation(row(R_W), row(R_M), AF.Exp, bias=mst[0:1, mp:mp + 1], scale=-1.0), reads=["M", "mst"], writes=["w"])
                    P.op("dve", lambda e: e.tensor_tensor(out=row(R_T), in0=row(R_B), in1=row(R_M), op=ALU.add), reads=["b", "M"], writes=["t"])
                    P.op("act", lambda e: e.activation(row(R_EMR), row(R_T), AF.Exp, scale=-1.0), reads=["t"], writes=["emr"])
                    P.op("dve", lambda e: e.tensor_tensor(out=mst[0:1, 2:3], in0=rw[0:1, R_B, c * 128 + 127:c * 128 + 128], in1=mst[0:1, mn:mn + 1], op=ALU.subtract),
                         reads=["b", "mstn"], writes=["dlt"])
                    P.op("act", lambda e: e.activation(row(R_WS), row(R_GI), AF.Exp, bias=mst[0:1, 2:3]), reads=["gi", "dlt"], writes=["ws"])
                    P.op("act", lambda e: e.activation(mst[0:1, 3:4], mst[0:1, mp:mp + 1], AF.Exp, bias=mst[0:1, 2:3]), reads=["mst", "dlt"], writes=["a11"])
                    if MSTOP < 2:
                        return
                    for ci, (r, key) in enumerate([(R_GI, "gi"), (R_W, "w"), (R_EMR, "emr"), (R_WS, "ws")]):
                        P.op("pe", lambda e, ci=ci, r=r: e.matmul(pb[BC][:, 2 * ci:2 * ci + 2], row(r), onesrow[0:1, 0:2], start=True, stop=True), reads=[key, "onesrow"], writes=["BC"])
                    P.op("dve", lambda e: e.tensor_copy(mst2[0:1, 0:1], mst[0:1, 3:4]), reads=["a11"], writes=["a11b"])
                    P.op("dve", lambda e: e.tensor_copy(mst2[0:1, 1:2], mst[0:1, 3:4]), reads=["a11"], writes=["a11b"])
                    P.op("pe", lambda e: e.matmul(pb[BC][:, 8:10], onesrow[0:1, :], mst2[0:1, 0:2], start=True, stop=True), reads=["a11b", "onesrow"], writes=["BC"])
                    P.op("pe", lambda e: e.matmul(pb[BC][:, 128:256], onesrow[0:1, :], row(R_NM), start=True, stop=False), reads=["nM", "onesrow"], writes=["BC"])
                    P.op("pe", lambda e: e.matmul(pb[BC][:, 128:256], ident_f[:], trimask[:], start=False, stop=True), reads=["mconst", "ident_f"], writes=["BC"])
                    cb_, cbkey = cols_ring.next()
                    P.op("dve", lambda e: e.tensor_copy(cols_sb[:, cb_, 0:5], pb[BC][:, 0:10].rearrange("p (c two) -> p c two", two=2)[:, :, 0]), reads=["BC"], writes=[cbkey])
                    P.op("act", lambda e: e.activation(ET[:], pb[BC][:, 128:256], AF.Exp, bias=cols_sb[:, cb_, 0:1]), reads=["BC", cbkey], writes=["ET"])
                    if MSTOP < 3:
                        return
                    for k in range(KC):
                        P.op("pe", lambda e, k=k: e.matmul(pb[BT][:, 0:384], u_t[:, us, k, cs_], wM_t[:, k, 128:512], start=(k == 0), stop=(k == KC - 1)),
                             reads=[ukey, "wM"], writes=["BT"])
                    P.op("dve", lambda e: e.tensor_scalar(out=cols_sb[:, cb_, 5:6], in0=cols_sb[:, cb_, 3:4], scalar1=128.0 ** -0.5, scalar2=None, op0=ALU.mult), reads=[cbkey], writes=[cbkey])
                    P.op("dve", lambda e: e.tensor_scalar(out=kw[:], in0=pb[BT][:, 0:128], scalar1=cols_sb[:, cb_, 5:6], scalar2=None, op0=ALU.mult),
                         reads=["BT", cbkey], writes=["kw"])
                    vs, vkey = v_ring.next()
                    if MSTOP == 3 and os.environ.get("KMSUB") == "a":
                        return
                    P.op("act", lambda e: e.activation(vaug[:, vs, 0:128], pb[BT][:, 128:256], AF.Copy), reads=["BT", "vaug_init", "kw"], writes=[vkey])
                    if MSTOP == 3 and os.environ.get("KMSUB") == "b":
                        return
                    P.op("act", lambda e: e.activation(sgo[:, 0, :], pb[BT][:, 256:384], AF.Exp, scale=-1.0), reads=["BT"], writes=["sgo0"])
                    P.op("dve", lambda e: e.tensor_scalar(out=sgo[:, 0, :], in0=sgo[:, 0, :], scalar1=1.0, scalar2=None, op0=ALU.add), reads=["sgo0"], writes=["sgo0"])
                    P.op("dve", lambda e: e.reciprocal(sgo[:, 1, :], sgo[:, 0, :]), reads=["sgo0"], writes=["sgo1"])
                    if MSTOP < 4:
                        return
                    P.op("pe", lambda e: e.matmul(pb[BS][:, 0:128], KmT[:, cs_], QmT[:, cs_], start=True, stop=True), reads=["KmT", "QmT"], writes=["BS"])
                    P.op("dve", lambda e: e.tensor_tensor(out=AT[:], in0=ET[:], in1=pb[BS][:, 0:128], op=ALU.mult), reads=["ET", "BS"], writes=["AT"])
                    P.op("pe", lambda e: e.matmul(pb[BS][:, 128:257], AT[:], vaug[:, vs, 0:129], start=True, stop=True), reads=["AT", vkey], writes=["BS"])
                    P.op("pe", lambda e: e.matmul(pb[BU][:, 0:129], QmT[:, cs_], Cb[:], start=True, stop=True), reads=["QmT", "Cb"], writes=["BU"])
                    P.op("dve", lambda e: e.tensor_scalar(out=tmpn[:], in0=pb[BU][:, 0:129], scalar1=cols_sb[:, cb_, 1:2], scalar2=None, op0=ALU.mult), reads=["BU", cbkey], writes=["tmpn"])
                    P.op("dve", lambda e: e.tensor_tensor(out=num[:], in0=tmpn[:], in1=pb[BS][:, 128:257], op=ALU.add), reads=["tmpn", "BS"], writes=["num"])
                    P.op("dve", lambda e: e.tensor_scalar(out=dn[:, 1:2], in0=num[:, 128:129], scalar1=-1.0, scalar2=None, op0=ALU.mult), reads=["num"], writes=["dn1"])
                    P.op("dve", lambda e: e.tensor_tensor(out=dn[:, 0:1], in0=num[:, 128:129], in1=dn[:, 1:2], op=ALU.max), reads=["num", "dn1"], writes=["dn0"])
                    P.op("dve", lambda e: e.tensor_scalar(out=dn[:, 0:1], in0=dn[:, 0:1], scalar1=cols_sb[:, cb_, 2:3], scalar2=None, op0=ALU.max), reads=["dn0", cbkey], writes=["dn0"])
                    P.op("dve", lambda e: e.reciprocal(dn[:, 1:2], dn[:, 0:1]), reads=["dn0"], writes=["dn1"])
                    P.op("dve", lambda e: e.scalar_tensor_tensor(out=hm[:], in0=num[:, 0:128], scalar=dn[:, 1:2], in1=sgo[:, 1, :], op0=ALU.mult, op1=ALU.mult),
                         reads=["num", "dn1", "sgo1"], writes=["hm"])
                    P.op("act", lambda e: e.activation(junkm[:], hm[:], AF.Square), reads=["hm"], writes=["junkm"])
                    P.op("dve", lambda e: e.reduce_sum(out=dn[:, 2:3], in_=junkm[:], axis=AX.X), reads=["junkm"], writes=["dn2"])
                    P.op("act", lambda e: e.activation(dn[:, 3:4], dn[:, 2:3], AF.Sqrt, bias=EPS, scale=1.0 / 128), reads=["dn2"], writes=["dn3"])
                    P.op("dve", lambda e: e.reciprocal(dn[:, 2:3], dn[:, 3:4]), reads=["dn3"], writes=["dn4"])
                    hs_, hkey_ = ho_ring.next()
                    P.op("dve", lambda e: e.scalar_tensor_tensor(out=hout[:, hs_, :], in0=hm[:], scalar=dn[:, 2:3], in1=gml_bc[:], op0=ALU.mult, op1=ALU.mult),
                         reads=["hm", "dn4", "mconst"], writes=[hkey_])
                    qt, j = cc // 4, cc % 4
                    r0 = (qt // 4) * 512 + j * 128
                    P.dma("sp", lambda e: e.dma_start(out=ag2_in[qt % 4].ap().rearrange("p (t c) -> (p t) c", c=256)[r0:r0 + 128, 128:256], in_=hout[:, hs_, :]),
                          ("hout_st", hs_), reads=[hkey_], writes=["ag2_in"])
                    P.op("pe", lambda e: e.matmul(pb[BU][:, 256:385], kw[:], vaug[:, vs, 0:129], start=True, stop=True), reads=["kw", vkey], writes=["BU"])
                    P.op("dve", lambda e: e.scalar_tensor_tensor(out=Cst[:], in0=Cst[:], scalar=cols_sb[:, cb_, 4:5], in1=pb[BU][:, 256:385], op0=ALU.mult, op1=ALU.add),
                         reads=["BU", cbkey, "Cst"], writes=["Cst"])
                    P.op("act", lambda e: e.activation(Cb[:], Cst[:], AF.Copy), reads=["Cst"], writes=["Cb"])

                def m_tile(tt, us, ukey):
                    rank, c0 = tt // 4, (tt % 4) * 512
                    P.dma("sp", lambda e: [e.dma_start(
                        out=u_t[:, us, 2 * i:2 * i + 2, :], in_=ag1_out[i].ap()[rank * 128:(rank + 1) * 128, :].rearrange("p (k n) -> p k n", k=2)[:, :, c0:c0 + 512])
                        for i in range(4)], ("um_ld", us), reads=[("ag1_out", i) for i in range(4)], writes=[ukey], n=4)
                    for (bank, c0w, dst, key, scl) in [(BQ, 0, QmT, "QmT", 1.0), (BK, 128, KmT, "KmT", 128.0 ** -0.5)]:
                        for k in range(KC):
                            P.op("pe", lambda e, k=k, bank=bank, c0w=c0w: e.matmul(pb[bank][:, :], wM_t[:, k, c0w:c0w + 128], u_t[:, us, k, :], start=(k == 0), stop=(k == KC - 1)),
                                 reads=[ukey, "wM"], writes=[("bank", bank)])
                        P.op("act", lambda e, bank=bank, dst=dst, scl=scl: e.activation(dst[:], pb[bank][:, :], AF.Copy, scale=scl), reads=[("bank", bank)], writes=[key])
                    for (bank, col) in [(BI, 512), (BF_, 513)]:
                        for k in range(KC):
                            P.op("pe", lambda e, k=k, bank=bank, col=col: e.matmul(pb[bank][0:1, :], wM_t[:, k, col:col + 1], u_t[:, us, k, :], start=(k == 0), stop=(k == KC - 1)),
                                 reads=[ukey, "wM"], writes=[("bank", bank)])
                    P.op("dve", lambda e: e.tensor_scalar(out=rw[0:1, R_IG, :], in0=pb[BI][0:1, :], scalar1=gb[0:1, 0:1], scalar2=None, op0=ALU.add), reads=[("bank", BI), "mconst"], writes=["ig"])
                    P.op("act", lambda e: e.activation(rw[0:1, R_E, :], pb[BF_][0:1, :], AF.Exp, bias=gb[0:1, 2:3], scale=-1.0), reads=[("bank", BF_), "mconst"], writes=["rwe"])
                    P.op("act", lambda e: e.activation(rw[0:1, R_E, :], rw[0:1, R_E, :], AF.Ln, bias=1.0), reads=["rwe"], writes=["rwe"])
                    P.op("dve", lambda e: e.tensor_scalar(out=rw[0:1, R_LF, :], in0=rw[0:1, R_E, :], scalar1=-1.0, scalar2=None, op0=ALU.mult), reads=["rwe"], writes=["lf"])
                    for c in range(4):
                        m_chunk(tt, c, us, ukey)

                nmt = int(os.environ.get("KNMT", 16))
                for tt in range(nmt):
                    us, ukey = u_ring.next()
                    m_tile(tt, us, ukey)
                mfin = (nmt * 4) % 2
                P.dma("sp", lambda e: [e.dma_start(out=C_p_out, in_=Cst[:, 0:128]), e.dma_start(out=n_p_out, in_=Cst[:, 128:129]),
                                       e.dma_start(out=m_p_out, in_=mst[0:1, mfin:mfin + 1])], "mout", reads=["Cst", "mstn"], n=3)

        if "M" in stages:
            phase_M()

        def phase_S1():
            ph, pb = new_phase(6)
            with ph:
                stg = sb("stgs", [128, 2, 1408], F32)
                stg_ring = Ring("stg", 2)
                wIn_t = sb("wIn_t", [128, KC, IN_COLS], BF16)
                us_t = sb("us_t", [128, KC, TS], BF16)
                z_sb = sb("z_sb", [TS, IN_COLS], F32)
                Cpm = sb("Cpm", [64, 128, 128], F32)
                pmq = sb("pmq", [64, 4, 128], F32)
                pmg = sb("pmg", [64, 16], F32)
                npm = sb("npm", [64, 128], F32)
                gbs = sb("gbs", [64, 4], F32)
                gml_pm = sb("gml_pm", [64, 128], F32)
                ks = sb("ks", [64, 128], F32)
                kwv = sb("kwv", [64, 128], F32)
                qCacc = sb("qCacc", [64, 128], F32)
                tv = sb("tv", [64, 2, 128], F32)
                jk = sb("jk", [64, 128], F32)
                hs_ = sb("hs_", [64, 128], F32)
                hob = sb("hob", [64, 128], BF16)
                for half in range(2):
                    c0 = half * 1364
                    for k in range(KC):
                        load_rows(stg, stg_ring, w_in_d, k * 128, c0, 1364, wIn_t[:, k, c0:c0 + 1364], "wIn")
                P.dma("sp", lambda e: [e.dma_start(out=us_t[:], in_=us_d.ap().rearrange("p (k n) -> p k n", k=KC)),
                                       e.dma_start(out=Cpm[:], in_=stC_d.rearrange("p (d e) -> p d e", d=128)),
                                       e.dma_start(out=npm[:], in_=stn_d), e.dma_start(out=pmg[:, 0:1], in_=stm_d),
                                       e.dma_start(out=gbs[:], in_=gbs_d), e.dma_start(out=gml_pm[:], in_=gml_pm_d)],
                      "s1ld", reads=["us_d"], writes=["us_t", "Cpm", "pmconst"], n=6)
                col = 0
                gi_ = 0
                while col < IN_COLS:
                    w_ = min(512, IN_COLS - col)
                    a = gi_ % 4
                    for k in range(KC):
                        P.op("pe", lambda e, k=k, a=a, col=col, w_=w_: e.matmul(pb[a][0:TS, 0:w_], us_t[:, k, :], wIn_t[:, k, col:col + w_], start=(k == 0), stop=(k == KC - 1)),
                             reads=["us_t", "wIn"], writes=[("acc", a)])
                    P.op("act", lambda e, a=a, col=col, w_=w_: e.activation(z_sb[:, col:col + w_], pb[a][0:TS, 0:w_], AF.Copy), reads=[("acc", a)], writes=["z_sb"])
                    col += w_
                    gi_ += 1
                P.dma("sp", lambda e: e.dma_start(out=zs_d.ap(), in_=z_sb[:]), "zs_st", reads=["z_sb"], writes=["zs_d"])
                def pm_load(e):
                    outs = []
                    for h in range(4):
                        for qi, off in enumerate([OFF_MQ, OFF_MK, OFF_MV, OFF_MO]):
                            outs.append(e.dma_start(out=pmq[h * 16:(h + 1) * 16, qi, :], in_=zs_d.ap()[:, off + h * 128:off + (h + 1) * 128]))
                        outs.append(e.dma_start(out=pmg[h * 16:(h + 1) * 16, 1:2], in_=zs_d.ap()[:, OFF_MI + h:OFF_MI + h + 1], allow_slow_non_contiguous=True))
                        outs.append(e.dma_start(out=pmg[h * 16:(h + 1) * 16, 2:3], in_=zs_d.ap()[:, OFF_MF + h:OFF_MF + h + 1], allow_slow_non_contiguous=True))
                    return outs
                P.dma("sp", pm_load, "pmld", reads=["zs_d"], writes=["pm"], n=24)
                G = lambda c: pmg[:, c:c + 1]
                P.op("dve", lambda e: e.tensor_tensor(out=G(3), in0=G(1), in1=gbs[:, 0:1], op=ALU.add), reads=["pm", "pmconst"], writes=["g3"])
                P.op("act", lambda e: e.activation(G(15), G(2), AF.Exp, bias=gbs[:, 2:3], scale=-1.0), reads=["pm", "pmconst"], writes=["g15"])
                P.op("act", lambda e: e.activation(G(15), G(15), AF.Ln, bias=1.0), reads=["g15"], writes=["g15"])
                P.op("dve", lambda e: e.tensor_scalar(out=G(4), in0=G(15), scalar1=-1.0, scalar2=None, op0=ALU.mult), reads=["g15"], writes=["g4"])
                P.op("dve", lambda e: e.tensor_tensor(out=G(5), in0=G(4), in1=G(0), op=ALU.add), reads=["g4", "pmconst"], writes=["g5"])
                P.op("dve", lambda e: e.tensor_tensor(out=G(6), in0=G(5), in1=G(3), op=ALU.max), reads=["g5", "g3"], writes=["g6"])
                P.op("dve", lambda e: e.tensor_tensor(out=G(15), in0=G(5), in1=G(6), op=ALU.subtract), reads=["g5", "g6", "g15"], writes=["g15b"])
                P.op("act", lambda e: e.activation(G(7), G(15), AF.Exp), reads=["g15b"], writes=["g7"])
                P.op("dve", lambda e: e.tensor_tensor(out=G(15), in0=G(3), in1=G(6), op=ALU.subtract), reads=["g3", "g6", "g7"], writes=["g15c"])
                P.op("act", lambda e: e.activation(G(8), G(15), AF.Exp), reads=["g15c"], writes=["g8"])
                P.op("act", lambda e: e.activation(G(9), G(6), AF.Exp, scale=-1.0), reads=["g6"], writes=["g9"])
                P.op("dve", lambda e: e.tensor_scalar(out=ks[:], in0=pmq[:, 1, :], scalar1=128.0 ** -0.5, scalar2=None, op0=ALU.mult), reads=["pm"], writes=["ks"])
                P.op("dve", lambda e: e.tensor_tensor(out=jk[:], in0=pmq[:, 0, :], in1=ks[:], op=ALU.mult), reads=["pm", "ks"], writes=["jk"])
                P.op("dve", lambda e: e.reduce_sum(out=G(10), in_=jk[:], axis=AX.X), reads=["jk"], writes=["g10"])
                P.op("dve", lambda e: e.tensor_tensor(out=jk[:], in0=pmq[:, 0, :], in1=npm[:], op=ALU.mult), reads=["pm", "Cpm", "g10"], writes=["jk"])
                P.op("dve", lambda e: e.reduce_sum(out=G(11), in_=jk[:], axis=AX.X), reads=["jk"], writes=["g11"])
                P.op("dve", lambda e: e.tensor_tensor(out=G(12), in0=G(10), in1=G(8), op=ALU.mult), reads=["g10", "g8"], writes=["g12"])
                P.op("dve", lambda e: e.tensor_scalar(out=kwv[:], in0=ks[:], scalar1=G(8), scalar2=None, op0=ALU.mult), reads=["ks", "g8"], writes=["kwv"])
                P.op("dve", lambda e: e.memset(qCacc[:], 0.0), writes=["qCacc"])
                for d in range(128):
                    P.op("dve", lambda e, d=d: e.scalar_tensor_tensor(out=qCacc[:], in0=Cpm[:, d, :], scalar=pmq[:, 0, d:d + 1], in1=qCacc[:], op0=ALU.mult, op1=ALU.add),
                         reads=["Cpm", "pm", "qCacc"], writes=["qCacc"])
                    P.op("dve", lambda e, d=d: e.tensor_scalar(out=tv[:, d % 2, :], in0=pmq[:, 2, :], scalar1=kwv[:, d:d + 1], scalar2=None, op0=ALU.mult),
                         reads=["pm", "kwv"], writes=[("tv", d % 2)])
                    P.op("dve", lambda e, d=d: e.scalar_tensor_tensor(out=Cpm[:, d, :], in0=Cpm[:, d, :], scalar=G(7), in1=tv[:, d % 2, :], op0=ALU.mult, op1=ALU.add),
                         reads=[("tv", d % 2), "g7", "qCacc"], writes=["Cpm"])
                P.dma("sp", lambda e: e.dma_start(out=C_s_out.rearrange("p (d e) -> p d e", d=128), in_=Cpm[:]), "cs_st", reads=["Cpm"])
                P.op("dve", lambda e: e.scalar_tensor_tensor(out=jk[:], in0=npm[:], scalar=G(7), in1=kwv[:], op0=ALU.mult, op1=ALU.add), reads=["g7", "kwv", "g11"], writes=["jk2"])
                P.dma("sp", lambda e: [e.dma_start(out=n_s_out, in_=jk[:]), e.dma_start(out=m_s_out, in_=pmg[:, 6:7])], "ns_st", reads=["jk2", "g6"], n=2)
                P.op("dve", lambda e: e.tensor_scalar(out=hs_[:], in0=qCacc[:], scalar1=G(7), scalar2=None, op0=ALU.mult), reads=["qCacc", "g7"], writes=["hs"])
                P.op("dve", lambda e: e.scalar_tensor_tensor(out=hs_[:], in0=pmq[:, 2, :], scalar=G(12), in1=hs_[:], op0=ALU.mult, op1=ALU.add), reads=["hs", "g12", "pm"], writes=["hs"])
                P.op("dve", lambda e: e.scalar_tensor_tensor(out=G(13), in0=G(11), scalar=G(7), in1=G(12), op0=ALU.mult, op1=ALU.add), reads=["g11", "g7", "g12"], writes=["g13"])
                P.op("dve", lambda e: e.tensor_scalar(out=G(15), in0=G(13), scalar1=-1.0, scalar2=None, op0=ALU.mult), reads=["g13", "g8"], writes=["g15d"])
                P.op("dve", lambda e: e.tensor_tensor(out=G(13), in0=G(13), in1=G(15), op=ALU.max), reads=["g15d"], writes=["g13b"])
                P.op("dve", lambda e: e.tensor_tensor(out=G(13), in0=G(13), in1=G(9), op=ALU.max), reads=["g13b", "g9"], writes=["g13c"])
                P.op("dve", lambda e: e.reciprocal(G(14), G(13)), reads=["g13c"], writes=["g14"])
                P.op("act", lambda e: e.activation(jk[:], pmq[:, 3, :], AF.Exp, scale=-1.0), reads=["pm", "jk2"], writes=["jk2", "jk3"])
                P.op("dve", lambda e: e.tensor_scalar(out=jk[:], in0=jk[:], scalar1=1.0, scalar2=None, op0=ALU.add), reads=["jk3"], writes=["jk3"])
                P.op("dve", lambda e: e.reciprocal(jk[:], jk[:]), reads=["jk3"], writes=["jk3"])
                P.op("dve", lambda e: e.scalar_tensor_tensor(out=hs_[:], in0=hs_[:], scalar=G(14), in1=jk[:], op0=ALU.mult, op1=ALU.mult), reads=["hs", "g14", "jk3"], writes=["hs2"])
                P.op("act", lambda e: e.activation(jk[:], hs_[:], AF.Square), reads=["hs2"], writes=["jk4"])
                P.op("dve", lambda e: e.reduce_sum(out=G(15), in_=jk[:], axis=AX.X), reads=["jk4"], writes=["g15e"])
                P.op("act", lambda e: e.activation(G(13), G(15), AF.Sqrt, bias=EPS, scale=1.0 / 128), reads=["g15e"], writes=["g13d"])
                P.op("dve", lambda e: e.reciprocal(G(14), G(13)), reads=["g13d"], writes=["g14b"])
                P.op("dve", lambda e: e.scalar_tensor_tensor(out=hob[:], in0=hs_[:], scalar=G(14), in1=gml_pm[:], op0=ALU.mult, op1=ALU.mult), reads=["hs2", "g14b", "pmconst"], writes=["hob"])
                P.dma("sp", lambda e: [e.dma_start(out=mixs_d.ap().rearrange("p (r c) -> p r c", r=4)[:, h, 128:256], in_=hob[h * 16:(h + 1) * 16, :]) for h in range(4)],
                      "hob_st", reads=["hob"], writes=["mixs_hm"], n=4)

        if "S" in stages:
            phase_S1()

        def phase_S2():
            ph, pb = new_phase(6)
            with ph:
                pT_b = ph.enter_context(nc.psum_tensor("pT_b", [128, 1024], BF16))
                pTr_b = ph.enter_context(nc.psum_tensor("pTr_b", [128, 1024], BF16))
                C0, D0, E_, F_, X0, X1 = range(6)
                stg = sb("stg2", [128, 2, 1408], F32)
                stg_ring = Ring("stg", 2)
                wuq_t = sb("wuq_t", [128, 3, 768], BF16)
                wukT_t = sb("wukT_t", [128, 8, 256], BF16)
                wuv_t = sb("wuv_t", [128, 2, 512], BF16)
                zt = sb("zt", [TS, 672], F32)
                gq_bc = sb("gq_bc", [TS, 384], F32)
                gkv16 = sb("gkv16", [TS, 256], F32)
                c8 = sb("c8", [TS, 2, 8, 32], F32)
                iota_p = sb("iota_p", [128, 1], F32)
                pt_i = sb("pt_i", [128, 1024], I32)
                pt_f = sb("pt_f", [128, 1024], F32)
                idx_i = sb("idx_i", [128, 1024], I32)
                jk = sb("jk2", [TS, 768], F32)
                sts = sb("sts", [128, 8], F32)
                qlat_b = sb("qlat_b", [TS, 384], BF16)
                qlatT = sb("qlatT", [128, 3, TS], BF16)
                q_sb = sb("q_sb", [TS, 8, 96], F32)
                qsw = sb("qsw", [TS, 8, 32], F32)
                qn_b = sb("qn_b", [TS, 8, 64], BF16)
                qr_b = sb("qr_b", [TS, 8, 32], BF16)
                qnT = sb("qnT", [128, 4, TS], BF16)
                QAT = sb("QAT", [128, 2, TS, 8], BF16)
                QRT = sb("QRT", [32, TS, 8], BF16)
                kvs = sb("kvs", [TS, 288], F32)
                ksw = sb("ksw", [TS, 32], F32)
                Gnew = sb("Gnew", [128, TS, 288], F32)
                ones_f = sb("ones_f", [128, 128], F32)
                negrow = sb("negrow", [1, 128], F32)
                G = sb("Gst", [128, 3, 4, 288], F32)
                g_ring = Ring("G", 3)
                Gb = sb("Gb", [128, 2, 65, 288], BF16)
                KT = sb("KTs", [128, 2, 512], BF16)
                KTr = sb("KTrs", [32, 512], BF16)
                PT = sb("PTs", [128, 65, 8], BF16)
                mx1 = sb("mx1", [128, 8], F32)
                mxh = sb("mxh", [8, 2], F32)
                diag = sb("diag", [8, 8], F32)
                negmx = sb("negmx", [128, 8], F32)
                t8 = sb("t8", [128, 8], F32)
                pr = sb("pr", [128, 8], F32)
                rden = sb("rden", [128, 8], F32)
                OLT = sb("OLT", [128, 2, 8, TS], BF16)
                amix = sb("amix", [TS, 4, 128], BF16)

                for c in range(3):
                    load_rows(stg, stg_ring, w_uq_d, c * 128, 0, 768, wuq_t[:, c, :], "wuq")
                for i in range(8):
                    load_rows(stg, stg_ring, wukT_d, 0, i * 256, 256, wukT_t[:, i, :], "wukT")
                for c in range(2):
                    load_rows(stg, stg_ring, w_uv_d, c * 128, 0, 512, wuv_t[:, c, :], "wuv")
                P.dma("sp", lambda e: [e.dma_start(out=zt[:], in_=zs_d.ap()[:, 0:672]), e.dma_start(out=gq_bc[:], in_=gq_bc_d), e.dma_start(out=gkv16[:], in_=gkv_bc_d[0:TS, :]),
                                       e.dma_start(out=c8[:], in_=c8_d), e.dma_start(out=iota_p[:], in_=iota_d), e.dma_start(out=negrow[:], in_=negrow_d),
                                       e.dma_start(out=pt_i[:], in_=pt_d.partition_broadcast(128))],
                      "s2ld", reads=["zs_d"], writes=["s2c"], n=7)
                P.op("dve", lambda e: e.memset(ones_f[:], 1.0), writes=["ones_f"])
                P.op("pool", lambda e: e.memset(Gnew[:], 0.0), writes=["Gnew"])
                P.op("pool", lambda e: e.memset(OLT[:], 0.0), writes=["OLT"])
                P.op("dve", lambda e: e.tensor_copy(pt_f[:], pt_i[:]), reads=["s2c"], writes=["pt_f"])
                P.op("dve", lambda e: e.tensor_scalar(out=pt_f[:], in0=pt_f[:], scalar1=128.0, scalar2=iota_p[:, 0:1], op0=ALU.mult, op1=ALU.add), reads=["pt_f", "s2c"], writes=["pt_f2"])
                P.op("dve", lambda e: e.tensor_copy(idx_i[:], pt_f[:]), reads=["pt_f2"], writes=["idx_i"])

                S2STOP = int(os.environ.get("KS2STOP", 9))

                def rms_tok(src_ap, dim, gain_ap, dst_ap, tag):
                    P.op("act", lambda e: e.activation(jk[:, 0:dim], src_ap, AF.Square), reads=["s2c"], writes=["jk"])
                    P.op("dve", lambda e: e.reduce_sum(out=sts[0:TS, 0:1], in_=jk[:, 0:dim], axis=AX.X), reads=["jk"], writes=["sts0"])
                    P.op("act", lambda e: e.activation(sts[0:TS, 1:2], sts[0:TS, 0:1], AF.Sqrt, bias=EPS, scale=1.0 / dim), reads=["sts0"], writes=["sts1"])
                    P.op("dve", lambda e: e.reciprocal(sts[0:TS, 2:3], sts[0:TS, 1:2]), reads=["sts1"], writes=["sts2"])
                    P.op("dve", lambda e: e.scalar_tensor_tensor(out=dst_ap, in0=src_ap, scalar=sts[0:TS, 2:3], in1=gain_ap, op0=ALU.mult, op1=ALU.mult), reads=["sts2", "s2c"], writes=[tag])
                rms_tok(zt[:, 384:640], 256, gkv16[:], kvs[:, 0:256], "kvs")
                P.op("dve", lambda e: e.tensor_copy(ksw[:, 0:16], zt[:, 656:672]), reads=["s2c"], writes=["ksw"])
                P.op("dve", lambda e: e.tensor_copy(ksw[:, 16:32], zt[:, 640:656]), reads=["s2c"], writes=["ksw"])
                P.op("dve", lambda e: e.tensor_tensor(out=ksw[:], in0=ksw[:], in1=c8[:, 1, 0, :], op=ALU.mult), reads=["ksw", "s2c"], writes=["ksw"])
                P.op("dve", lambda e: e.tensor_tensor(out=kvs[:, 256:288], in0=zt[:, 640:672], in1=c8[:, 0, 0, :], op=ALU.mult), reads=["s2c", "kvs"], writes=["kvs"])
                P.op("dve", lambda e: e.tensor_tensor(out=kvs[:, 256:288], in0=kvs[:, 256:288], in1=ksw[:], op=ALU.add), reads=["kvs", "ksw"], writes=["kvs"])
                P.dma("sp", lambda e: [e.dma_start(out=ckv_s_out, in_=kvs[:, 0:256]), e.dma_start(out=kr_s_out, in_=kvs[:, 256:288]),
                                       e.dma_start(out=Gnew[0:1, :, :], in_=kvs[:, :])], "kvs_st", reads=["kvs", "Gnew"], writes=["Gnew2"], n=3)
                if S2STOP < 2:
                    return
                rms_tok(zt[:, 0:384], 384, gq_bc[:], qlat_b[:], "qlat_b")
                for c in range(3):
                    P.op("pe", lambda e, c=c: e.transpose(pT_b[:, c * TS:(c + 1) * TS], qlat_b[:, c * 128:(c + 1) * 128], ident_bf[0:TS, 0:TS]), reads=["qlat_b", "ident_bf"], writes=["PSA"])
                P.op("act", lambda e: e.activation(qlatT[:], pT_b[:, 0:3 * TS].rearrange("p (c n) -> p c n", c=3), AF.Copy), reads=["PSA"], writes=["qlatT"])
                for hf in range(2):
                    for c in range(3):
                        P.op("pe", lambda e, c=c, hf=hf: e.matmul(pb[X0 + hf][0:TS, 0:384], qlatT[:, c, :], wuq_t[:, c, hf * 384:(hf + 1) * 384], start=(c == 0), stop=(c == 2)),
                             reads=["qlatT", "wuq"], writes=[("acc", X0 + hf)])
                    P.op("act", lambda e, hf=hf: e.activation(q_sb[:, hf * 4:(hf + 1) * 4, :], pb[X0 + hf][0:TS, 0:384].rearrange("p (h d) -> p h d", h=4), AF.Copy),
                         reads=[("acc", X0 + hf)], writes=["q_sb"])
                S2SUB = os.environ.get("KS2SUB", "z")
                if S2SUB < "b":
                    return
                P.op("dve", lambda e: e.tensor_copy(qsw[:, :, 0:16], q_sb[:, :, 80:96]), reads=["q_sb"], writes=["qsw"])
                P.op("dve", lambda e: e.tensor_copy(qsw[:, :, 16:32], q_sb[:, :, 64:80]), reads=["q_sb"], writes=["qsw"])
                P.op("dve", lambda e: e.tensor_tensor(out=qsw[:], in0=qsw[:], in1=c8[:, 1, :, :], op=ALU.mult), reads=["qsw", "s2c"], writes=["qsw"])
                P.op("dve", lambda e: e.tensor_tensor(out=q_sb[:, :, 64:96], in0=q_sb[:, :, 64:96], in1=c8[:, 0, :, :], op=ALU.mult), reads=["q_sb", "s2c", "qsw"], writes=["q_sb"])
                P.op("dve", lambda e: e.tensor_tensor(out=qr_b[:], in0=q_sb[:, :, 64:96], in1=qsw[:], op=ALU.add), reads=["q_sb", "qsw"], writes=["qr_b"])
                P.op("dve", lambda e: e.tensor_copy(qn_b[:], q_sb[:, :, 0:64]), reads=["q_sb"], writes=["qn_b"])
                if S2SUB < "c":
                    return
                for i in range(4):
                    P.op("pe", lambda e, i=i: e.transpose(pT_b[:, 64 + i * TS:64 + (i + 1) * TS], qn_b[:, 2 * i:2 * i + 2, :], ident_bf[0:TS, 0:TS]), reads=["qn_b", "ident_bf"], writes=["PSA"])
                P.op("act", lambda e: e.activation(qnT[:], pT_b[:, 64:64 + 4 * TS].rearrange("p (i n) -> p i n", i=4), AF.Copy), reads=["PSA"], writes=["qnT"])
                if S2SUB < "d":
                    return
                for h in range(8):
                    P.op("pe", lambda e, h=h: e.transpose(pTr_b[0:32, h * TS:(h + 1) * TS], qr_b[:, h, :], ident_bf[0:TS, 0:TS]), reads=["qr_b", "ident_bf"], writes=["PSB"])
                P.op("act", lambda e: e.activation(QRT[:], pTr_b[0:32, 0:8 * TS].rearrange("p (h b) -> p b h", h=8), AF.Copy), reads=["PSB"], writes=["QRT"])
                if S2SUB < "e":
                    return
                for cc in range(2):
                    for h in range(8):
                        i, e_ = h // 2, h % 2
                        P.op("pe", lambda e, cc=cc, h=h, i=i, e_=e_: e.matmul(pb[X0 + cc][:, h * TS:(h + 1) * TS], wukT_t[:, h, cc * 128:(cc + 1) * 128],
                                                                            qnT[:, i, :], start=True, stop=True),
                             reads=["qnT", "wukT"], writes=[("acc", X0 + cc)])
                    P.op("act", lambda e, cc=cc: e.activation(QAT[:, cc, :, :], pb[X0 + cc][:, 0:8 * TS].rearrange("p (h b) -> p b h", h=8), AF.Copy), reads=[("acc", X0 + cc)], writes=["QAT"])

                if S2STOP < 3:
                    return

                def do_pages(b, bs, pages, slot_fn, sc_bank, sc_col0, sckey, mask_new):
                    npg = len(pages)
                    if S2STOP < 4:
                        return
                    for cc in range(2):
                        for jj, pg in enumerate(pages):
                            P.op("pe", lambda e, cc=cc, jj=jj, pg=pg: e.transpose(pT_b[:, cc * 512 + jj * 128:cc * 512 + (jj + 1) * 128], Gb[:, bs, pg, cc * 128:(cc + 1) * 128], ident_bf[:]),
                                 reads=[("Gb", bs), "ident_bf"], writes=["PSA"])
                    for jj, pg in enumerate(pages):
                        P.op("pe", lambda e, jj=jj, pg=pg: e.transpose(pTr_b[0:32, jj * 128:(jj + 1) * 128], Gb[:, bs, pg, 256:288], ident_bf[:]), reads=[("Gb", bs), "ident_bf"], writes=["PSB"])
                    P.op("act", lambda e: e.activation(KT[:, :, 0:npg * 128], pT_b[:, :].rearrange("p (c n) -> p c n", c=2)[:, :, 0:npg * 128], AF.Copy), reads=["PSA"], writes=["KT"])
                    P.op("dve", lambda e: e.tensor_copy(KTr[:, 0:npg * 128], pTr_b[0:32, 0:npg * 128]), reads=["PSB"], writes=["KTr"])
                    for jj, pg in enumerate(pages):
                        cols = slice(sc_col0 + jj * 8, sc_col0 + jj * 8 + 8)
                        P.op("pe", lambda e, jj=jj, cols=cols: e.matmul(pb[sc_bank][:, cols], KT[:, 0, jj * 128:(jj + 1) * 128], QAT[:, 0, b, :], start=True, stop=False), reads=["KT", "QAT"], writes=[sckey])
                        P.op("pe", lambda e, jj=jj, cols=cols: e.matmul(pb[sc_bank][:, cols], KT[:, 1, jj * 128:(jj + 1) * 128], QAT[:, 1, b, :], start=False, stop=False), reads=["KT", "QAT"], writes=[sckey])
                        P.op("pe", lambda e, jj=jj, cols=cols: e.matmul(pb[sc_bank][:, cols], KTr[:, jj * 128:(jj + 1) * 128], QRT[:, b, :], start=False, stop=(not mask_new)), reads=["KTr", "QRT"], writes=[sckey])
                        if mask_new:
                            P.op("pe", lambda e, cols=cols: e.matmul(pb[sc_bank][:, cols], negrow[0:1, :], ones_f[0:1, 0:8], start=False, stop=True), reads=["s2c", "ones_f"], writes=[sckey])

                def sample(b):
                    bs = b % 2
                    def consume(g, gs, gkey):
                        P.op("pool", lambda e: e.tensor_copy(Gb[:, bs, g * 4:(g + 1) * 4, :], G[:, gs, :, :]), reads=[gkey], writes=[("Gb", bs)])
                        do_pages(b, bs, [g * 4 + jj for jj in range(4)], None, C0, g * 32, "scC", False)
                    prev = None
                    for g in range(16):
                        gs, gkey = g_ring.next()
                        P.dma("pool", lambda e, gs=gs, g=g: [e.indirect_dma_start(out=G[:, gs, jj, :], out_offset=None, in_=cache_d,
                                                                                 in_offset=bass.IndirectOffsetOnAxis(ap=idx_i[:, b * 64 + g * 4 + jj:b * 64 + g * 4 + jj + 1], axis=0))
                                                            for jj in range(4)], ("G_ld", gs), reads=["idx_i"], writes=[gkey], n=4)
                        if prev is not None:
                            consume(*prev)
                        prev = (g, gs, gkey)
                    consume(*prev)
                    P.op("pool", lambda e: e.tensor_copy(Gb[:, bs, 64, :], Gnew[:, b, :]), reads=["Gnew2"], writes=[("Gb", bs)])
                    do_pages(b, bs, [64], None, D0, 0, "scD", True)
                    if S2STOP < 5:
                        return
                    P.op("dve", lambda e: e.tensor_reduce(out=mx1[:], in_=pb[C0][:, :].rearrange("p (j h) -> p h j", h=8), axis=AX.X, op=ALU.max), reads=["scC"], writes=["mx1"])
                    P.op("dve", lambda e: e.tensor_tensor(out=mx1[:], in0=mx1[:], in1=pb[D0][:, 0:8], op=ALU.max), reads=["mx1", "scD"], writes=["mx1"])
                    P.op("pe", lambda e: e.transpose(pb[F_][0:8, 0:128], mx1[:], ident_f[:]), reads=["mx1", "ident_f"], writes=["PSF"])
                    P.op("dve", lambda e: e.reduce_max(out=mxh[:, 0:1], in_=pb[F_][0:8, 0:128], axis=AX.X), reads=["PSF"], writes=["mxh"])
                    P.op("dve", lambda e: e.tensor_scalar(out=diag[:], in0=ident_f[0:8, 0:8], scalar1=mxh[:, 0:1], scalar2=None, op0=ALU.mult), reads=["mxh", "ident_f"], writes=["diag"])
                    P.op("pe", lambda e: e.matmul(pb[F_][:, 128:136], ones_f[0:8, :], diag[:], start=True, stop=True), reads=["diag", "ones_f"], writes=["PSF"])
                    P.op("dve", lambda e: e.tensor_scalar(out=negmx[:], in0=pb[F_][:, 128:136], scalar1=-SCALE, scalar2=None, op0=ALU.mult), reads=["PSF"], writes=["negmx"])
                    for h in range(8):
                        P.op("act", lambda e, h=h: e.activation(PT[:, 0:64, h], pb[C0][:, :].rearrange("p (j h) -> p j h", h=8)[:, :, h], AF.Exp, bias=negmx[:, h:h + 1], scale=SCALE),
                             reads=["scC", "negmx"], writes=["PT"])
                    P.op("dve", lambda e: e.tensor_scalar(out=t8[:], in0=pb[D0][:, 0:8], scalar1=SCALE, scalar2=None, op0=ALU.mult), reads=["scD"], writes=["t8"])
                    P.op("dve", lambda e: e.tensor_tensor(out=t8[:], in0=t8[:], in1=negmx[:], op=ALU.add), reads=["t8", "negmx"], writes=["t8"])
                    P.op("act", lambda e: e.activation(PT[:, 64, :], t8[:], AF.Exp), reads=["t8"], writes=["PT"])
                    P.op("dve", lambda e: e.tensor_reduce(out=pr[:], in_=PT[:, :, :].rearrange("p j h -> p h j"), axis=AX.X, op=ALU.add), reads=["PT"], writes=["pr"])
                    P.op("pe", lambda e: e.matmul(pb[E_][:, 16:24], ones_f[:], pr[:], start=True, stop=True), reads=["pr", "ones_f"], writes=["PSE"])
                    for cc in range(2):
                        for pg in range(65):
                            P.op("pe", lambda e, cc=cc, pg=pg: e.matmul(pb[E_][:, cc * 8:(cc + 1) * 8], Gb[:, bs, pg, cc * 128:(cc + 1) * 128], PT[:, pg, :], start=(pg == 0), stop=(pg == 64)),
                                 reads=["PT", ("Gb", bs)], writes=["PSE"])
                    P.op("dve", lambda e: e.reciprocal(rden[:], pb[E_][:, 16:24]), reads=["PSE"], writes=["rden"])
                    for cc in range(2):
                        P.op("dve", lambda e, cc=cc: e.tensor_tensor(out=OLT[:, cc, :, b], in0=pb[E_][:, cc * 8:(cc + 1) * 8], in1=rden[:], op=ALU.mult), reads=["PSE", "rden"], writes=["OLT"])

                nsmp = int(os.environ.get("KNS", TS))
                for b in range(nsmp):
                    sample(b)
                if S2STOP < 6:
                    return
                for h in range(8):
                    for cc in range(2):
                        P.op("pe", lambda e, h=h, cc=cc: e.matmul(pb[X0][0:TS, h * 64:(h + 1) * 64], OLT[:, cc, h, :], wuv_t[:, cc, h * 64:(h + 1) * 64], start=(cc == 0), stop=(cc == 1)),
                             reads=["OLT", "wuv"], writes=[("acc", X0)])
                P.op("act", lambda e: e.activation(amix[:], pb[X0][0:TS, 0:512].rearrange("p (r c) -> p r c", r=4), AF.Copy), reads=[("acc", X0)], writes=["amix"])
                P.dma("sp", lambda e: e.dma_start(out=mixs_d.ap().rearrange("p (r c) -> p r c", r=4)[:, :, 0:128], in_=amix[:]), "amix_st", reads=["amix"], writes=["mixs_a"])

        if "T" in stages:
            phase_S2()

        rank_cache = {}

        def phase_C1():
            P.barrier()
            for i in range(4):
                P.dma("pool", lambda e, i=i: e.collective_compute("AllGather", ALU.bypass, replica_groups=GROUPS,
                                                                  ins=[ag2_in[i].ap()], outs=[ag2_out[i].ap()]),
                      ("cc2", i), reads=["ag2_in"], writes=[("ag2_out", i)], inc=1)
            def grab(e):
                rank = e.partition_id() % 4
                return [e.dma_start(out=mixloc[i].ap(), in_=ag2_out[i].ap().rearrange("r (t c) -> (r t) c", c=256).rearrange("(rr q) c -> q rr c", rr=4)[bass.ds(rank * 512, 512), :, :])
                        for i in range(4)]
            P.dma("pool", grab, "grab", reads=[("ag2_out", i) for i in range(4)], writes=[("mixloc", i) for i in range(4)], n=4)
            ph, pb = new_phase(6)
            with ph:
                pbT = ph.enter_context(nc.psum_tensor("pbT_c1", [128, 1024], BF16))
                stg = sb("stgc", [128, 2, 1408], F32)
                stg_ring = Ring("stg", 2)
                wout_t = sb("wout_t", [128, KC, D], BF16)
                gattn_bc = sb("gattn_bc", [128, 512], F32)
                NT = 256
                h_t = sb("hc_t", [128, 2, KC, NT], F32)
                h_ring = Ring("hc", 2)
                mt = sb("mt", [128, 2, 4, 256], BF16)
                mt_ring = Ring("mt", 2)
                junk = sb("junkc", [128, 512], F32)
                stc = sb("stc", [128, 4], F32)
                mixn = sb("mixn", [128, D], BF16)
                mixT = sb("mixT", [128, KC, NT], BF16)
                acc_ring = Ring("acc", 4)
                for k in range(KC):
                    load_rows(stg, stg_ring, wout_d, k * 128, 0, 1024, wout_t[:, k, :], "wout")
                P.dma("sp", lambda e: e.dma_start(out=gattn_bc[:], in_=gattn_bc_d), "c0", writes=["gattn_bc"])

                def mix_block(np_, src_fn, src_reads, col0):
                    ms, mkey = mt_ring.next()
                    P.dma("sp", lambda e: src_fn(e, ms), ("mt_ld", ms), reads=src_reads, writes=[mkey])
                    P.op("act", lambda e: e.activation(junk[0:np_, :].rearrange("p (r c) -> p r c", r=4), mt[0:np_, ms, :, 0:128], AF.Square), reads=[mkey], writes=["junkc"])
                    P.op("dve", lambda e: e.reduce_sum(out=stc[0:np_, 0:1], in_=junk[0:np_, :], axis=AX.X), reads=["junkc"], writes=["stc0"])
                    P.op("act", lambda e: e.activation(stc[0:np_, 1:2], stc[0:np_, 0:1], AF.Sqrt, bias=EPS, scale=1.0 / 512), reads=["stc0"], writes=["stc1"])
                    P.op("dve", lambda e: e.reciprocal(stc[0:np_, 2:3], stc[0:np_, 1:2]), reads=["stc1"], writes=["stc2"])
                    for rr in range(4):
                        P.op("dve", lambda e, rr=rr: e.scalar_tensor_tensor(out=mixn[0:np_, rr * 128:(rr + 1) * 128], in0=mt[0:np_, ms, rr, 0:128], scalar=stc[0:np_, 2:3],
                                                                           in1=gattn_bc[0:np_, rr * 128:(rr + 1) * 128], op0=ALU.mult, op1=ALU.mult),
                             reads=[mkey, "stc2", "gattn_bc"], writes=["mixn"])
                    P.op("pool", lambda e: e.tensor_copy(mixn[0:np_, 512:1024].rearrange("p (r c) -> p r c", r=4), mt[0:np_, ms, :, 128:256]), reads=[mkey], writes=["mixn"])
                    for k in range(KC):
                        P.op("pe", lambda e, k=k: e.transpose(pbT[:, k * 128:k * 128 + np_], mixn[0:np_, k * 128:(k + 1) * 128], ident_bf[0:np_, 0:np_]),
                             reads=["mixn", "ident_bf"], writes=["PSA"])
                    P.op("act", lambda e: e.activation(mixT[:, :, col0:col0 + np_], pbT[:, :].rearrange("p (k c) -> p k c", k=KC)[:, :, 0:np_], AF.Copy), reads=["PSA"], writes=["mixT"])

                def c1_tile(t0, n, hs, hkey):
                    P.dma("sp", lambda e: e.dma_start(out=h_t[:, hs, :, 0:n], in_=hT_d[:, t0:t0 + n].rearrange("(k p) n -> p k n", p=128)),
                          ("hc_ld", hs), reads=["hT_d"], writes=[hkey])
                    if t0 < TP:
                        for b2 in range(2):
                            blk = t0 // 128 + b2
                            i, j = blk // 4, blk % 4

                            def src(e, ms, i=i, j=j):
                                return e.dma_start(out=mt[:, ms, :, :], in_=mixloc[i].ap()[j * 128:(j + 1) * 128, :, :])
                            mix_block(128, src, [("mixloc", i)], b2 * 128)
                    else:
                        mix_block(TS, lambda e, ms: e.dma_start(out=mt[0:TS, ms, :, :], in_=mixs_d.ap().rearrange("p (r c) -> p r c", r=4)), ["mixs_d"], 0)
                    for o in range(KC):
                        a, akey = acc_ring.next()
                        for k in range(KC):
                            P.op("pe", lambda e, k=k, o=o, a=a: e.matmul(pb[a][:, 0:n], wout_t[:, k, o * 128:(o + 1) * 128], mixT[:, k, 0:n], start=(k == 0), stop=(k == KC - 1)),
                                 reads=["mixT", "wout"], writes=[akey])
                        P.op("dve", lambda e, o=o, a=a: e.tensor_tensor(out=h_t[:, hs, o, 0:n], in0=h_t[:, hs, o, 0:n], in1=pb[a][:, 0:n], op=ALU.add), reads=[akey, hkey], writes=[hkey])
                    P.dma("sp", lambda e: e.dma_start(out=hT_d[:, t0:t0 + n].rearrange("(k p) n -> p k n", p=128), in_=h_t[:, hs, :, 0:n]),
                          ("hc_st", hs), reads=[hkey], writes=["hT_d2"])

                for (t0, n) in token_tiles():
                    hs, hkey = h_ring.next()
                    c1_tile(t0, n, hs, hkey)

        def phase_C3():
            ph, pb = new_phase(6)
            with ph:
                stg = sb("stgp", [128, 2, 1408], F32)
                stg_ring = Ring("stg", 2)
                wpg_t = sb("wpg_t", [128, KC, D], BF16)
                wpp_t = sb("wpp_t", [128, 2, D], BF16)
                NT = 256
                h_t = sb("hp_t", [128, 2, KC, NT], F32)
                h_ring = Ring("hp", 2)
                sq_t = sb("sqp_t", [128, KC, NT], BF16)
                xn_t = sb("xnp_t", [128, KC, NT], BF16)
                y_t = sb("y_t", [128, KC, NT], F32)
                rs_t = sb("rsp_t", [128, 2, NT], F32)
                pe32 = sb("pe32", [128, 2, NT], F32)
                pe16 = sb("pe16", [128, 2, NT], BF16)
                sg_t = sb("sgp_t", [128, 2, NT], F32)
                ss_ps = pb[4]
                for k in range(KC):
                    load_rows(stg, stg_ring, wpg_d, k * 128, 0, 1024, wpg_t[:, k, :], "wpg")
                for k in range(2):
                    load_rows(stg, stg_ring, wpp_d, k * 128, 0, 1024, wpp_t[:, k, :], "wpp")

                def rms_fm(src_ap_fn, src_key, gcol, dst_t, dst_key, n):
                    P.op("act", lambda e: e.activation(sq_t[:, :, 0:n], src_ap_fn(slice(0, KC)), AF.Square), reads=[src_key], writes=["sq"])
                    for k in range(KC):
                        P.op("pe", lambda e, k=k: e.matmul(ss_ps[:, 0:n], ones_t[:], sq_t[:, k, 0:n], start=(k == 0), stop=(k == KC - 1)), reads=["sq", "ones_t"], writes=["ss_ps"])
                    P.op("act", lambda e: e.activation(rs_t[:, 0, 0:n], ss_ps[:, 0:n], AF.Sqrt, bias=EPS, scale=1.0 / D), reads=["ss_ps"], writes=["rs0"])
                    P.op("dve", lambda e: e.reciprocal(rs_t[:, 1, 0:n], rs_t[:, 0, 0:n]), reads=["rs0"], writes=["rs1"])
                    for k in range(KC):
                        P.op("dve", lambda e, k=k: e.scalar_tensor_tensor(out=dst_t[:, k, 0:n], in0=src_ap_fn(k), scalar=gc_t[:, gcol + k:gcol + k + 1],
                                                                         in1=rs_t[:, 1, 0:n], op0=ALU.mult, op1=ALU.mult), reads=[src_key, "rs1", "gc_t"], writes=[dst_key])

                def c3_tile(t0, n, hs, hkey):
                    P.dma("sp", lambda e: [e.dma_start(out=h_t[:, hs, :, 0:n], in_=hT_d[:, t0:t0 + n].rearrange("(k p) n -> p k n", p=128)),
                                           e.dma_start(out=pe32[:, :, 0:n], in_=peT[:, t0:t0 + n].rearrange("(k p) n -> p k n", p=128))],
                          ("hp_ld", hs), reads=["hT_d"], writes=[hkey, "pe32"], n=2)
                    P.op("pool", lambda e: e.tensor_copy(pe16[:, :, 0:n], pe32[:, :, 0:n]), reads=["pe32"], writes=["pe16"])
                    rms_fm(lambda k: h_t[:, hs, k, 0:n], hkey, GC_PLE, xn_t, "xn", n)
                    for o in range(KC):
                        ga, gk_ = o % 2, ("acc", o % 2)
                        pa, pk_ = 2 + o % 2, ("acc", 2 + o % 2)
                        for k in range(KC):
                            P.op("pe", lambda e, k=k, o=o, ga=ga: e.matmul(pb[ga][:, 0:n], wpg_t[:, k, o * 128:(o + 1) * 128], xn_t[:, k, 0:n], start=(k == 0), stop=(k == KC - 1)),
                                 reads=["xn", "wpg"], writes=[gk_])
                        for k in range(2):
                            P.op("pe", lambda e, k=k, o=o, pa=pa: e.matmul(pb[pa][:, 0:n], wpp_t[:, k, o * 128:(o + 1) * 128], pe16[:, k, 0:n], start=(k == 0), stop=(k == 1)),
                                 reads=["pe16", "wpp"], writes=[pk_])
                        s_ = o % 2
                        P.op("act", lambda e, ga=ga, s_=s_: e.activation(sg_t[:, s_, 0:n], pb[ga][:, 0:n], AF.Exp, scale=-1.0), reads=[gk_], writes=[("sgp", s_)])
                        P.op("dve", lambda e, s_=s_: e.tensor_scalar(out=sg_t[:, s_, 0:n], in0=sg_t[:, s_, 0:n], scalar1=1.0, scalar2=None, op0=ALU.add), reads=[("sgp", s_)], writes=[("sgp", s_)])
                        P.op("dve", lambda e, s_=s_: e.reciprocal(sg_t[:, s_, 0:n], sg_t[:, s_, 0:n]), reads=[("sgp", s_)], writes=[("sgp", s_)])
                        P.op("dve", lambda e, s_=s_, pa=pa: e.tensor_tensor(out=sg_t[:, s_, 0:n], in0=sg_t[:, s_, 0:n], in1=pb[pa][:, 0:n], op=ALU.mult), reads=[("sgp", s_), pk_], writes=[("sgp", s_)])
                        P.op("dve", lambda e, s_=s_, o=o: e.tensor_tensor(out=h_t[:, hs, o, 0:n], in0=h_t[:, hs, o, 0:n], in1=sg_t[:, s_, 0:n], op=ALU.add), reads=[("sgp", s_), hkey], writes=[hkey])
                    rms_fm(lambda k: h_t[:, hs, k, 0:n], hkey, GC_FIN, y_t, "y", n)
                    P.dma("sp", lambda e: e.dma_start(out=yT_out[:, t0:t0 + n].rearrange("(k p) n -> p k n", p=128), in_=y_t[:, :, 0:n]), "y_st", reads=["y"])

                for (t0, n) in token_tiles():
                    hs, hkey = h_ring.next()
                    c3_tile(t0, n, hs, hkey)

        if "C" in stages:
            phase_C1()
            ffn_sweep(wg2, wu2, wd2, GC_FF2, hT_d, hT_d, False)
            phase_C3()

        if DBG:
            P.barrier()
            P.dma("sp", lambda e: [e.dma_start(out=dbg_a[rr * 2048 + i * 512:rr * 2048 + (i + 1) * 512, :],
                                       in_=ag2_in[i].ap().rearrange("p (t c) -> (p t) c", c=256)[rr * 512:(rr + 1) * 512, :]) for i in range(4) for rr in range(4)],
                  "dbga", reads=["ag2_in"], n=16)
        P.barrier_final("sp")
        P.emit(st)
        print("plan stats", P.stats, "sems", P.n_sems, "sbuf top", A.top)
    return nc, dram_in, dram_out


def fm_cols(g, nk):
    return np.ascontiguousarray(np.asarray(g, np.float32).reshape(nk, 128).T)


def rope_tables(pos):
    half = 16
    inv = (np.float32(10000.0) ** (-(np.arange(half, dtype=np.float32) / np.float32(half)))).astype(np.float32)
    ang = (pos.astype(np.float32)[:, None] * inv[None, :]).astype(np.float32)
    c = np.cos(ang.astype(np.float64)).astype(np.float32)
    s = np.sin(ang.astype(np.float64)).astype(np.float32)
    C32 = np.concatenate([c, c], axis=1)
    S32 = np.concatenate([-s, s], axis=1)
    return C32, S32


def prep_inputs(inp, dram_in):
    bf = ml_dtypes.bfloat16
    maps = []
    gcols = np.zeros((128, 48), np.float32)
    gcols[:, 0:8] = fm_cols(inp["g_ff1"][0], 8)
    gcols[:, 8:16] = fm_cols(inp["g_mix"][0], 8)
    gcols[:, 16:24] = fm_cols(inp["g_ff2"][0], 8)
    gcols[:, 24:32] = fm_cols(inp["g_ple"][0], 8)
    gcols[:, 32:40] = fm_cols(inp["g_final"], 8)
    gcols[:, 40:43] = fm_cols(inp["g_q"][0], 3)
    gcols[:, 43:45] = fm_cols(inp["g_kv"][0], 2)
    w_in = inp["w_in"][0]
    w_uq = inp["w_uq"][0]
    w_uk = inp["w_uk"][0]
    w_uv = inp["w_uv"][0]
    C32, S32 = rope_tables(np.arange(SEQ))
    Cs, Ss = rope_tables(np.array([SEQ]))
    c8 = np.stack([np.broadcast_to(Cs[0][None, None, :], (TS, 8, 32)), np.broadcast_to(Ss[0][None, None, :], (TS, 8, 32))], axis=1).astype(np.float32)
    cache_cat = None
    wukT_z = np.zeros((128, 8 * 256), np.float32)
    for hh in range(8):
        wukT_z[(hh % 2) * 64:(hh % 2 + 1) * 64, hh * 256:(hh + 1) * 256] = w_uk[:, hh * 64:(hh + 1) * 64].T
    if "cache" in dram_in:
        cache_cat = np.concatenate([inp["cache_ckv"][0].reshape(-1, 256), inp["cache_krope"][0].reshape(-1, 32)], axis=1)
    p = np.arange(128)[:, None, None]
    j = np.arange(4)[None, :, None]
    col = np.arange(512)[None, None, :]
    amask = np.where(j * 128 + p <= col, 0.0, NEG).astype(bf)
    shared = {
        "gc": gcols,
        "wg1": inp["w_ff1_gate"][0], "wu1": inp["w_ff1_up"][0], "wd1": inp["w_ff1_down"][0],
        "ones_bf": np.ones((128, 128), bf),
        "ident_bf": np.eye(128, dtype=np.float32).astype(bf),
        "ident_f": np.eye(128, dtype=np.float32),
        "gkv_bc": np.broadcast_to(inp["g_kv"][0][None, :], (128, 256)),
        "CT": C32.T, "ST": S32.T, "Ctok": C32, "Stok": S32,
        "amask": amask,
        "w_in": w_in,
        "w_uq": w_uq, "w_uv": w_uv,
        "wukT": wukT_z,
        "gq_bc": np.broadcast_to(inp["g_q"][0][None, :], (TS, 384)),
        "c8": c8, "iota": np.arange(128, dtype=np.float32)[:, None],
        "negrow": np.concatenate([[0.0], np.full(127, NEG)]).astype(np.float32)[None, :],
        "cache": cache_cat,
        "wout": inp["w_out"][0], "wpg": inp["w_ple_gate"][0], "wpp": inp["w_ple_proj"][0],
        "wg2": inp["w_ff2_gate"][0], "wu2": inp["w_ff2_up"][0], "wd2": inp["w_ff2_down"][0],
        "gattn_bc": np.broadcast_to(inp["g_attn_out"][0][None, :], (128, 512)),
        "trimask": np.where(np.arange(128)[:, None] <= np.arange(128)[None, :], 0.0, NEG).astype(np.float32),
    }
    sw = np.concatenate([np.arange(16, 32), np.arange(0, 16)])
    for c in range(NCORES):
        b, r = c // 4, c % 4
        m = {}
        xp = inp["x_prompt"][b, r * TP:(r + 1) * TP, :]
        xs = inp["x_sample"][c * TS:(c + 1) * TS, 0, :]
        m["xT"] = np.concatenate([xp, xs], axis=0).T
        m["peT"] = np.concatenate([inp["p_prompt"][0, b, r * TP:(r + 1) * TP, :], inp["p_sample"][0, c * TS:(c + 1) * TS, 0, :]], axis=0).T
        wB = np.zeros((D, 1280), np.float32)
        wB[:, 0:640] = w_in[:, 0:640]
        wB[:, 640:672] = w_in[:, OFF_KR:OFF_KR + 32]
        wB[:, 672:704] = w_in[:, OFF_KR + sw]
        wB[:, 704:832] = w_in[:, OFF_MQ + r * 128:OFF_MQ + (r + 1) * 128]
        wB[:, 832:960] = w_in[:, OFF_MK + r * 128:OFF_MK + (r + 1) * 128]
        wB[:, 960:1088] = w_in[:, OFF_MV + r * 128:OFF_MV + (r + 1) * 128]
        wB[:, 1088:1216] = w_in[:, OFF_MO + r * 128:OFF_MO + (r + 1) * 128]
        wB[:, 1216] = w_in[:, OFF_MI + r]
        wB[:, 1217] = w_in[:, OFF_MF + r]
        m["wB"] = wB
        wQm = np.zeros((384, 320), np.float32)
        wKm = np.zeros((256, 256), np.float32)
        wVm = np.zeros((256, 128), np.float32)
        for jj in range(2):
            hh = 2 * r + jj
            wQm[:, jj * 160 + 0:jj * 160 + 32] = w_uq[:, hh * 96 + 64:hh * 96 + 96]
            wQm[:, jj * 160 + 64:jj * 160 + 128] = w_uq[:, hh * 96:hh * 96 + 64]
            wQm[:, jj * 160 + 128:jj * 160 + 160] = w_uq[:, hh * 96 + 64 + sw]
            wKm[:, jj * 128 + 64:jj * 128 + 128] = w_uk[:, hh * 64:(hh + 1) * 64]
            wVm[:, jj * 64:(jj + 1) * 64] = w_uv[:, hh * 64:(hh + 1) * 64]
        m["wQm"], m["wKm"], m["wVm"] = wQm, wKm, wVm
        sl = slice(c * TS, (c + 1) * TS)
        m["pt"] = inp["page_table"][sl].reshape(1, 1024).astype(np.int32)
        m["stC"] = inp["state_C"][0, sl].transpose(1, 0, 2, 3).reshape(64, 16384)
        m["stn"] = inp["state_n"][0, sl].transpose(1, 0, 2).reshape(64, 128)
        m["stm"] = inp["state_m"][0, sl].transpose(1, 0).reshape(64, 1)
        gbs = np.zeros((64, 4), np.float32)
        gbs[:, 0] = np.repeat(inp["b_gate_i"][0], 16); gbs[:, 1] = np.repeat(inp["b_gate_f"][0], 16); gbs[:, 2] = -gbs[:, 1]
        m["gbs"] = gbs
        m["gml_pm"] = np.repeat(inp["g_mlstm_out"][0], 16, axis=0)
        m["gml_bc"] = np.broadcast_to(inp["g_mlstm_out"][0, r][None, :], (128, 128))
        m["gb"] = np.array([[inp["b_gate_i"][0, r], inp["b_gate_f"][0, r], -inp["b_gate_f"][0, r], 0.0]], np.float32)
        for k_ in dram_in:
            if k_ not in m:
                m[k_] = shared[k_]
        out = {}
        for k_, (shape, dt) in dram_in.items():
            a = np.ascontiguousarray(m[k_])
            assert tuple(a.shape) == shape, (k_, a.shape, shape)
            out[k_] = a
        maps.append(out)
    return maps


_CACHE = {}


def run(inp, stages):
    if stages not in _CACHE:
        _CACHE[stages] = build(stages)
    nc, dram_in, dram_out = _CACHE[stages]
    maps = prep_inputs(inp, dram_in)
    res = run_bass_kernel_spmd(nc, maps, core_ids=list(range(NCORES)))
    return res.results


def assemble(res):
    f32 = np.float32
    y_p = np.zeros((2, SEQ, D), f32); y_s = np.zeros((128, 1, D), f32)
    ckv_p = np.zeros((1, 2, SEQ, 256), f32); kr_p = np.zeros((1, 2, SEQ, 32), f32)
    C_p = np.zeros((1, 2, 4, 128, 128), f32); n_p = np.zeros((1, 2, 4, 128), f32); m_p = np.zeros((1, 2, 4), f32)
    ckv_s = np.zeros((1, 128, 1, 256), f32); kr_s = np.zeros((1, 128, 1, 32), f32)
    C_s = np.zeros((1, 128, 4, 128, 128), f32); n_s = np.zeros((1, 128, 4, 128), f32); m_s = np.zeros((1, 128, 4), f32)
    for c in range(NCORES):
        b, r = c // 4, c % 4
        o = res[c]
        yT = np.asarray(o["yT_out"])
        y_p[b, r * TP:(r + 1) * TP] = yT[:, :TP].T
        sl = slice(c * TS, (c + 1) * TS)
        y_s[sl, 0] = yT[:, TP:].T
        ckv_p[0, b, r * TP:(r + 1) * TP] = np.asarray(o["ckv_p_out"])[r * TP:(r + 1) * TP]
        kr_p[0, b, r * TP:(r + 1) * TP] = np.asarray(o["kr_p_out"])[r * TP:(r + 1) * TP]
        C_p[0, b, r] = np.asarray(o["C_p_out"]); n_p[0, b, r] = np.asarray(o["n_p_out"])[:, 0]; m_p[0, b, r] = np.asarray(o["m_p_out"])[0, 0]
        ckv_s[0, sl, 0] = np.asarray(o["ckv_s_out"]); kr_s[0, sl, 0] = np.asarray(o["kr_s_out"])
        C_s[0, sl] = np.asarray(o["C_s_out"]).reshape(4, TS, 128, 128).transpose(1, 0, 2, 3)
        n_s[0, sl] = np.asarray(o["n_s_out"]).reshape(4, TS, 128).transpose(1, 0, 2)
        m_s[0, sl] = np.asarray(o["m_s_out"]).reshape(4, TS).T
    return (y_p, y_s, ckv_p, kr_p, C_p, n_p, m_p, ckv_s, kr_s, C_s, n_s, m_s)


def kernel(**inputs):
    inp = {k: np.asarray(v) for k, v in inputs.items()}
    res = run(inp, "AGBMSTC")
    return assemble(res)
```
